# Optimizing a Trainium2 kernel written in Bass

```python
import jax, jax.numpy as jnp
from jax import lax
import numpy as np

D_MODEL = 1024
BATCH = 8
SEQ = 4096
DEPTH = 2
DEC_BATCH = 32
DEC_SEQ = 32
PAST_LEN = 2048

CHUNK = 64
Q_BLOCK = 128
MIX_WIDTH = D_MODEL
FOX_HEADS = 8
FOX_HEAD_DIM = MIX_WIDTH // 2 // FOX_HEADS
FOX_WIDTH = FOX_HEADS * FOX_HEAD_DIM
MLA_HEADS = 4
MLA_NOPE_DIM = 128
MLA_ROPE_DIM = 64
MLA_V_DIM = MIX_WIDTH // 2 // MLA_HEADS
MLA_WIDTH = MLA_HEADS * MLA_V_DIM
MLA_Q_RANK = 256
MLA_KV_RANK = 128
ROPE_THETA = 10000.0
N_MEM = 256
X_HEADS = 4
X_HEAD_DIM = D_MODEL // X_HEADS
D_FF = 4 * D_MODEL
EPS = 1e-6
FORGET_BIAS = 2.0
IN_SPLITS = [FOX_WIDTH, 2 * FOX_WIDTH, 3 * FOX_WIDTH, 3 * FOX_WIDTH + FOX_HEADS,
             3 * FOX_WIDTH + FOX_HEADS + MLA_Q_RANK,
             3 * FOX_WIDTH + FOX_HEADS + MLA_Q_RANK + MLA_KV_RANK]
IN_COLS = 3 * FOX_WIDTH + FOX_HEADS + MLA_Q_RANK + MLA_KV_RANK + MLA_ROPE_DIM

kernel_name = "hybrid_fox_mla_streaming_encoder_step"


def rmsnorm(x, g):
    xf = x.astype(jnp.float32)
    y = xf * lax.rsqrt(jnp.mean(jnp.square(xf), axis=-1, keepdims=True) + EPS)
    return (y * g.astype(jnp.float32)).astype(x.dtype)


def rope(x, pos):
    half = x.shape[-1] // 2
    inv = ROPE_THETA ** (-jnp.arange(half, dtype=jnp.float32) / half)
    ang = pos.astype(jnp.float32)[:, None] * inv[None, :]
    shape = (1, x.shape[1]) + (1,) * (x.ndim - 3) + (half,)
    cos = jnp.cos(ang).reshape(shape)
    sin = jnp.sin(ang).reshape(shape)
    xf = x.astype(jnp.float32)
    x1, x2 = xf[..., :half], xf[..., half:]
    return jnp.concatenate([x1 * cos - x2 * sin, x2 * cos + x1 * sin], axis=-1).astype(x.dtype)


def attend(q, k, v, q_pos, k_pos, mask_kind, f_q=None, f_k=None):
    scale = q.shape[-1] ** -0.5
    s = jnp.einsum("bqhd,bkhd->bhqk", q, k).astype(jnp.float32) * scale
    if f_q is not None:
        s = s + (jnp.swapaxes(f_q, 1, 2)[:, :, :, None] - jnp.swapaxes(f_k, 1, 2)[:, :, None, :])
    if mask_kind == "frame":
        allowed = k_pos[None, :] <= q_pos[:, None]
    else:
        allowed = (k_pos[None, :] // CHUNK) <= (q_pos[:, None] // CHUNK)
    s = jnp.where(allowed[None, None], s, -jnp.inf)
    p = jax.nn.softmax(s, axis=-1)
    return jnp.einsum("bhqk,bkhd->bqhd", p.astype(v.dtype), v)


def sweep_attention(q, k, v, q_pos, k_pos, mask_kind, f_q=None, f_k=None):
    b, sq, h, _ = q.shape
    if sq % Q_BLOCK != 0:
        return attend(q, k, v, q_pos, k_pos, mask_kind, f_q, f_k)
    nb = sq // Q_BLOCK

    def to_blocks(a):
        return jnp.moveaxis(a.reshape((b, nb, Q_BLOCK) + a.shape[2:]), 1, 0)

    qb = to_blocks(q)
    pb = q_pos.reshape(nb, Q_BLOCK)
    if f_q is None:
        out = lax.map(lambda a: attend(a[0], k, v, a[1], k_pos, mask_kind), (qb, pb))
    else:
        out = lax.map(lambda a: attend(a[0], k, v, a[1], k_pos, mask_kind, a[2], f_k),
                      (qb, pb, to_blocks(f_q)))
    return jnp.moveaxis(out, 0, 1).reshape(b, sq, h, v.shape[-1])


def memory_kv(mem, g, w_mk, w_mv):
    b, n, _ = mem.shape
    m = rmsnorm(mem, g)
    return ((m @ w_mk).reshape(b, n, X_HEADS, X_HEAD_DIM),
            (m @ w_mv).reshape(b, n, X_HEADS, X_HEAD_DIM))


def cross_attention(h, mk, mv, w_xq, w_xo):
    b, s, _ = h.shape
    q = (h @ w_xq).reshape(b, s, X_HEADS, X_HEAD_DIM)
    sc = jnp.einsum("bqhd,bkhd->bhqk", q, mk).astype(jnp.float32) * (X_HEAD_DIM ** -0.5)
    p = jax.nn.softmax(sc, axis=-1)
    o = jnp.einsum("bhqk,bkhd->bqhd", p.astype(mv.dtype), mv)
    return o.reshape(b, s, X_HEADS * X_HEAD_DIM) @ w_xo


def setup_inputs(seed: int = 0) -> dict:
    key = jax.random.key(seed)
    ks = jax.random.split(key, 28)
    f32 = jnp.float32

    def nrm(i, shape, scale=1.0):
        return jax.random.normal(ks[i], shape, f32) * scale

    def gain(i, shape):
        return 1.0 + 0.1 * nrm(i, shape)

    return {
        "x_prompt": nrm(0, (BATCH, SEQ, D_MODEL)),
        "x_sample": nrm(1, (DEC_BATCH, DEC_SEQ, D_MODEL)),
        "mem_prompt": nrm(2, (BATCH, N_MEM, D_MODEL)),
        "cache_fox_k": nrm(3, (DEPTH, DEC_BATCH, PAST_LEN, FOX_HEADS, FOX_HEAD_DIM)),
        "cache_fox_v": nrm(4, (DEPTH, DEC_BATCH, PAST_LEN, FOX_HEADS, FOX_HEAD_DIM)),
        "cache_fox_logf": jax.nn.log_sigmoid(FORGET_BIAS + nrm(5, (DEPTH, DEC_BATCH, PAST_LEN, FOX_HEADS))),
        "cache_mla_ckv": nrm(6, (DEPTH, DEC_BATCH, PAST_LEN, MLA_KV_RANK)),
        "cache_mla_krope": nrm(7, (DEPTH, DEC_BATCH, PAST_LEN, MLA_ROPE_DIM)),
        "cache_mem_k": nrm(8, (DEPTH, DEC_BATCH, N_MEM, X_HEADS, X_HEAD_DIM)),
        "cache_mem_v": nrm(9, (DEPTH, DEC_BATCH, N_MEM, X_HEADS, X_HEAD_DIM)),
        "norm_mix": gain(10, (DEPTH, D_MODEL)),
        "w_in": nrm(11, (DEPTH, D_MODEL, IN_COLS), D_MODEL ** -0.5),
        "b_forget": FORGET_BIAS + 0.1 * nrm(12, (DEPTH, FOX_HEADS)),
        "mla_q_norm": gain(13, (DEPTH, MLA_Q_RANK)),
        "w_uq": nrm(14, (DEPTH, MLA_Q_RANK, MLA_HEADS * (MLA_NOPE_DIM + MLA_ROPE_DIM)), MLA_Q_RANK ** -0.5),
        "mla_kv_norm": gain(15, (DEPTH, MLA_KV_RANK)),
        "w_ukv": nrm(16, (DEPTH, MLA_KV_RANK, MLA_HEADS * (MLA_NOPE_DIM + MLA_V_DIM)), MLA_KV_RANK ** -0.5),
        "w_out": nrm(17, (DEPTH, MIX_WIDTH, D_MODEL), MIX_WIDTH ** -0.5),
        "norm_cross": gain(18, (DEPTH, D_MODEL)),
        "norm_mem": gain(19, (DEPTH, D_MODEL)),
        "w_xq": nrm(20, (DEPTH, D_MODEL, X_HEADS * X_HEAD_DIM), D_MODEL ** -0.5),
        "w_mk": nrm(21, (DEPTH, D_MODEL, X_HEADS * X_HEAD_DIM), D_MODEL ** -0.5),
        "w_mv": nrm(22, (DEPTH, D_MODEL, X_HEADS * X_HEAD_DIM), D_MODEL ** -0.5),
        "w_xo": nrm(23, (DEPTH, X_HEADS * X_HEAD_DIM, D_MODEL), (X_HEADS * X_HEAD_DIM) ** -0.5),
        "norm_mlp": gain(24, (DEPTH, D_MODEL)),
        "w_up": nrm(25, (DEPTH, D_MODEL, D_FF), D_MODEL ** -0.5),
        "w_down": nrm(26, (DEPTH, D_FF, D_MODEL), D_FF ** -0.5),
        "norm_final": gain(27, (D_MODEL,)),
    }


def reference(x_prompt, x_sample, mem_prompt, cache_fox_k, cache_fox_v, cache_fox_logf,
              cache_mla_ckv, cache_mla_krope, cache_mem_k, cache_mem_v,
              norm_mix, w_in, b_forget, mla_q_norm, w_uq, mla_kv_norm, w_ukv, w_out,
              norm_cross, norm_mem, w_xq, w_mk, w_mv, w_xo, norm_mlp, w_up, w_down, norm_final):

    def mixers(h, pos, l, past):
        b, s, _ = h.shape
        z = h @ w_in[l]
        q_f, k_f, v_f, g_f, c_q, c_kv, k_r = jnp.split(z, IN_SPLITS, axis=-1)
        q_f = q_f.reshape(b, s, FOX_HEADS, FOX_HEAD_DIM)
        k_f = k_f.reshape(b, s, FOX_HEADS, FOX_HEAD_DIM)
        v_f = v_f.reshape(b, s, FOX_HEADS, FOX_HEAD_DIM)
        logf = jax.nn.log_sigmoid((g_f + b_forget[l]).astype(jnp.float32))
        c_kv = rmsnorm(c_kv, mla_kv_norm[l])
        k_r = rope(k_r, pos)
        rows = (k_f, v_f, logf, c_kv, k_r)
        if past is None:
            kf, vf, lf, ckv, kr, k_pos = k_f, v_f, logf, c_kv, k_r, pos
        else:
            pk, pv, plf, pckv, pkr = past
            kf = jnp.concatenate([pk, k_f], axis=1)
            vf = jnp.concatenate([pv, v_f], axis=1)
            lf = jnp.concatenate([plf.astype(jnp.float32), logf], axis=1)
            ckv = jnp.concatenate([pckv, c_kv], axis=1)
            kr = jnp.concatenate([pkr, k_r], axis=1)
            k_pos = jnp.arange(pk.shape[1] + s)
        cum = jnp.cumsum(lf, axis=1)
        fox = sweep_attention(q_f, kf, vf, pos, k_pos, "frame", cum[:, -s:], cum)
        sk = ckv.shape[1]
        q_m = (rmsnorm(c_q, mla_q_norm[l]) @ w_uq[l]).reshape(b, s, MLA_HEADS, MLA_NOPE_DIM + MLA_ROPE_DIM)
        q_m = jnp.concatenate([q_m[..., :MLA_NOPE_DIM], rope(q_m[..., MLA_NOPE_DIM:], pos)], axis=-1)
        kv = (ckv @ w_ukv[l]).reshape(b, sk, MLA_HEADS, MLA_NOPE_DIM + MLA_V_DIM)
        k_m = jnp.concatenate([kv[..., :MLA_NOPE_DIM],
                               jnp.broadcast_to(kr[:, :, None, :], (b, sk, MLA_HEADS, MLA_ROPE_DIM))], axis=-1)
        mla = sweep_attention(q_m, k_m, kv[..., MLA_NOPE_DIM:], pos, k_pos, "chunk")
        mixed = jnp.concatenate([fox.reshape(b, s, FOX_WIDTH), mla.reshape(b, s, MLA_WIDTH)], axis=-1)
        return mixed @ w_out[l], rows

    def layer(x, pos, l, past, mk, mv):
        mix, rows = mixers(rmsnorm(x, norm_mix[l]), pos, l, past)
        x = x + mix
        x = x + cross_attention(rmsnorm(x, norm_cross[l]), mk, mv, w_xq[l], w_xo[l])
        hm = rmsnorm(x, norm_mlp[l])
        x = x + jnp.square(jax.nn.relu(hm @ w_up[l])) @ w_down[l]
        return x, rows

    pos_p = jnp.arange(x_prompt.shape[1])
    pos_s = cache_fox_k.shape[2] + jnp.arange(x_sample.shape[1])
    xp, xs = x_prompt, x_sample
    rows_p, rows_s, mk_p, mv_p = [], [], [], []
    for l in range(DEPTH):
        mk, mv = memory_kv(mem_prompt, norm_mem[l], w_mk[l], w_mv[l])
        xp, rp = layer(xp, pos_p, l, None, mk, mv)
        past = (cache_fox_k[l], cache_fox_v[l], cache_fox_logf[l], cache_mla_ckv[l], cache_mla_krope[l])
        xs, rs = layer(xs, pos_s, l, past, cache_mem_k[l], cache_mem_v[l])
        rows_p.append(rp)
        rows_s.append(rs)
        mk_p.append(mk)
        mv_p.append(mv)

    def stack(rows, i):
        return jnp.stack([r[i] for r in rows])

    y_prompt = rmsnorm(xp, norm_final)
    y_sample = rmsnorm(xs, norm_final)
    return (y_prompt, y_sample,
            stack(rows_p, 0), stack(rows_p, 1), stack(rows_p, 2), stack(rows_p, 3), stack(rows_p, 4),
            jnp.stack(mk_p), jnp.stack(mv_p),
            stack(rows_s, 0), stack(rows_s, 1), stack(rows_s, 2), stack(rows_s, 3), stack(rows_s, 4))
```

```python
import contextlib
import numpy as np
import concourse.bass as bass
import concourse.mybir as mybir
from concourse.bass_utils import run_bass_kernel_spmd

F32 = mybir.dt.float32
BF16 = mybir.dt.bfloat16
ALU = mybir.AluOpType
AF = mybir.ActivationFunctionType
AX = mybir.AxisListType

PE, ACT, DVE, POOL, SP = "tensor", "scalar", "vector", "gpsimd", "sync"
COMPUTE = (PE, ACT, DVE, POOL)


class Trk:
    __slots__ = ("name", "sem", "sem_cnt")

    def __init__(self, name):
        self.name = name
        self.sem = None
        self.sem_cnt = 0


class Buf:
    __slots__ = ("name", "w_eng", "w_dma", "r_eng", "r_dma", "trk")

    def __init__(self, name):
        self.name = name
        self.w_eng = {}
        self.w_dma = []
        self.r_eng = {}
        self.r_dma = []
        self.trk = {}


class Op:
    __slots__ = ("eng", "fn", "deps_eng", "deps_dma", "need_sig", "sig", "idx", "is_dma", "dma_buf", "dma_val")

    def __init__(self, eng, fn):
        self.eng = eng
        self.fn = fn
        self.deps_eng = {}
        self.deps_dma = {}
        self.need_sig = False
        self.sig = None
        self.is_dma = False
        self.dma_buf = None
        self.dma_val = 0


class Prog:
    def __init__(self, nc):
        self.nc = nc
        self.ops = {e: [] for e in (PE, ACT, DVE, POOL, SP)}
        self.nbuf = 0
        self.dma_bufs = []

    def buf(self, name=None):
        self.nbuf += 1
        return Buf(f"{name or 'b'}{self.nbuf}")

    def _dep(self, op, other):
        if other is op:
            return
        if other.is_dma:
            b = other.dma_buf
            if op.deps_dma.get(b, 0) < other.dma_val:
                op.deps_dma[b] = other.dma_val
        else:
            if other.eng == PE and op.eng == PE and not op.is_dma:
                return
            cur = op.deps_eng.get(other.eng)
            if cur is None or cur.idx < other.idx:
                op.deps_eng[other.eng] = other

    def add(self, eng, fn, reads=(), writes=(), dma=None, after=()):
        op = Op(eng, fn)
        op.idx = len(self.ops[eng])
        if dma is not None:
            op.is_dma = True
            t = dma.trk.get(eng)
            if t is None:
                t = Trk(f"{dma.name}_{eng[:2]}")
                dma.trk[eng] = t
                self.dma_bufs.append(t)
            op.dma_buf = t
            t.sem_cnt += 1
            op.dma_val = 16 * t.sem_cnt
        for o in after:
            self._dep(op, o)
        for b in reads:
            for w in b.w_eng.values():
                self._dep(op, w)
            for w in b.w_dma:
                self._dep(op, w)
        for b in writes:
            if b.r_eng or b.r_dma:
                for r in b.r_eng.values():
                    self._dep(op, r)
                for r in b.r_dma:
                    self._dep(op, r)
                b.w_eng = {}
                b.w_dma = []
                b.r_eng = {}
                b.r_dma = []
            else:
                for w in b.w_eng.values():
                    if w.eng != eng or op.is_dma:
                        self._dep(op, w)
                if not op.is_dma:
                    for w in b.w_dma:
                        self._dep(op, w)
        for b in reads:
            if op.is_dma:
                b.r_dma.append(op)
            else:
                b.r_eng[eng] = op
        for b in writes:
            if op.is_dma:
                b.w_dma.append(op)
            else:
                b.w_eng[eng] = op
        for o in op.deps_eng.values():
            o.need_sig = True
        self.ops[eng].append(op)
        return op

    def dma(self, q, out, in_, reads, writes, track, after=()):
        return self.add(q, lambda e: e.dma_start(out=out, in_=in_), reads=reads, writes=writes, dma=track, after=after)

    def emit(self, final_wait_bufs=()):
        nc = self.nc
        with contextlib.ExitStack() as st:
            esem = {e: st.enter_context(nc.semaphore(f"s_{e}")) for e in COMPUTE}
            for b in self.dma_bufs:
                b.sem = st.enter_context(nc.semaphore(f"d_{b.name}"))
            for e in COMPUTE:
                k = 0
                for op in self.ops[e]:
                    if op.need_sig:
                        k += 1
                        op.sig = k
            block = st.enter_context(nc.Block())
            prog = self

            def run(e, eng):
                waited = {}
                for op in prog.ops[e]:
                    for pe_, o in op.deps_eng.items():
                        s = esem[pe_]
                        if waited.get(s.num, 0) < o.sig:
                            eng.wait_ge(s, o.sig)
                            waited[s.num] = o.sig
                    for b, v in op.deps_dma.items():
                        if waited.get(b.sem.num, 0) < v:
                            eng.wait_ge(b.sem, v)
                            waited[b.sem.num] = v
                    ins = op.fn(eng)
                    if op.is_dma:
                        ins.then_inc(op.dma_buf.sem, 16)
                    elif op.need_sig:
                        ins.then_inc(esem[e], 1)
                if e == SP:
                    for b in final_wait_bufs:
                        for t in b.trk.values():
                            eng.wait_ge(t.sem, 16 * t.sem_cnt)

            @block.tensor
            def _(eng):
                run(PE, eng)

            @block.scalar
            def _(eng):
                run(ACT, eng)

            @block.vector
            def _(eng):
                run(DVE, eng)

            @block.gpsimd
            def _(eng):
                run(POOL, eng)

            @block.sync
            def _(eng):
                run(SP, eng)


D = 1024
S = 4096
NL = 2
PAST = 2048
NSB = 4
SQ = 32
T = 256
NT = S // T
DFF = 4096
EPS = 1e-6
IN_COLS = 1992
NEG = -30000.0
G_MIX, G_CROSS, G_MEM, G_MLP, G_QN = 0, 8, 16, 24, 32
G_L = 34
G_FINAL = 68
GC = 76
RB_BF = 0
RB_KVN = 16
RBC = 16 + 256


def build(dbg=None):
    dbg = dbg or {}
    stop = dbg.get('stop', 'end')
    nc = bass.Bass("TRN2", target_bir_lowering=False)
    P = Prog(nc)

    out_bufs = []

    def din(name, shape, dt=F32):
        return nc.dram_tensor(name, list(shape), dt, kind="ExternalInput").ap()

    def dout(name, shape):
        return nc.dram_tensor(name, list(shape), F32, kind="ExternalOutput").ap()

    def dscr(name, shape, dt):
        return nc.dram_tensor(name, list(shape), dt, kind="Internal").ap()

    def sb(name, shape, dt):
        return nc.alloc_sbuf_tensor("sb_" + name, list(shape), dt)

    xp = din("xp", [S, D]); xs = din("xs", [128, D]); mem = din("mem", [256, D])
    cfk = din("cfk", [NL, NSB, PAST, 512]); cfv = din("cfv", [NL, NSB, PAST, 512])
    clf = din("clf", [NL, NSB, PAST, 8]); cckv = din("cckv", [NL, NSB, PAST, 128])
    ckr = din("ckr", [NL, NSB, PAST, 64])
    cmk = din("cmk", [NL, NSB, 256, D]); cmv = din("cmv", [NL, NSB, 256, D])
    wsrc = {
        "win": din("w_in", [NL, D, IN_COLS]), "wuq": din("w_uq_r", [NL, 256, 1024]),
        "wukT": din("w_ukT", [NL, 512, 128]), "wuv": din("w_uv", [NL, 128, 512]),
        "wout": din("w_out", [NL, D, D]), "wxq": din("w_xq", [NL, D, D]), "wxo": din("w_xo", [NL, D, D]),
        "wmk": din("w_mk", [NL, D, D]), "wmv": din("w_mv", [NL, D, D]),
        "wup": din("w_up", [NL, D, DFF]), "wdown": din("w_down", [NL, DFF, D]),
    }
    gains_d = din("gains", [128, GC]); rowsb_d = din("rowsb", [128, RBC])
    cosT_d = din("cosT", [128, S + 128]); sinT_d = din("sinT", [128, S + 128])
    cstok_d = din("cstok", [S + SQ, 64]); sntok_d = din("sntok", [S + SQ, 64])

    y_p = dout("y_p", [S, D]); y_s = dout("y_s", [128, D])
    fk_p = dout("fk_p", [NL, S, 512]); fv_p = dout("fv_p", [NL, S, 512]); lf_p = dout("lf_p", [NL, S, 8])
    ckv_p = dout("ckv_p", [NL, S, 128]); kr_p = dout("kr_p", [NL, S, 64])
    mk_p = dout("mk_p", [NL, 256, D]); mv_p = dout("mv_p", [NL, 256, D])
    fk_s = dout("fk_s", [NL, 128, 512]); fv_s = dout("fv_s", [NL, 128, 512]); lf_s = dout("lf_s", [NL, 128, 8])
    ckv_s = dout("ckv_s", [NL, 128, 128]); kr_s = dout("kr_s", [NL, 128, 64])

    if dbg.get("dump"):
        dbgx = dout("dbgx", [D, T])
        dbgo = nc.dram_tensor("dbgo", [D, T], BF16, kind="ExternalOutput").ap()
    dump_b = P.buf("dump")

    def dump():
        if not dbg.get("dump"):
            return
        for c in range(8):
            P.dma(POOL, dbgx[c * 128:(c + 1) * 128, :], xT[:, c, :], [xT_b[c]], [], dump_b)
            P.dma(POOL, dbgo[c * 128:(c + 1) * 128, :], oT[:, c, :], [oT_b], [], dump_b)
        if dump_b not in out_bufs:
            out_bufs.append(dump_b)

    wscr = {k: dscr("s_" + k, list(v.shape), BF16) for k, v in wsrc.items()}
    _grp = {"wmk": 0, "wmv": 0, "win": 0, "wuq": 1, "wukT": 1, "wuv": 1, "wout": 1, "wxq": 1, "wxo": 1, "wup": 2, "wdown": 2}
    _gb = {(g, l): P.buf(f"ws{g}_{l}") for g in range(3) for l in range(NL)}
    wscr_b = {(k, l): _gb[(_grp[k] if l == 0 else 0, l)] for k in wsrc for l in range(NL)}
    xscr = nc.dram_tensor("xscr", [D, S], F32, kind=dbg.get("xscr_kind", "ExternalOutput")).ap()
    _xscr_one = P.buf("xscr")
    xscr_b = [_xscr_one for _ in range(NT)]

    ident_bf = sb("ident_bf", [128, 128], BF16); ident_f = sb("ident_f", [128, 128], F32)
    ones_bf = sb("ones_bf", [128, 128], BF16); ones_f = sb("ones_f", [128, 128], F32)
    tri_f = sb("tri_f", [128, 128], F32)
    mask_fox = sb("mask_fox", [128, 128], BF16)
    mask_mla = sb("mask_mla", [128, 128], BF16)
    epsc = sb("epsc", [128, 1], F32); onec = sb("onec", [128, 1], F32)
    gains = sb("gains", [128, GC], F32); rowsb = sb("rowsb", [128, RBC], F32)
    Bc = P.buf("consts")

    def cinit(eng, fn):
        P.add(eng, fn, writes=[Bc])

    cinit(POOL, lambda e: e.memset(ident_bf[:], 0.0))
    P.add(POOL, lambda e: e.affine_select(out=ident_bf[:], in_=ident_bf[:], pattern=[[-1, 128]],
                                          compare_op=ALU.not_equal, fill=1.0, base=0, channel_multiplier=1),
          reads=[Bc], writes=[Bc])
    cinit(POOL, lambda e: e.memset(ident_f[:], 0.0))
    P.add(POOL, lambda e: e.affine_select(out=ident_f[:], in_=ident_f[:], pattern=[[-1, 128]],
                                          compare_op=ALU.not_equal, fill=1.0, base=0, channel_multiplier=1),
          reads=[Bc], writes=[Bc])
    cinit(POOL, lambda e: e.memset(ones_bf[:], 1.0))
    cinit(POOL, lambda e: e.memset(ones_f[:], 1.0))
    cinit(POOL, lambda e: e.memset(tri_f[:], 1.0))
    P.add(POOL, lambda e: e.affine_select(out=tri_f[:], in_=tri_f[:], pattern=[[1, 128]],
                                          compare_op=ALU.is_ge, fill=0.0, base=0, channel_multiplier=-1),
          reads=[Bc], writes=[Bc])
    cinit(POOL, lambda e: e.memset(mask_fox[:], 0.0))
    P.add(POOL, lambda e: e.affine_select(out=mask_fox[:], in_=mask_fox[:], pattern=[[1, 128]],
                                          compare_op=ALU.is_ge, fill=NEG, base=0, channel_multiplier=-1),
          reads=[Bc], writes=[Bc])
    cinit(POOL, lambda e: e.memset(mask_mla[:], 0.0))
    P.add(POOL, lambda e: e.memset(mask_mla[64:128, 0:64], NEG), reads=[Bc], writes=[Bc])
    cinit(POOL, lambda e: e.memset(epsc[:], EPS))
    cinit(POOL, lambda e: e.memset(onec[:], 1.0))
    P.dma(SP, gains[:], gains_d, [], [Bc], Bc)
    P.dma(SP, rowsb[:], rowsb_d, [], [Bc], Bc)

    ps = [nc.alloc_psum_tensor(f"ps{i}", [128, 512], F32) for i in range(7)]
    psT = nc.alloc_psum_tensor("psT", [128, 1024], BF16)
    psb = [P.buf(f"psb{i}") for i in range(7)]
    psTb = P.buf("psTb")
    SHORT = [0, 1, 2, 3, 4]
    AC = [5, 6]
    rot = {"short": 0, "ac": 0}
    mlp_mode = [False]

    def bank(kind):
        if kind == "ac":
            i = AC[rot["ac"] % 2]
            rot["ac"] += 1
        else:
            lst = [0, 1, 2] if mlp_mode[0] else SHORT
            i = lst[rot["short"] % len(lst)]
            rot["short"] += 1
        return ps[i], psb[i]

    NKT = S // 128
    KT = sb("KT", [128, 4, S], BF16)
    VA = sb("VA", [128, NKT, 512], BF16)
    CKVT = sb("CKVT", [128, S], BF16)
    CKV = sb("CKV", [128, NKT, 128], BF16)
    KRT = sb("KRT", [128, S], BF16)
    CUM = sb("CUM", [128, NKT, 8], F32)
    cumtot = sb("cumtot", [128, 8], F32)
    cblk = sb("cblk", [128, 8], F32)
    biasall = sb("biasall", [128, NKT, 8], F32)
    store_b = [P.buf("store") for _ in range(NKT)]
    cum_b = [P.buf("cum") for _ in range(NKT)]
    cumtot_b = P.buf("cumtot"); cblk_b = P.buf("cblk"); bias_b = P.buf("bias")

    xy = sb("xy", [128, 2 * D], F32)
    xtok = [xy[:, i * D:(i + 1) * D] for i in range(2)]
    xtok_b = [P.buf("xtok") for _ in range(2)]
    xT = sb("xT", [128, 8, T], F32); xT_b = [P.buf("xT") for _ in range(8)]
    xTs = sb("xTs", [128, 8, 128], F32); xTs_b = [P.buf("xTs") for _ in range(8)]
    hT = sb("hT", [128, 8, T], BF16); hT_b = P.buf("hT")
    sq = [sb(f"sq{i}", [128, T], BF16) for i in range(2)]; sq_b = [P.buf("sq") for _ in range(2)]
    rstd = sb("rstd", [128, T], F32); nrm_b = P.buf("nrm")
    lnv = rstd
    QT = sb("QT", [128, 4, T], BF16); QT_b = P.buf("QT")
    cqf = sb("cqf", [128, 2, T], F32); cqf_b = P.buf("cqf")
    cqT = sb("cqT", [128, 2, T], BF16); cqT_b = P.buf("cqT")
    qnT = sb("qnT", [128, 4, T], BF16); qnT_b = P.buf("qnT")
    qaT = sb("qaT", [128, 4, T], BF16); qaT_b = P.buf("qaT")
    qrf = sb("qrf", [128, T], F32); qrsf = sb("qrsf", [128, T], F32); qrf_b = P.buf("qrf")
    qrT = sb("qrT", [128, 2, T], BF16); qrT_b = P.buf("qrT")
    oT = sb("oT", [128, 8, T], BF16); oT_b = P.buf("oT")
    olat = [sb(f"olat{i}", [128, T], BF16) for i in range(2)]; olat_b = [P.buf("olat") for _ in range(2)]
    rden = [sb(f"rden{i}", [128, T], F32) for i in range(2)]; rden_b = [P.buf("rden") for _ in range(2)]
    NPB = 3
    Pt = [sb(f"Pt{i}", [128, 512], BF16) for i in range(NPB)]; Pt_b = [P.buf("Pt") for _ in range(NPB)]
    prot = [0]
    cosT = sb("cosT", [128, T], F32); sinT = sb("sinT", [128, T], F32); rope_b = P.buf("rope")
    cstok = sb("cstok", [128, 64], F32); sntok = sb("sntok", [128, 64], F32); ropetok_b = P.buf("ropetok")
    ktok = [sb(f"ktok{i}", [128, 512], F32) for i in range(2)]; ktok_b = [P.buf("ktok") for _ in range(2)]
    vtok = [sb(f"vtok{i}", [128, 512], F32) for i in range(2)]; vtok_b = [P.buf("vtok") for _ in range(2)]
    smtok = [sb(f"smtok{i}", [128, 200], F32) for i in range(2)]; smtok_b = [P.buf("smtok") for _ in range(2)]
    kbf = sb("kbf", [128, 512], BF16); kbf_b = P.buf("kbf")
    ckbf = sb("ckbf", [128, 256], BF16); ckbf_b = P.buf("ckbf")
    tmpa = sb("tmpa", [128, 136], F32); tmpb = sb("tmpb", [128, 136], F32); tmp_b = P.buf("tmp")
    stat = sb("stat", [128, 4], F32); stat_b = P.buf("stat")
    rr = sb("rr", [128, T], F32); rr_b = P.buf("rr")
    actT = [sb(f"actT{i}", [128, T], BF16) for i in range(3)]; actT_b = [P.buf("actT") for _ in range(3)]
    NWB = 3
    wbuf = [sb(f"wbuf{i}", [128, 4096], BF16) for i in range(NWB)]; wbuf_b = [P.buf("wbuf") for _ in range(NWB)]
    wrot = [0]
    wuq_sb = sb("wuq_sb", [128, 2, 1024], BF16); wukT_sb = sb("wukT_sb", [128, 4, 128], BF16)
    wuv_sb = sb("wuv_sb", [128, 512], BF16); wmla_b = P.buf("wmla")
    MKT = sb("MKT", [128, 8, 256], BF16); MV = sb("MV", [128, 2, D], BF16); memkv_b = P.buf("memkv")
    mtok = sb("mtok", [128, D], F32); mtok_b = P.buf("mtok")
    ytok = sb("ytok", [128, D], F32); ytok_b = P.buf("ytok")
    yT = xy[:, :].rearrange("p (c t) -> p c t", c=8)
    rot2 = {"k": 0, "v": 0, "sm": 0, "act": 0, "xtok": 0, "sq": 0}

    def mm(out, lhsT, rhs, start, stop, reads, writes):
        P.add(PE, lambda e: e.matmul(out, lhsT=lhsT, rhs=rhs, start=start, stop=stop, skip_group_check=True),
              reads=reads, writes=writes)

    def tr(out, in_, ident, reads, writes):
        P.add(PE, lambda e: e.transpose(out=out, in_=in_, identity=ident), reads=reads, writes=writes)

    def tt(eng, out, in0, in1, op, reads, writes):
        P.add(eng, lambda e: e.tensor_tensor(out=out, in0=in0, in1=in1, op=op), reads=reads, writes=writes)

    def ts(eng, out, in_, scalar, op, reads, writes):
        P.add(eng, lambda e: e.tensor_single_scalar(out=out, in_=in_, scalar=scalar, op=op), reads=reads, writes=writes)

    def stt(eng, out, in0, scalar, in1, op0, op1, reads, writes):
        P.add(eng, lambda e: e.scalar_tensor_tensor(out=out, in0=in0, scalar=scalar, in1=in1, op0=op0, op1=op1),
              reads=reads, writes=writes)

    def cp(eng, out, in_, reads, writes):
        if eng == ACT:
            P.add(ACT, lambda e: e.copy(out=out, in_=in_), reads=reads, writes=writes)
        else:
            P.add(eng, lambda e: e.tensor_copy(out=out, in_=in_), reads=reads, writes=writes)

    def act(out, in_, func, reads, writes, bias=None, scale=1.0):
        if bias is None:
            P.add(ACT, lambda e: e.activation(out=out, in_=in_, func=func, scale=scale), reads=reads, writes=writes)
        else:
            P.add(ACT, lambda e: e.activation(out=out, in_=in_, func=func, bias=bias, scale=scale),
                  reads=reads, writes=writes)

    STQ = {"sp": SP, "pool": POOL, "act": ACT}[dbg.get("stq", "sp")]

    def store(dst, src, src_bufs, track):
        P.dma(STQ, dst, src, src_bufs, [], track)
        if track not in out_bufs:
            out_bufs.append(track)

    def load_w(key, l, src3):
        i = wrot[0] % NWB
        wrot[0] += 1
        a, b = src3.shape[1], src3.shape[2]
        view = wbuf[i][:, 0:a * b].rearrange("p (a b) -> p a b", a=a)
        P.dma(SP, view, src3, [wscr_b[(key, l)]], [wbuf_b[i]], wbuf_b[i])
        return view, wbuf_b[i]

    def wrows(key, l):
        return wscr[key][l].rearrange("(c p) n -> p c n", p=128)

    order = ["wmk", "wmv", "win", "wuq", "wukT", "wuv", "wout", "wxq", "wxo", "wup", "wdown"]
    cast_ops = []
    CAST_DEPTH = dbg.get("cast_depth", 2)
    for l in range(NL):
        for k in order:
            rows = wsrc[k].shape[1]
            nsplit = 4 if rows >= 1024 else 1
            step = rows // nsplit
            for j in range(nsplit):
                op_ = P.dma(POOL, wscr[k][l, j * step:(j + 1) * step, :], wsrc[k][l, j * step:(j + 1) * step, :],
                            [], [wscr_b[(k, l)]], wscr_b[(k, l)], after=cast_ops[-CAST_DEPTH:-CAST_DEPTH + 1] if len(cast_ops) >= CAST_DEPTH else ())
                cast_ops.append(op_)

    def rmsnorm_T(src, src_bufs, nch, Tn, gcol, dst, dst_buf, Dn):
        pb, pbb = bank("gen")
        for c in range(nch):
            i = rot2["sq"] % 2
            rot2["sq"] += 1
            sbc = src_bufs[c] if isinstance(src_bufs[c], list) else [src_bufs[c]]
            tt(POOL, sq[i][:, :Tn], src[:, c, :Tn], src[:, c, :Tn], ALU.mult, sbc, [sq_b[i]])
            mm(pb[:, :Tn], ones_bf[:], sq[i][:, :Tn], c == 0, c == nch - 1, [sq_b[i], Bc], [pbb])
        act(lnv[:, :Tn], pb[:, :Tn], AF.Ln, [pbb, Bc], [nrm_b], bias=epsc[:, 0:1], scale=1.0 / Dn)
        act(rstd[:, :Tn], lnv[:, :Tn], AF.Exp, [nrm_b], [nrm_b], scale=-0.5)
        for c in range(nch):
            sbc = src_bufs[c] if isinstance(src_bufs[c], list) else [src_bufs[c]]
            stt(DVE, dst[:, c, :Tn], src[:, c, :Tn], gains[:, gcol + c:gcol + c + 1], rstd[:, :Tn],
                ALU.mult, ALU.mult, sbc + [nrm_b, Bc], [dst_buf])

    def ingest_k(kt, R, src, src_buf):
        cp(POOL, kbf[:R, :], src, [src_buf], [kbf_b])
        for p in range(4):
            tr(psT[:, p * 128:p * 128 + R], kbf[:R, p * 128:(p + 1) * 128], ident_bf[:R, :R], [kbf_b, Bc], [psTb])
        P.add(DVE, lambda e: e.tensor_copy(
            out=KT[:, :, kt * 128:kt * 128 + R],
            in_=psT[:, 0:512].rearrange("p (a b) -> p a b", a=4)[:, :, 0:R]), reads=[psTb], writes=[store_b[kt]])

    def ingest_v(kt, R, src, src_buf):
        cp(POOL, VA[:R, kt, :], src, [src_buf], [store_b[kt]])

    def ingest_ckv_kr(kt, R, ckv_src, kr_src, src_buf):
        cp(POOL, ckbf[:R, 0:128], ckv_src, [src_buf], [ckbf_b])
        cp(POOL, ckbf[:R, 128:192], kr_src, [src_buf], [ckbf_b])
        cp(POOL, ckbf[:R, 192:256], kr_src, [src_buf], [ckbf_b])
        cp(POOL, CKV[:R, kt, :], ckbf[:R, 0:128], [ckbf_b], [store_b[kt]])
        tr(psT[:, 512:512 + R], ckbf[:R, 0:128], ident_bf[:R, :R], [ckbf_b, Bc], [psTb])
        tr(psT[:, 640:640 + R], ckbf[:R, 128:256], ident_bf[:R, :R], [ckbf_b, Bc], [psTb])
        cp(DVE, CKVT[:, kt * 128:kt * 128 + R], psT[:, 512:512 + R], [psTb], [store_b[kt]])
        cp(DVE, KRT[:, kt * 128:kt * 128 + R], psT[:, 640:640 + R], [psTb], [store_b[kt]])

    def ingest_logf(kt, R, src, src_buf, first):
        if first:
            P.add(POOL, lambda e: e.memset(cumtot[:], 0.0), writes=[cumtot_b])
        pm, pmb = bank("gen")
        mm(pm[:R, 0:8], tri_f[:R, :R], src, True, True, [src_buf, Bc], [pmb])
        mm(pm[:, 8:16], ones_f[:R, :], src, False, True, [src_buf, Bc], [pmb])
        tt(DVE, CUM[:R, kt, :], pm[:R, 0:8], cumtot[:R, :], ALU.add, [pmb, cumtot_b], [cum_b[kt]])
        tt(DVE, cumtot[:], pm[:, 8:16], cumtot[:], ALU.add, [pmb, cumtot_b], [cumtot_b])

    def next_P():
        i = prot[0] % NPB
        prot[0] += 1
        return Pt[i], Pt_b[i]

    def fox_attention(pr, q0, Tq, tiles, diag):
        accs = [bank("ac"), bank("ac")]
        n = len(tiles)
        sc = {}

        def qk(i):
            kt, R = tiles[i]
            c0 = diag.get(kt, 0)
            bs = [bank("sc"), bank("sc")]
            for hh in range(2):
                r0 = hh * 64
                sb_, sbb = bs[hh]
                mm(sb_[:R, c0:Tq], KT[r0:r0 + 64, pr, kt * 128:kt * 128 + R],
                   QT[r0:r0 + 64, pr, q0 + c0:q0 + Tq], True, kt not in diag, [store_b[kt], QT_b], [sbb])
            if kt in diag:
                w = min(128, Tq - c0)
                for hh in range(2):
                    sb_, sbb = bs[hh]
                    mm(sb_[:R, c0:c0 + w], ident_bf[:R, :R], mask_fox[:R, 0:w], False, True, [Bc], [sbb])
            sc[i] = (bs, c0)

        def pv(i):
            kt, R = tiles[i]
            bs, c0 = sc.pop(i)
            pt, ptb = next_P()
            for hh in range(2):
                h = pr * 2 + hh
                sb_, sbb = bs[hh]
                act(pt[:R, hh * 256 + c0:hh * 256 + Tq], sb_[:R, c0:Tq], AF.Exp,
                    [sbb, bias_b], [ptb], bias=biasall[:R, kt, h:h + 1], scale=1.0)
            for hh in range(2):
                h = pr * 2 + hh
                ab, abb = accs[hh]
                mm(ab[0:64, c0:Tq], VA[:R, kt, h * 64:(h + 1) * 64],
                   pt[:R, hh * 256 + c0:hh * 256 + Tq], i == 0, False, [store_b[kt], ptb], [abb])
                mm(ab[0:64, 256 + c0:256 + Tq], ones_bf[:R, 0:64],
                   pt[:R, hh * 256 + c0:hh * 256 + Tq], False, i == n - 1, [Bc, ptb], [abb])

        qk(0)
        for i in range(n):
            if i + 1 < n:
                qk(i + 1)
            pv(i)
        for hh in range(2):
            ab, abb = accs[hh]
            P.add(DVE, lambda e, ab=ab, hh=hh: e.reciprocal(out=rden[hh][0:64, :Tq], in_=ab[0:64, 256:256 + Tq]),
                  reads=[abb], writes=[rden_b[hh]])
            tt(DVE, oT[hh * 64:hh * 64 + 64, pr, q0:q0 + Tq], ab[0:64, 0:Tq], rden[hh][0:64, :Tq], ALU.mult,
               [abb, rden_b[hh]], [oT_b])

    def mla_attention(pr, q0, Tq, tiles, diag, l):
        accs = [bank("ac"), bank("ac")]
        n = len(tiles)
        sc = {}

        def qk(i):
            kt, R = tiles[i]
            c0 = diag.get(kt, 0)
            bs = [bank("sc"), bank("sc")]
            for hh in range(2):
                h = pr * 2 + hh
                sb_, sbb = bs[hh]
                mm(sb_[:R, c0:Tq], CKVT[:, kt * 128:kt * 128 + R], qaT[:, h, q0 + c0:q0 + Tq],
                   True, False, [store_b[kt], qaT_b], [sbb])
            for hh in range(2):
                r0 = hh * 64
                sb_, sbb = bs[hh]
                mm(sb_[:R, c0:Tq], KRT[r0:r0 + 64, kt * 128:kt * 128 + R],
                   qrT[r0:r0 + 64, pr, q0 + c0:q0 + Tq], False, kt not in diag, [store_b[kt], qrT_b], [sbb])
            if kt in diag:
                w = min(128, Tq - c0)
                for hh in range(2):
                    sb_, sbb = bs[hh]
                    mm(sb_[:R, c0:c0 + w], ident_bf[:R, :R], mask_mla[:R, 0:w], False, True, [Bc], [sbb])
            sc[i] = (bs, c0)

        def pv(i):
            kt, R = tiles[i]
            bs, c0 = sc.pop(i)
            pt, ptb = next_P()
            for hh in range(2):
                sb_, sbb = bs[hh]
                act(pt[:R, hh * 256 + c0:hh * 256 + Tq], sb_[:R, c0:Tq], AF.Exp, [sbb], [ptb])
            for hh in range(2):
                ab, abb = accs[hh]
                mm(ab[:, c0:Tq], CKV[:R, kt, :], pt[:R, hh * 256 + c0:hh * 256 + Tq], i == 0, False,
                   [store_b[kt], ptb], [abb])
                mm(ab[:, 256 + c0:256 + Tq], ones_bf[:R, :], pt[:R, hh * 256 + c0:hh * 256 + Tq], False, i == n - 1,
                   [Bc, ptb], [abb])

        qk(0)
        for i in range(n):
            if i + 1 < n:
                qk(i + 1)
            pv(i)
        for hh in range(2):
            h = pr * 2 + hh
            ab, abb = accs[hh]
            P.add(DVE, lambda e, ab=ab, hh=hh: e.reciprocal(out=rden[hh][:, :Tq], in_=ab[:, 256:256 + Tq]),
                  reads=[abb], writes=[rden_b[hh]])
            tt(DVE, olat[hh][:, :Tq], ab[:, 0:Tq], rden[hh][:, :Tq], ALU.mult, [abb, rden_b[hh]], [olat_b[hh]])
            gb, gbb = bank("gen")
            mm(gb[:, :Tq], wuv_sb[:, h * 128:(h + 1) * 128], olat[hh][:, :Tq], True, True, [wmla_b, olat_b[hh]], [gbb])
            cp(ACT, oT[:, 4 + h, q0:q0 + Tq], gb[:, :Tq], [gbb], [oT_b])

    def cross_attention(q0, Tq):
        for h in range(4):
            sb_, sbb = bank("sc")
            for j in range(2):
                for c in range(2):
                    ch = h * 2 + c
                    src = (qnT if ch < 4 else qaT)[:, ch % 4, q0:q0 + Tq]
                    mm(sb_[:, j * 256:j * 256 + Tq], MKT[:, ch, j * 128:(j + 1) * 128], src,
                       j == 0 and c == 0, c == 1, [memkv_b, qnT_b, qaT_b], [sbb])
            pt, ptb = next_P()
            if Tq == 256:
                act(pt[:, :], sb_[:, :], AF.Exp, [sbb], [ptb])
            else:
                for j in range(2):
                    act(pt[:, j * 256:j * 256 + Tq], sb_[:, j * 256:j * 256 + Tq], AF.Exp, [sbb], [ptb])
            ab, abb = bank("ac")
            db, dbb = bank("ac")
            for c in range(2):
                for j in range(2):
                    mm(ab[:, c * 256:c * 256 + Tq], MV[:, j, h * 256 + c * 128:h * 256 + (c + 1) * 128],
                       pt[:, j * 256:j * 256 + Tq], c == 0 and j == 0, j == 1, [memkv_b, ptb], [abb])
            for j in range(2):
                mm(db[:, 0:Tq], ones_bf[:], pt[:, j * 256:j * 256 + Tq], j == 0, j == 1, [Bc, ptb], [dbb])
            P.add(DVE, lambda e, db=db: e.reciprocal(out=rden[0][:, :Tq], in_=db[:, 0:Tq]), reads=[dbb], writes=[rden_b[0]])
            for c in range(2):
                tt(DVE, oT[:, h * 2 + c, q0:q0 + Tq], ab[:, c * 256:c * 256 + Tq], rden[0][:, :Tq], ALU.mult,
                   [abb, rden_b[0]], [oT_b])

    def ingest_mem(j, which):
        if which == "v":
            cp(POOL, MV[:, j, :], mtok[:], [mtok_b], [memkv_b])
            return
        cp(POOL, kbf[:, :], mtok[:, 0:512], [mtok_b], [kbf_b])
        for p in range(4):
            tr(psT[:, p * 128:(p + 1) * 128], kbf[:, p * 128:(p + 1) * 128], ident_bf[:], [kbf_b, Bc], [psTb])
        P.add(DVE, lambda e: e.tensor_copy(out=MKT[:, 0:4, j * 128:(j + 1) * 128],
                                           in_=psT[:, 0:512].rearrange("p (a b) -> p a b", a=4)),
              reads=[psTb], writes=[memkv_b])
        cp(POOL, kbf[:, :], mtok[:, 512:1024], [mtok_b], [kbf_b])
        for p in range(4):
            tr(psT[:, p * 128:(p + 1) * 128], kbf[:, p * 128:(p + 1) * 128], ident_bf[:], [kbf_b, Bc], [psTb])
        P.add(DVE, lambda e: e.tensor_copy(out=MKT[:, 4:8, j * 128:(j + 1) * 128],
                                           in_=psT[:, 0:512].rearrange("p (a b) -> p a b", a=4)),
              reads=[psTb], writes=[memkv_b])

    def prompt_memory_kv(l):
        for j in range(2):
            P.dma(SP, mtok[:], mem[j * 128:(j + 1) * 128, :], [], [mtok_b], mtok_b)
            for half in range(2):
                gb, gbb = bank("gen")
                for c in range(4):
                    tr(gb[:, c * 128:(c + 1) * 128], mtok[:, (half * 4 + c) * 128:(half * 4 + c + 1) * 128], ident_f[:],
                       [mtok_b, Bc], [gbb])
                for c in range(4):
                    cp(DVE, yT[:, half * 4 + c, j * 128:(j + 1) * 128], gb[:, c * 128:(c + 1) * 128], [gbb], xtok_b)
        rmsnorm_T(yT, [xtok_b] * 8, 8, 256, l * G_L + G_MEM, hT, hT_b, D)
        for which, key, dst in (("k", "wmk", mk_p), ("v", "wmv", mv_p)):
            for half in range(2):
                wv, wvb = load_w(key, l, wrows(key, l)[:, :, half * 512:(half + 1) * 512])
                for j in range(2):
                    gb, gbb = bank("gen")
                    for kc in range(8):
                        mm(gb[:, :], hT[:, kc, j * 128:(j + 1) * 128], wv[:, kc, :], kc == 0, kc == 7, [hT_b, wvb], [gbb])
                    i = rot2["k"] % 2
                    rot2["k"] += 1
                    cp(DVE, ktok[i][:], gb[:, :], [gbb], [ktok_b[i]])
                    store(dst[l, j * 128:(j + 1) * 128, half * 512:(half + 1) * 512], ktok[i][:], [ktok_b[i]], ktok_b[i])
                    if which == "v":
                        cp(POOL, MV[:, j, half * 512:(half + 1) * 512], ktok[i][:], [ktok_b[i]], [memkv_b])
                    else:
                        cp(POOL, kbf[:, :], ktok[i][:], [ktok_b[i]], [kbf_b])
                        for p in range(4):
                            tr(psT[:, p * 128:(p + 1) * 128], kbf[:, p * 128:(p + 1) * 128], ident_bf[:], [kbf_b, Bc], [psTb])
                        P.add(DVE, lambda e, half=half, j=j: e.tensor_copy(
                            out=MKT[:, half * 4:half * 4 + 4, j * 128:(j + 1) * 128],
                            in_=psT[:, 0:512].rearrange("p (a b) -> p a b", a=4)), reads=[psTb], writes=[memkv_b])

    def sample_memory_kv(l, b):
        for j in range(2):
            P.dma(SP, mtok[:], cmk[l, b, j * 128:(j + 1) * 128, :], [], [mtok_b], mtok_b)
            ingest_mem(j, "k")
            P.dma(SP, mtok[:], cmv[l, b, j * 128:(j + 1) * 128, :], [], [mtok_b], mtok_b)
            ingest_mem(j, "v")

    def load_mla_weights(l):
        P.dma(SP, wuq_sb[:], wscr["wuq"][l].rearrange("(c p) n -> p c n", p=128), [wscr_b[("wuq", l)]], [wmla_b], wmla_b)
        P.dma(SP, wukT_sb[:], wscr["wukT"][l].rearrange("(c p) n -> p c n", p=128), [wscr_b[("wukT", l)]], [wmla_b], wmla_b)
        P.dma(SP, wuv_sb[:], wscr["wuv"][l], [wscr_b[("wuv", l)]], [wmla_b], wmla_b)

    def phase_A(l, X, X_b, Tn, tcol):
        rmsnorm_T(X, X_b, 8, Tn, l * G_L + G_MIX, hT, hT_b, D)
        P.dma(SP, cosT[:, :Tn], cosT_d[:, tcol:tcol + Tn], [], [rope_b], rope_b)
        P.dma(SP, sinT[:, :Tn], sinT_d[:, tcol:tcol + Tn], [], [rope_b], rope_b)
        win3 = wrows("win", l)
        wq, wqb = load_w("win", l, win3[:, :, 0:512])
        for p in range(4):
            gb, gbb = bank("gen")
            for kc in range(8):
                mm(gb[:, :Tn], wq[:, kc, p * 128:(p + 1) * 128], hT[:, kc, :Tn], kc == 0, kc == 7, [wqb, hT_b], [gbb])
            ts(DVE, QT[:, p, :Tn], gb[:, :Tn], 0.125, ALU.mult, [gbb], [QT_b])
        wr, wrb = load_w("win", l, win3[:, :, 1536:1992])
        for c in range(2):
            gb, gbb = bank("gen")
            for kc in range(8):
                mm(gb[:, :Tn], wr[:, kc, 8 + c * 128:8 + (c + 1) * 128], hT[:, kc, :Tn], kc == 0, kc == 7, [wrb, hT_b], [gbb])
            cp(ACT, cqf[:, c, :Tn], gb[:, :Tn], [gbb], [cqf_b])
        rmsnorm_T(cqf, [cqf_b] * 2, 2, Tn, l * G_L + G_QN, cqT, cqT_b, 256)
        qs = 192.0 ** -0.5
        for h in range(4):
            gb, gbb = bank("gen")
            for kc in range(2):
                mm(gb[:, :Tn], wuq_sb[:, kc, h * 128:(h + 1) * 128], cqT[:, kc, :Tn], kc == 0, kc == 1, [wmla_b, cqT_b], [gbb])
            cp(ACT, qnT[:, h, :Tn], gb[:, :Tn], [gbb], [qnT_b])
        for h in range(4):
            gb, gbb = bank("gen")
            mm(gb[:, :Tn], wukT_sb[:, h, :], qnT[:, h, :Tn], True, True, [wmla_b, qnT_b], [gbb])
            ts(DVE, qaT[:, h, :Tn], gb[:, :Tn], qs, ALU.mult, [gbb], [qaT_b])
        for p in range(2):
            for sw, dstt in ((0, qrf), (1, qrsf)):
                gb, gbb = bank("gen")
                c0 = 512 + sw * 256 + p * 128
                for kc in range(2):
                    mm(gb[:, :Tn], wuq_sb[:, kc, c0:c0 + 128], cqT[:, kc, :Tn], kc == 0, kc == 1, [wmla_b, cqT_b], [gbb])
                stt(DVE, dstt[:, :Tn], gb[:, :Tn], qs, (cosT if sw == 0 else sinT)[:, :Tn], ALU.mult, ALU.mult,
                    [gbb, rope_b], [qrf_b])
            tt(DVE, qrT[:, p, :Tn], qrf[:, :Tn], qrsf[:, :Tn], ALU.add, [qrf_b], [qrT_b])
        return None

    def phase_rows(l, W, groups, outs):
        fk, fv, lf, ckvo, kro = outs
        win3 = wrows("win", l)
        wk, wkb = load_w("win", l, win3[:, :, 512:1024])
        for (r0, R, kt, first, row0, trow, snap) in groups:
            gb, gbb = bank("gen")
            for kc in range(8):
                mm(gb[:R, :], hT[:, kc, r0:r0 + R], wk[:, kc, :], kc == 0, kc == 7, [hT_b, wkb], [gbb])
            i = rot2["k"] % 2
            rot2["k"] += 1
            cp(ACT, ktok[i][:R, :], gb[:R, :], [gbb], [ktok_b[i]])
            store(fk[row0:row0 + R, :], ktok[i][:R, :], [ktok_b[i]], ktok_b[i])
            ingest_k(kt, R, ktok[i][:R, :], ktok_b[i])
        wv_, wvb = load_w("win", l, win3[:, :, 1024:1536])
        for (r0, R, kt, first, row0, trow, snap) in groups:
            gb, gbb = bank("gen")
            for kc in range(8):
                mm(gb[:R, :], hT[:, kc, r0:r0 + R], wv_[:, kc, :], kc == 0, kc == 7, [hT_b, wvb], [gbb])
            i = rot2["v"] % 2
            rot2["v"] += 1
            cp(ACT, vtok[i][:R, :], gb[:R, :], [gbb], [vtok_b[i]])
            store(fv[row0:row0 + R, :], vtok[i][:R, :], [vtok_b[i]], vtok_b[i])
            ingest_v(kt, R, vtok[i][:R, :], vtok_b[i])
        wr, wrb = load_w("win", l, win3[:, :, 1536:1992])
        for (r0, R, kt, first, row0, trow, snap) in groups:
            gb, gbb = bank("gen")
            for kc in range(8):
                mm(gb[:R, 0:456], hT[:, kc, r0:r0 + R], wr[:, kc, :], kc == 0, kc == 7, [hT_b, wrb], [gbb])
            i = rot2["sm"] % 2
            rot2["sm"] += 1
            sm, smb = smtok[i], smtok_b[i]
            tt(DVE, tmpa[:R, 0:8], gb[:R, 0:8], rowsb[:R, RB_BF + l * 8:RB_BF + l * 8 + 8], ALU.add, [gbb, Bc], [tmp_b])
            act(tmpa[:R, 0:8], tmpa[:R, 0:8], AF.Exp, [tmp_b], [tmp_b], scale=-1.0)
            act(tmpa[:R, 0:8], tmpa[:R, 0:8], AF.Ln, [tmp_b, Bc], [tmp_b], bias=onec[:R, 0:1], scale=1.0)
            ts(DVE, sm[:R, 0:8], tmpa[:R, 0:8], -1.0, ALU.mult, [tmp_b], [smb])
            cp(DVE, tmpa[:R, 8:136], gb[:R, 264:392], [gbb], [tmp_b])
            tt(DVE, tmpb[:R, 8:136], tmpa[:R, 8:136], tmpa[:R, 8:136], ALU.mult, [tmp_b], [tmp_b])
            P.add(DVE, lambda e, R=R: e.reduce_sum(out=stat[:R, 0:1], in_=tmpb[:R, 8:136], axis=AX.X),
                  reads=[tmp_b], writes=[stat_b])
            act(stat[:R, 1:2], stat[:R, 0:1], AF.Ln, [stat_b, Bc], [stat_b], bias=epsc[:R, 0:1], scale=1.0 / 128)
            act(stat[:R, 2:3], stat[:R, 1:2], AF.Exp, [stat_b], [stat_b], scale=-0.5)
            stt(DVE, sm[:R, 8:136], tmpa[:R, 8:136], stat[:R, 2:3], rowsb[:R, RB_KVN + l * 128:RB_KVN + (l + 1) * 128],
                ALU.mult, ALU.mult, [tmp_b, stat_b, Bc], [smb])
            P.dma(SP, cstok[:R, :], cstok_d[trow:trow + R, :], [], [ropetok_b], ropetok_b)
            P.dma(SP, sntok[:R, :], sntok_d[trow:trow + R, :], [], [ropetok_b], ropetok_b)
            tt(DVE, sm[:R, 136:200], gb[:R, 392:456], cstok[:R, :], ALU.mult, [gbb, ropetok_b], [smb])
            tt(DVE, tmpb[:R, 72:104], gb[:R, 424:456], sntok[:R, 0:32], ALU.mult, [gbb, ropetok_b], [tmp_b])
            tt(DVE, tmpb[:R, 104:136], gb[:R, 392:424], sntok[:R, 32:64], ALU.mult, [gbb, ropetok_b], [tmp_b])
            tt(DVE, sm[:R, 136:200], sm[:R, 136:200], tmpb[:R, 72:136], ALU.add, [smb, tmp_b], [smb])
            store(lf[row0:row0 + R, :], sm[:R, 0:8], [smb], smb)
            store(ckvo[row0:row0 + R, :], sm[:R, 8:136], [smb], smb)
            store(kro[row0:row0 + R, :], sm[:R, 136:200], [smb], smb)
            ingest_logf(kt, R, sm[:R, 0:8], smb, first)
            if snap:
                cp(DVE, cblk[:], cumtot[:], [cumtot_b], [cblk_b])
            ingest_ckv_kr(kt, R, sm[:R, 8:136], sm[:R, 136:200], smb)

    def proj_residual(key, l, src, src_b, Tn, X, X_b):
        w3 = wrows(key, l)
        for half in range(2):
            wv, wvb = load_w(key, l, w3[:, :, half * 512:(half + 1) * 512])
            for dd in range(4):
                d = half * 4 + dd
                gb, gbb = bank("gen")
                for kc in range(8):
                    mm(gb[:, :Tn], wv[:, kc, dd * 128:(dd + 1) * 128], src[:, kc, :Tn], kc == 0, kc == 7, [wvb, src_b], [gbb])
                tt(DVE, X[:, d, :Tn], gb[:, :Tn], X[:, d, :Tn], ALU.add, [gbb, X_b[d]], [X_b[d]])

    def proj_xq(l, Tn):
        w3 = wrows("wxq", l)
        for half in range(2):
            wv, wvb = load_w("wxq", l, w3[:, :, half * 512:(half + 1) * 512])
            for dd in range(4):
                gb, gbb = bank("gen")
                for kc in range(8):
                    mm(gb[:, :Tn], wv[:, kc, dd * 128:(dd + 1) * 128], hT[:, kc, :Tn], kc == 0, kc == 7, [wvb, hT_b], [gbb])
                dst, dstb = (qnT, qnT_b) if half == 0 else (qaT, qaT_b)
                ts(DVE, dst[:, dd, :Tn], gb[:, :Tn], 1.0 / 16.0, ALU.mult, [gbb], [dstb])

    def mlp(l, Tn, X, X_b):
        rmsnorm_T(X, X_b, 8, Tn, l * G_L + G_MLP, hT, hT_b, D)
        up3 = wrows("wup", l)
        dn3 = wscr["wdown"][l].rearrange("(g j p) n -> g p j n", j=4, p=128)
        accb = [(ps[i], psb[i]) for i in (3, 4, 5, 6)]
        mlp_mode[0] = True
        pend = None

        def down(fi, ai, wd, wdb):
            for d in range(8):
                ab, abb = accb[d // 2]
                mm(ab[:, (d % 2) * 256:(d % 2) * 256 + Tn], wd[:, fi % 4, d * 128:(d + 1) * 128], actT[ai][:, :Tn],
                   fi == 0 and d % 2 == 0, fi == 31, [wdb, actT_b[ai]], [abb])

        for g in range(8):
            wu, wub = load_w("wup", l, up3[:, :, g * 512:(g + 1) * 512])
            wd, wdb = load_w("wdown", l, dn3[g])
            for j in range(4):
                fi = g * 4 + j
                gb, gbb = bank("gen")
                for kc in range(8):
                    mm(gb[:, :Tn], wu[:, kc, j * 128:(j + 1) * 128], hT[:, kc, :Tn], kc == 0, kc == 7, [wub, hT_b], [gbb])
                if pend is not None:
                    down(*pend)
                ai = rot2["act"] % 3
                rot2["act"] += 1
                ts(DVE, rr[:, :Tn], gb[:, :Tn], 0.0, ALU.max, [gbb], [rr_b])
                tt(POOL, actT[ai][:, :Tn], rr[:, :Tn], rr[:, :Tn], ALU.mult, [rr_b], [actT_b[ai]])
                pend = (fi, ai, wd, wdb)
        down(*pend)
        for d in range(8):
            ab, abb = accb[d // 2]
            tt(DVE, X[:, d, :Tn], ab[:, (d % 2) * 256:(d % 2) * 256 + Tn], X[:, d, :Tn], ALU.add, [abb, X_b[d]], [X_b[d]])
        mlp_mode[0] = False

    def final_out(X, X_b, Tn, dst, row0):
        rmsnorm_f32(X, X_b, Tn)
        for sub in range(Tn // 128):
            for half in range(2):
                gb, gbb = bank("gen")
                for c in range(4):
                    tr(gb[:, c * 128:(c + 1) * 128], yT[:, half * 4 + c, sub * 128:(sub + 1) * 128], ident_f[:], xtok_b + [Bc], [gbb])
                cp(ACT, ytok[:, half * 512:(half + 1) * 512], gb[:, :], [gbb], [ytok_b])
            store(dst[row0 + sub * 128:row0 + (sub + 1) * 128, :], ytok[:], [ytok_b], ytok_b)

    def rmsnorm_f32(X, X_b, Tn):
        pb, pbb = bank("gen")
        for c in range(8):
            i = rot2["sq"] % 2
            rot2["sq"] += 1
            tt(POOL, sq[i][:, :Tn], X[:, c, :Tn], X[:, c, :Tn], ALU.mult, [X_b[c]], [sq_b[i]])
            mm(pb[:, :Tn], ones_bf[:], sq[i][:, :Tn], c == 0, c == 7, [sq_b[i], Bc], [pbb])
        act(lnv[:, :Tn], pb[:, :Tn], AF.Ln, [pbb, Bc], [nrm_b], bias=epsc[:, 0:1], scale=1.0 / D)
        act(rstd[:, :Tn], lnv[:, :Tn], AF.Exp, [nrm_b], [nrm_b], scale=-0.5)
        for c in range(8):
            stt(DVE, yT[:, c, :Tn], X[:, c, :Tn], gains[:, G_FINAL + c:G_FINAL + c + 1], rstd[:, :Tn],
                ALU.mult, ALU.mult, [X_b[c], nrm_b, Bc], xtok_b)

    def compute_bias(nkt, mid_note=None):
        P.add(DVE, lambda e: e.tensor_tensor(out=biasall[:, 0:nkt, :],
                                             in0=cblk[:].unsqueeze(1).broadcast_to([128, nkt, 8]),
                                             in1=CUM[:, 0:nkt, :], op=ALU.subtract),
              reads=[cblk_b] + cum_b[0:nkt], writes=[bias_b])

    for l in range(dbg.get('nl', NL)):
        load_mla_weights(l)
        prompt_memory_kv(l)
        if stop == 'mem':
            break
        for ti in range(dbg.get('nt', NT)):
            t0 = ti * T
            if l == 0:
                for sub in range(2):
                    i = rot2["xtok"] % 2
                    rot2["xtok"] += 1
                    P.dma(SP, xtok[i], xp[t0 + sub * 128:t0 + (sub + 1) * 128, :], [], [xtok_b[i]], xtok_b[i])
                    for half in range(2):
                        gb, gbb = bank("gen")
                        for c in range(4):
                            tr(gb[:, c * 128:(c + 1) * 128], xtok[i][:, (half * 4 + c) * 128:(half * 4 + c + 1) * 128],
                               ident_f[:], [xtok_b[i], Bc], [gbb])
                        for c in range(4):
                            cp(DVE if half == 0 else ACT, xT[:, half * 4 + c, sub * 128:(sub + 1) * 128],
                               gb[:, c * 128:(c + 1) * 128], [gbb], [xT_b[half * 4 + c]])
            else:
                P.dma(SP, xT[:, :, :], xscr.rearrange("(c p) t -> p c t", p=128)[:, :, t0:t0 + T], [xscr_b[ti]], xT_b, xT_b[0])
            W = phase_A(l, xT, xT_b, T, t0)
            phase_rows(l, W, [(sub * 128, 128, ti * 2 + sub, (ti == 0 and sub == 0), t0 + sub * 128, t0 + sub * 128, sub == 0)
                              for sub in range(2)], (fk_p[l], fv_p[l], lf_p[l], ckv_p[l], kr_p[l]))
            if stop == 'A':
                continue
            tiles = [(kt, 128) for kt in range(ti * 2 + 2)]
            diag = {ti * 2: 0, ti * 2 + 1: 128}
            compute_bias(ti * 2 + 2)
            for pr in range(4):
                fox_attention(pr, 0, T, tiles, diag)
            for pr in range(2):
                mla_attention(pr, 0, T, tiles, diag, l)
            if stop == 'attn':
                dump()
                continue
            proj_residual("wout", l, oT, oT_b, T, xT, xT_b)
            if stop == 'wout':
                dump()
                continue
            rmsnorm_T(xT, xT_b, 8, T, l * G_L + G_CROSS, hT, hT_b, D)
            proj_xq(l, T)
            cross_attention(0, T)
            proj_residual("wxo", l, oT, oT_b, T, xT, xT_b)
            if stop == 'cross':
                dump()
                continue
            mlp(l, T, xT, xT_b)
            if stop == 'mlp':
                dump()
            if l == 0:
                for c in range(8):
                    P.dma(STQ, xscr[c * 128:(c + 1) * 128, t0:t0 + T], xT[:, c, :], [xT_b[c]], [xscr_b[ti]], xscr_b[ti])
            else:
                final_out(xT, xT_b, T, y_p, t0)
        if not dbg.get('sample', True):
            continue
        if l == 0:
            i = rot2["xtok"] % 2
            rot2["xtok"] += 1
            P.dma(SP, xtok[i], xs, [], [xtok_b[i]], xtok_b[i])
            for half in range(2):
                gb, gbb = bank("gen")
                for c in range(4):
                    tr(gb[:, c * 128:(c + 1) * 128], xtok[i][:, (half * 4 + c) * 128:(half * 4 + c + 1) * 128],
                       ident_f[:], [xtok_b[i], Bc], [gbb])
                for c in range(4):
                    cp(DVE, xTs[:, half * 4 + c, :], gb[:, c * 128:(c + 1) * 128], [gbb], [xTs_b[half * 4 + c]])
        sstop = dbg.get('sstop', 'end')
        W = phase_A(l, xTs, xTs_b, 128, S)
        if sstop == 'A':
            continue
        for b in range(dbg.get('nsb', NSB)):
            for kt in range(16):
                i = rot2["k"] % 2
                rot2["k"] += 1
                P.dma(SP, ktok[i][:], cfk[l, b, kt * 128:(kt + 1) * 128, :], [], [ktok_b[i]], ktok_b[i])
                ingest_k(kt, 128, ktok[i][:], ktok_b[i])
                i = rot2["v"] % 2
                rot2["v"] += 1
                P.dma(SP, vtok[i][:], cfv[l, b, kt * 128:(kt + 1) * 128, :], [], [vtok_b[i]], vtok_b[i])
                ingest_v(kt, 128, vtok[i][:], vtok_b[i])
                i = rot2["sm"] % 2
                rot2["sm"] += 1
                sm, smb = smtok[i], smtok_b[i]
                P.dma(SP, sm[:, 0:8], clf[l, b, kt * 128:(kt + 1) * 128, :], [], [smb], smb)
                P.dma(SP, sm[:, 8:136], cckv[l, b, kt * 128:(kt + 1) * 128, :], [], [smb], smb)
                P.dma(SP, sm[:, 136:200], ckr[l, b, kt * 128:(kt + 1) * 128, :], [], [smb], smb)
                ingest_logf(kt, 128, sm[:, 0:8], smb, kt == 0)
                ingest_ckv_kr(kt, 128, sm[:, 8:136], sm[:, 136:200], smb)
            cp(DVE, cblk[:], cumtot[:], [cumtot_b], [cblk_b])
            if sstop == 'ingest':
                continue
            phase_rows(l, W, [(b * SQ, SQ, 16, False, b * SQ, S, False)],
                       (fk_s[l], fv_s[l], lf_s[l], ckv_s[l], kr_s[l]))
            if sstop == 'rows':
                continue
            compute_bias(17)
            tiles = [(kt, 128) for kt in range(16)] + [(16, SQ)]
            for pr in range(4):
                fox_attention(pr, b * SQ, SQ, tiles, {16: 0})
            for pr in range(2):
                mla_attention(pr, b * SQ, SQ, tiles, {}, l)
        if sstop == 'attn':
            continue
        proj_residual("wout", l, oT, oT_b, 128, xTs, xTs_b)
        rmsnorm_T(xTs, xTs_b, 8, 128, l * G_L + G_CROSS, hT, hT_b, D)
        proj_xq(l, 128)
        if sstop == 'wout':
            continue
        if sstop == 'xq':
            continue
        for b in range(dbg.get('ncb', NSB)):
            sample_memory_kv(l, b)
            if sstop == 'memkv':
                continue
            cross_attention(b * SQ, SQ)
        if sstop == 'cross':
            continue
        proj_residual("wxo", l, oT, oT_b, 128, xTs, xTs_b)
        mlp(l, 128, xTs, xTs_b)
        if l == NL - 1:
            final_out(xTs, xTs_b, 128, y_s, 0)

    P.emit(final_wait_bufs=out_bufs)
    return nc


_NC_CACHE = {}
SINGLE_LAUNCH = True


def _consts():
    half = 32
    inv = (10000.0 ** (-np.arange(half, dtype=np.float32) / half)).astype(np.float32)
    pos_p = np.arange(S, dtype=np.float32)
    pos_s = (PAST + np.arange(SQ)).astype(np.float32)
    ang_p = (pos_p[:, None] * inv[None, :]).astype(np.float32)
    ang_s = (pos_s[:, None] * inv[None, :]).astype(np.float32)
    ang_T = np.concatenate([ang_p] + [ang_s] * NSB, axis=0)
    cos = np.cos(ang_T.astype(np.float64)).astype(np.float32)
    sin = np.sin(ang_T.astype(np.float64)).astype(np.float32)
    cosT = np.ascontiguousarray(np.tile(cos.T, (4, 1)))
    sgn = np.where((np.arange(128) % 64) < 32, -1.0, 1.0).astype(np.float32)[:, None]
    sinT = np.ascontiguousarray(np.tile(sin.T, (4, 1)) * sgn)
    ang_tok = np.concatenate([ang_p, ang_s], axis=0)
    c = np.cos(ang_tok.astype(np.float64)).astype(np.float32)
    s_ = np.sin(ang_tok.astype(np.float64)).astype(np.float32)
    cstok = np.ascontiguousarray(np.concatenate([c, c], axis=1))
    sntok = np.ascontiguousarray(np.concatenate([-s_, s_], axis=1))
    return cosT, sinT, cstok, sntok


def kernel(x_prompt, x_sample, mem_prompt, cache_fox_k, cache_fox_v, cache_fox_logf,
           cache_mla_ckv, cache_mla_krope, cache_mem_k, cache_mem_v,
           norm_mix, w_in, b_forget, mla_q_norm, w_uq, mla_kv_norm, w_ukv, w_out,
           norm_cross, norm_mem, w_xq, w_mk, w_mv, w_xo, norm_mlp, w_up, w_down, norm_final):
    f = lambda a: np.ascontiguousarray(np.asarray(a, dtype=np.float32))
    (x_prompt, x_sample, mem_prompt, cache_fox_k, cache_fox_v, cache_fox_logf, cache_mla_ckv, cache_mla_krope,
     cache_mem_k, cache_mem_v, norm_mix, w_in, b_forget, mla_q_norm, w_uq, mla_kv_norm, w_ukv, w_out, norm_cross,
     norm_mem, w_xq, w_mk, w_mv, w_xo, norm_mlp, w_up, w_down, norm_final) = map(f, (
        x_prompt, x_sample, mem_prompt, cache_fox_k, cache_fox_v, cache_fox_logf, cache_mla_ckv, cache_mla_krope,
        cache_mem_k, cache_mem_v, norm_mix, w_in, b_forget, mla_q_norm, w_uq, mla_kv_norm, w_ukv, w_out, norm_cross,
        norm_mem, w_xq, w_mk, w_mv, w_xo, norm_mlp, w_up, w_down, norm_final))
    if "nc" not in _NC_CACHE:
        _NC_CACHE["nc"] = build()
    nc = _NC_CACHE["nc"]
    n = 8
    gains = np.zeros((128, GC), np.float32)

    def fm(v):
        return v.reshape(-1, 128).T

    for l in range(NL):
        o = l * G_L
        gains[:, o + G_MIX:o + G_MIX + 8] = fm(norm_mix[l])
        gains[:, o + G_CROSS:o + G_CROSS + 8] = fm(norm_cross[l])
        gains[:, o + G_MEM:o + G_MEM + 8] = fm(norm_mem[l])
        gains[:, o + G_MLP:o + G_MLP + 8] = fm(norm_mlp[l])
        gains[:, o + G_QN:o + G_QN + 2] = fm(mla_q_norm[l])
    gains[:, G_FINAL:G_FINAL + 8] = fm(norm_final)
    rowsb = np.zeros((128, RBC), np.float32)
    for l in range(NL):
        rowsb[:, RB_BF + l * 8:RB_BF + l * 8 + 8] = b_forget[l][None, :]
        rowsb[:, RB_KVN + l * 128:RB_KVN + (l + 1) * 128] = mla_kv_norm[l][None, :]
    w_uq_r = np.zeros((NL, 256, 1024), np.float32)
    w_ukT = np.zeros((NL, 512, 128), np.float32)
    w_uv = np.zeros((NL, 128, 512), np.float32)
    for h in range(4):
        w_uq_r[:, :, h * 128:(h + 1) * 128] = w_uq[:, :, h * 192:h * 192 + 128]
        w_uq_r[:, :, 512 + h * 64:512 + (h + 1) * 64] = w_uq[:, :, h * 192 + 128:h * 192 + 192]
        w_uq_r[:, :, 768 + h * 64:768 + h * 64 + 32] = w_uq[:, :, h * 192 + 160:h * 192 + 192]
        w_uq_r[:, :, 768 + h * 64 + 32:768 + (h + 1) * 64] = w_uq[:, :, h * 192 + 128:h * 192 + 160]
        w_ukT[:, h * 128:(h + 1) * 128, :] = np.transpose(w_ukv[:, :, h * 256:h * 256 + 128], (0, 2, 1))
        w_uv[:, :, h * 128:(h + 1) * 128] = w_ukv[:, :, h * 256 + 128:h * 256 + 256]
    cosT, sinT, cstok, sntok = _consts()
    shared = {
        "w_in": w_in, "w_uq_r": w_uq_r, "w_ukT": w_ukT, "w_uv": w_uv, "w_out": w_out, "w_xq": w_xq, "w_xo": w_xo,
        "w_mk": w_mk, "w_mv": w_mv, "w_up": w_up, "w_down": w_down, "gains": gains, "rowsb": rowsb,
        "cosT": cosT, "sinT": sinT, "cstok": cstok, "sntok": sntok,
    }
    in_maps = []
    for c in range(n):
        sl = slice(c * NSB, (c + 1) * NSB)
        m = dict(shared)
        m["xp"] = x_prompt[c]
        m["xs"] = np.ascontiguousarray(x_sample[sl].reshape(NSB * SQ, D))
        m["mem"] = mem_prompt[c]
        m["cfk"] = np.ascontiguousarray(cache_fox_k[:, sl].reshape(NL, NSB, PAST, 512))
        m["cfv"] = np.ascontiguousarray(cache_fox_v[:, sl].reshape(NL, NSB, PAST, 512))
        m["clf"] = np.ascontiguousarray(cache_fox_logf[:, sl])
        m["cckv"] = np.ascontiguousarray(cache_mla_ckv[:, sl])
        m["ckr"] = np.ascontiguousarray(cache_mla_krope[:, sl])
        m["cmk"] = np.ascontiguousarray(cache_mem_k[:, sl].reshape(NL, NSB, 256, D))
        m["cmv"] = np.ascontiguousarray(cache_mem_v[:, sl].reshape(NL, NSB, 256, D))
        in_maps.append(m)
    if SINGLE_LAUNCH:
        res = run_bass_kernel_spmd(nc, in_maps, core_ids=list(range(n)))
        R = res.results
    else:
        R = []
        for c in range(n):
            r = run_bass_kernel_spmd(nc, [in_maps[c]], core_ids=[0])
            R.append(r.results[0])

    def cat_p(key, shp):
        return np.stack([np.asarray(R[c][key], dtype=np.float32).reshape((NL, S) + shp) for c in range(n)], axis=1)

    def cat_s(key, shp):
        return np.concatenate([np.asarray(R[c][key], dtype=np.float32).reshape((NL, NSB, SQ) + shp) for c in range(n)], axis=1)

    y_prompt = np.stack([np.asarray(R[c]["y_p"], dtype=np.float32) for c in range(n)], axis=0)
    y_sample = np.concatenate([np.asarray(R[c]["y_s"], dtype=np.float32).reshape(NSB, SQ, D) for c in range(n)], axis=0)
    mk = np.stack([np.asarray(R[c]["mk_p"], dtype=np.float32).reshape(NL, 256, 4, 256) for c in range(n)], axis=1)
    mv = np.stack([np.asarray(R[c]["mv_p"], dtype=np.float32).reshape(NL, 256, 4, 256) for c in range(n)], axis=1)
    return (y_prompt, y_sample,
            cat_p("fk_p", (8, 64)), cat_p("fv_p", (8, 64)), cat_p("lf_p", (8,)), cat_p("ckv_p", (128,)), cat_p("kr_p", (64,)),
            mk, mv,
            cat_s("fk_s", (8, 64)), cat_s("fv_s", (8, 64)), cat_s("lf_s", (8,)), cat_s("ckv_s", (128,)), cat_s("kr_s", (64,)))
```

```python
import contextlib
import numpy as np
import concourse.bass as bass
import concourse.mybir as mybir
from concourse.bass_utils import run_bass_kernel_spmd

F32 = mybir.dt.float32
BF16 = mybir.dt.bfloat16
ALU = mybir.AluOpType
AF = mybir.ActivationFunctionType
AX = mybir.AxisListType

PE, ACT, DVE, POOL, SP = "tensor", "scalar", "vector", "gpsimd", "sync"
COMPUTE = (PE, ACT, DVE, POOL)


class Trk:
    __slots__ = ("name", "sem", "sem_cnt")

    def __init__(self, name):
        self.name = name
        self.sem = None
        self.sem_cnt = 0


class Buf:
    __slots__ = ("name", "w_eng", "w_dma", "r_eng", "r_dma", "trk")

    def __init__(self, name):
        self.name = name
        self.w_eng = {}
        self.w_dma = []
        self.r_eng = {}
        self.r_dma = []
        self.trk = {}


class Op:
    __slots__ = ("eng", "fn", "deps_eng", "deps_dma", "need_sig", "sig", "idx", "is_dma", "dma_buf", "dma_val", "tag")

    def __init__(self, eng, fn):
        self.eng = eng
        self.fn = fn
        self.deps_eng = {}
        self.deps_dma = {}
        self.need_sig = False
        self.sig = None
        self.is_dma = False
        self.dma_buf = None
        self.dma_val = 0


class Prog:
    def __init__(self, nc):
        self.nc = nc
        self.ops = {e: [] for e in (PE, ACT, DVE, POOL, SP)}
        self.nbuf = 0
        self.dma_bufs = []
        self.tag = ''

    def buf(self, name=None):
        self.nbuf += 1
        return Buf(f"{name or 'b'}{self.nbuf}")

    def _dep(self, op, other):
        if other is op:
            return
        if other.is_dma:
            b = other.dma_buf
            if op.deps_dma.get(b, 0) < other.dma_val:
                op.deps_dma[b] = other.dma_val
        else:
            if other.eng == PE and op.eng == PE and not op.is_dma:
                return
            cur = op.deps_eng.get(other.eng)
            if cur is None or cur.idx < other.idx:
                op.deps_eng[other.eng] = other

    def add(self, eng, fn, reads=(), writes=(), dma=None, after=()):
        op = Op(eng, fn)
        op.tag = self.tag
        op.idx = len(self.ops[eng])
        if dma is not None:
            op.is_dma = True
            t = dma.trk.get(eng)
            if t is None:
                t = Trk(f"{dma.name}_{eng[:2]}")
                dma.trk[eng] = t
                self.dma_bufs.append(t)
            op.dma_buf = t
            t.sem_cnt += 1
            op.dma_val = 16 * t.sem_cnt
        for o in after:
            self._dep(op, o)
        for b in reads:
            for w in b.w_eng.values():
                self._dep(op, w)
            for w in b.w_dma:
                self._dep(op, w)
        for b in writes:
            if b.r_eng or b.r_dma:
                for r in b.r_eng.values():
                    self._dep(op, r)
                for r in b.r_dma:
                    self._dep(op, r)
                b.w_eng = {}
                b.w_dma = []
                b.r_eng = {}
                b.r_dma = []
            else:
                for w in b.w_eng.values():
                    if w.eng != eng or op.is_dma:
                        self._dep(op, w)
                if not op.is_dma:
                    for w in b.w_dma:
                        self._dep(op, w)
        for b in reads:
            if op.is_dma:
                b.r_dma.append(op)
            else:
                b.r_eng[eng] = op
        for b in writes:
            if op.is_dma:
                b.w_dma.append(op)
            else:
                b.w_eng[eng] = op
        for o in op.deps_eng.values():
            o.need_sig = True
        self.ops[eng].append(op)
        return op

    def dma(self, q, out, in_, reads, writes, track, after=()):
        return self.add(q, lambda e: e.dma_start(out=out, in_=in_), reads=reads, writes=writes, dma=track, after=after)

    def emit(self, final_wait_bufs=()):
        nc = self.nc
        with contextlib.ExitStack() as st:
            esem = {e: st.enter_context(nc.semaphore(f"s_{e}")) for e in COMPUTE}
            for b in self.dma_bufs:
                b.sem = st.enter_context(nc.semaphore(f"d_{b.name}"))
            for e in COMPUTE:
                k = 0
                for op in self.ops[e]:
                    if op.need_sig:
                        k += 1
                        op.sig = k
            block = st.enter_context(nc.Block())
            prog = self

            def run(e, eng):
                waited = {}
                for op in prog.ops[e]:
                    for pe_, o in op.deps_eng.items():
                        s = esem[pe_]
                        if waited.get(s.num, 0) < o.sig:
                            eng.wait_ge(s, o.sig)
                            waited[s.num] = o.sig
                    for b, v in op.deps_dma.items():
                        if waited.get(b.sem.num, 0) < v:
                            eng.wait_ge(b.sem, v)
                            waited[b.sem.num] = v
                    ins = op.fn(eng)
                    if op.is_dma:
                        ins.then_inc(op.dma_buf.sem, 16)
                    elif op.need_sig:
                        ins.then_inc(esem[e], 1)
                if e == SP:
                    for b in final_wait_bufs:
                        for t in b.trk.values():
                            eng.wait_ge(t.sem, 16 * t.sem_cnt)

            @block.tensor
            def _(eng):
                run(PE, eng)

            @block.scalar
            def _(eng):
                run(ACT, eng)

            @block.vector
            def _(eng):
                run(DVE, eng)

            @block.gpsimd
            def _(eng):
                run(POOL, eng)

            @block.sync
            def _(eng):
                run(SP, eng)


D = 1024
S = 4096
NL = 2
PAST = 2048
NSB = 4
SQ = 32
T = 256
NT = S // T
DFF = 4096
EPS = 1e-6
IN_COLS = 1992
NEG = -30000.0
G_MIX, G_CROSS, G_MEM, G_MLP, G_QN = 0, 8, 16, 24, 32
G_L = 34
G_FINAL = 68
GC = 76
RB_BF = 0
RB_KVN = 16
RBC = 16 + 256


def build(dbg=None):
    dbg = dbg or {}
    stop = dbg.get('stop', 'end')
    nc = bass.Bass("TRN2", target_bir_lowering=False)
    P = Prog(nc)

    out_bufs = []

    def din(name, shape, dt=F32):
        return nc.dram_tensor(name, list(shape), dt, kind="ExternalInput").ap()

    def dout(name, shape):
        return nc.dram_tensor(name, list(shape), F32, kind="ExternalOutput").ap()

    def dscr(name, shape, dt):
        return nc.dram_tensor(name, list(shape), dt, kind="Internal").ap()

    def sb(name, shape, dt):
        return nc.alloc_sbuf_tensor("sb_" + name, list(shape), dt)

    xp = din("xp", [S, D]); xs = din("xs", [128, D]); mem = din("mem", [256, D])
    cfk = din("cfk", [NL, NSB, PAST, 512]); cfv = din("cfv", [NL, NSB, PAST, 512])
    clf = din("clf", [NL, NSB, PAST, 8]); cckv = din("cckv", [NL, NSB, PAST, 128])
    ckr = din("ckr", [NL, NSB, PAST, 64])
    cmk = din("cmk", [NL, NSB, 256, D]); cmv = din("cmv", [NL, NSB, 256, D])
    wsrc = {
        "win": din("w_in", [NL, D, IN_COLS]), "wuq": din("w_uq_r", [NL, 256, 1024]),
        "wukT": din("w_ukT", [NL, 512, 128]), "wuv": din("w_uv", [NL, 128, 512]),
        "wout": din("w_out", [NL, D, D]), "wxq": din("w_xq", [NL, D, D]), "wxo": din("w_xo", [NL, D, D]),
        "wmk": din("w_mk", [NL, D, D]), "wmv": din("w_mv", [NL, D, D]),
        "wup": din("w_up", [NL, D, DFF]), "wdown": din("w_down", [NL, DFF, D]),
    }
    gains_d = din("gains", [128, GC]); rowsb_d = din("rowsb", [128, RBC])
    cosT_d = din("cosT", [128, S + 128]); sinT_d = din("sinT", [128, S + 128])
    cstok_d = din("cstok", [S + SQ, 64]); sntok_d = din("sntok", [S + SQ, 64])

    y_p = dout("y_p", [S, D]); y_s = dout("y_s", [128, D])
    fk_p = dout("fk_p", [NL, S, 512]); fv_p = dout("fv_p", [NL, S, 512]); lf_p = dout("lf_p", [NL, S, 8])
    ckv_p = dout("ckv_p", [NL, S, 128]); kr_p = dout("kr_p", [NL, S, 64])
    mk_p = dout("mk_p", [NL, 256, D]); mv_p = dout("mv_p", [NL, 256, D])
    fk_s = dout("fk_s", [NL, 128, 512]); fv_s = dout("fv_s", [NL, 128, 512]); lf_s = dout("lf_s", [NL, 128, 8])
    ckv_s = dout("ckv_s", [NL, 128, 128]); kr_s = dout("kr_s", [NL, 128, 64])

    if dbg.get("dump"):
        dbgx = dout("dbgx", [D, T])
        dbgo = nc.dram_tensor("dbgo", [D, T], BF16, kind="ExternalOutput").ap()
    dump_b = P.buf("dump")

    def dump():
        if not dbg.get("dump"):
            return
        for c in range(8):
            P.dma(POOL, dbgx[c * 128:(c + 1) * 128, :], xT[:, c, :], [xT_b[c]], [], dump_b)
            P.dma(POOL, dbgo[c * 128:(c + 1) * 128, :], oT[:, c, :], [oT_b], [], dump_b)
        if dump_b not in out_bufs:
            out_bufs.append(dump_b)

    wscr = {k: dscr("s_" + k, list(v.shape), BF16) for k, v in wsrc.items()}
    _grp = {"wmk": 0, "wmv": 0, "win": 0, "wuq": 1, "wukT": 1, "wuv": 1, "wout": 1, "wxq": 1, "wxo": 1, "wup": 2, "wdown": 2}
    _gb = {(g, l): P.buf(f"ws{g}_{l}") for g in range(3) for l in range(NL)}
    wscr_b = {(k, l): _gb[(_grp[k] if l == 0 else 0, l)] for k in wsrc for l in range(NL)}
    xscr = nc.dram_tensor("xscr", [D, S], F32, kind=dbg.get("xscr_kind", "ExternalOutput")).ap()
    _xscr_one = P.buf("xscr")
    xscr_b = [_xscr_one for _ in range(NT)]

    ident_bf = sb("ident_bf", [128, 128], BF16); ident_f = sb("ident_f", [128, 128], F32)
    ones_bf = sb("ones_bf", [128, 128], BF16); ones_f = sb("ones_f", [128, 128], F32)
    tri_f = sb("tri_f", [128, 128], F32)
    mask_fox = sb("mask_fox", [128, 128], BF16)
    mask_mla = sb("mask_mla", [128, 128], BF16)
    epsc = sb("epsc", [128, 1], F32); onec = sb("onec", [128, 1], F32)
    gains = sb("gains", [128, GC], F32); rowsb = sb("rowsb", [128, RBC], F32)
    Bc = P.buf("consts")

    def cinit(eng, fn):
        P.add(eng, fn, writes=[Bc])

    cinit(POOL, lambda e: e.memset(ident_bf[:], 0.0))
    P.add(POOL, lambda e: e.affine_select(out=ident_bf[:], in_=ident_bf[:], pattern=[[-1, 128]],
                                          compare_op=ALU.not_equal, fill=1.0, base=0, channel_multiplier=1),
          reads=[Bc], writes=[Bc])
    cinit(POOL, lambda e: e.memset(ident_f[:], 0.0))
    P.add(POOL, lambda e: e.affine_select(out=ident_f[:], in_=ident_f[:], pattern=[[-1, 128]],
                                          compare_op=ALU.not_equal, fill=1.0, base=0, channel_multiplier=1),
          reads=[Bc], writes=[Bc])
    cinit(POOL, lambda e: e.memset(ones_bf[:], 1.0))
    cinit(POOL, lambda e: e.memset(ones_f[:], 1.0))
    cinit(POOL, lambda e: e.memset(tri_f[:], 1.0))
    P.add(POOL, lambda e: e.affine_select(out=tri_f[:], in_=tri_f[:], pattern=[[1, 128]],
                                          compare_op=ALU.is_ge, fill=0.0, base=0, channel_multiplier=-1),
          reads=[Bc], writes=[Bc])
    cinit(POOL, lambda e: e.memset(mask_fox[:], 0.0))
    P.add(POOL, lambda e: e.affine_select(out=mask_fox[:], in_=mask_fox[:], pattern=[[1, 128]],
                                          compare_op=ALU.is_ge, fill=NEG, base=0, channel_multiplier=-1),
          reads=[Bc], writes=[Bc])
    cinit(POOL, lambda e: e.memset(mask_mla[:], 0.0))
    P.add(POOL, lambda e: e.memset(mask_mla[64:128, 0:64], NEG), reads=[Bc], writes=[Bc])
    cinit(POOL, lambda e: e.memset(epsc[:], EPS))
    cinit(POOL, lambda e: e.memset(onec[:], 1.0))
    P.dma(SP, gains[:], gains_d, [], [Bc], Bc)
    P.dma(SP, rowsb[:], rowsb_d, [], [Bc], Bc)

    ps = [nc.alloc_psum_tensor(f"ps{i}", [128, 512], F32) for i in range(7)]
    psT = nc.alloc_psum_tensor("psT", [128, 1024], BF16)
    psb = [P.buf(f"psb{i}") for i in range(7)]
    psTb = P.buf("psTb")
    SHORT = [0, 1, 2, 3, 4]
    AC = [5, 6]
    rot = {"short": 0, "ac": 0}
    mlp_mode = [False]

    def bank(kind):
        if kind == "ac":
            i = AC[rot["ac"] % 2]
            rot["ac"] += 1
        else:
            lst = [0, 1, 2] if mlp_mode[0] else SHORT
            i = lst[rot["short"] % len(lst)]
            rot["short"] += 1
        return ps[i], psb[i]

    NKT = S // 128
    KT = sb("KT", [128, 4, S], BF16)
    VA = sb("VA", [128, NKT, 512], BF16)
    CKVT = sb("CKVT", [128, S], BF16)
    CKV = sb("CKV", [128, NKT, 128], BF16)
    KRT = sb("KRT", [128, S], BF16)
    CUM = sb("CUM", [128, NKT, 8], F32)
    cumtot = sb("cumtot", [128, 8], F32)
    cblk = sb("cblk", [128, 8], F32)
    biasall = sb("biasall", [128, NKT, 8], F32)
    store_b = [P.buf("store") for _ in range(NKT)]
    cum_b = [P.buf("cum") for _ in range(NKT)]
    cumtot_b = P.buf("cumtot"); cblk_b = P.buf("cblk"); bias_b = P.buf("bias")

    xy = sb("xy", [128, 2 * D], F32)
    xtok = [xy[:, i * D:(i + 1) * D] for i in range(2)]
    xtok_b = [P.buf("xtok") for _ in range(2)]
    xT = sb("xT", [128, 8, T], F32); xT_b = [P.buf("xT") for _ in range(8)]
    xTs = sb("xTs", [128, 8, 128], F32); xTs_b = [P.buf("xTs") for _ in range(8)]
    hT = sb("hT", [128, 8, T], BF16); hT_b = [P.buf("hT") for _ in range(8)]
    sq = [sb(f"sq{i}", [128, T], BF16) for i in range(2)]; sq_b = [P.buf("sq") for _ in range(2)]
    rstd = sb("rstd", [128, T], F32); nrm_b = P.buf("nrm")
    lnv = rstd
    QT = sb("QT", [128, 4, T], BF16); QT_b = P.buf("QT")
    cqf = sb("cqf", [128, 2, T], F32); cqf_b = P.buf("cqf")
    cqT = sb("cqT", [128, 2, T], BF16); cqT_b = P.buf("cqT")
    qnT = sb("qnT", [128, 4, T], BF16); qnT_b = P.buf("qnT")
    qaT = sb("qaT", [128, 4, T], BF16); qaT_b = P.buf("qaT")
    qrf = sb("qrf", [128, T], F32); qrsf = sb("qrsf", [128, T], F32); qrf_b = P.buf("qrf")
    qrT = sb("qrT", [128, 2, T], BF16); qrT_b = P.buf("qrT")
    oT = sb("oT", [128, 8, T], BF16); oT_b = P.buf("oT")
    olat = [sb(f"olat{i}", [128, T], BF16) for i in range(2)]; olat_b = [P.buf("olat") for _ in range(2)]
    rden = [sb(f"rden{i}", [128, T], F32) for i in range(2)]; rden_b = [P.buf("rden") for _ in range(2)]
    NPB = 3
    Pt = [sb(f"Pt{i}", [128, 512], BF16) for i in range(NPB)]; Pt_b = [P.buf("Pt") for _ in range(NPB)]
    prot = [0]
    cosT = sb("cosT", [128, T], F32); sinT = sb("sinT", [128, T], F32); rope_b = P.buf("rope")
    cstok = sb("cstok", [128, 64], F32); sntok = sb("sntok", [128, 64], F32); ropetok_b = P.buf("ropetok")
    ktok = [sb(f"ktok{i}", [128, 512], F32) for i in range(2)]; ktok_b = [P.buf("ktok") for _ in range(2)]
    vtok = [sb(f"vtok{i}", [128, 512], F32) for i in range(2)]; vtok_b = [P.buf("vtok") for _ in range(2)]
    smtok = [sb(f"smtok{i}", [128, 200], F32) for i in range(2)]; smtok_b = [P.buf("smtok") for _ in range(2)]
    kbf = sb("kbf", [128, 512], BF16); kbf_b = P.buf("kbf")
    ckbf = sb("ckbf", [128, 256], BF16); ckbf_b = P.buf("ckbf")
    tmpa = sb("tmpa", [128, 136], F32); tmpb = sb("tmpb", [128, 136], F32); tmp_b = P.buf("tmp")
    stat = sb("stat", [128, 4], F32); stat_b = P.buf("stat")
    rr = sb("rr", [128, T], F32); rr_b = P.buf("rr")
    actT = [sb(f"actT{i}", [128, T], BF16) for i in range(3)]; actT_b = [P.buf("actT") for _ in range(3)]
    NWB = 3
    wbuf = [sb(f"wbuf{i}", [128, 4096], BF16) for i in range(NWB)]; wbuf_b = [P.buf("wbuf") for _ in range(NWB)]
    wrot = [0]
    wuq_sb = sb("wuq_sb", [128, 2, 1024], BF16); wukT_sb = sb("wukT_sb", [128, 4, 128], BF16)
    wuv_sb = sb("wuv_sb", [128, 512], BF16); wmla_b = P.buf("wmla")
    MKT = sb("MKT", [128, 8, 256], BF16); MV = sb("MV", [128, 2, D], BF16); memkv_b = P.buf("memkv")
    mtok = sb("mtok", [128, D], F32); mtok_b = P.buf("mtok")
    ytok = sb("ytok", [128, D], F32); ytok_b = P.buf("ytok")
    yT = xy[:, :].rearrange("p (c t) -> p c t", c=8)
    rot2 = {"k": 0, "v": 0, "sm": 0, "act": 0, "xtok": 0, "sq": 0}

    def mm(out, lhsT, rhs, start, stop, reads, writes):
        P.add(PE, lambda e: e.matmul(out, lhsT=lhsT, rhs=rhs, start=start, stop=stop, skip_group_check=True),
              reads=reads, writes=writes)

    def tr(out, in_, ident, reads, writes):
        P.add(PE, lambda e: e.transpose(out=out, in_=in_, identity=ident), reads=reads, writes=writes)

    def tt(eng, out, in0, in1, op, reads, writes):
        P.add(eng, lambda e: e.tensor_tensor(out=out, in0=in0, in1=in1, op=op), reads=reads, writes=writes)

    def ts(eng, out, in_, scalar, op, reads, writes):
        P.add(eng, lambda e: e.tensor_single_scalar(out=out, in_=in_, scalar=scalar, op=op), reads=reads, writes=writes)

    def stt(eng, out, in0, scalar, in1, op0, op1, reads, writes):
        P.add(eng, lambda e: e.scalar_tensor_tensor(out=out, in0=in0, scalar=scalar, in1=in1, op0=op0, op1=op1),
              reads=reads, writes=writes)

    def cp(eng, out, in_, reads, writes):
        if eng == ACT:
            P.add(ACT, lambda e: e.copy(out=out, in_=in_), reads=reads, writes=writes)
        else:
            P.add(eng, lambda e: e.tensor_copy(out=out, in_=in_), reads=reads, writes=writes)

    def act(out, in_, func, reads, writes, bias=None, scale=1.0):
        if bias is None:
            P.add(ACT, lambda e: e.activation(out=out, in_=in_, func=func, scale=scale), reads=reads, writes=writes)
        else:
            P.add(ACT, lambda e: e.activation(out=out, in_=in_, func=func, bias=bias, scale=scale),
                  reads=reads, writes=writes)

    STQ = {"sp": SP, "pool": POOL, "act": ACT}[dbg.get("stq", "pool")]

    def store(dst, src, src_bufs, track):
        P.dma(STQ, dst, src, src_bufs, [], track)
        if track not in out_bufs:
            out_bufs.append(track)

    def load_w(key, l, src3):
        i = wrot[0] % NWB
        wrot[0] += 1
        a, b = src3.shape[1], src3.shape[2]
        view = wbuf[i][:, 0:a * b].rearrange("p (a b) -> p a b", a=a)
        P.dma(SP, view, src3, [wscr_b[(key, l)]], [wbuf_b[i]], wbuf_b[i])
        return view, wbuf_b[i]

    def wrows(key, l):
        return wscr[key][l].rearrange("(c p) n -> p c n", p=128)

    order = ["wmk", "wmv", "win", "wuq", "wukT", "wuv", "wout", "wxq", "wxo", "wup", "wdown"]
    cast_ops = []
    CAST_DEPTH = dbg.get("cast_depth", 1000)
    def emit_casts(l):
        for k in order:
            rows = wsrc[k].shape[1]
            nsplit = 4 if rows >= 1024 else 1
            step = rows // nsplit
            for j in range(nsplit):
                op_ = P.dma(POOL, wscr[k][l, j * step:(j + 1) * step, :], wsrc[k][l, j * step:(j + 1) * step, :],
                            [], [wscr_b[(k, l)]], wscr_b[(k, l)], after=cast_ops[-CAST_DEPTH:-CAST_DEPTH + 1] if len(cast_ops) >= CAST_DEPTH else ())
                cast_ops.append(op_)

    emit_casts(0)

    def rmsnorm_T(src, src_bufs, nch, Tn, gcol, dst, dst_buf, Dn):
        pb, pbb = bank("gen")
        for c in range(nch):
            i = rot2["sq"] % 2
            rot2["sq"] += 1
            sbc = src_bufs[c] if isinstance(src_bufs[c], list) else [src_bufs[c]]
            tt(POOL, sq[i][:, :Tn], src[:, c, :Tn], src[:, c, :Tn], ALU.mult, sbc, [sq_b[i]])
            mm(pb[:, :Tn], ones_bf[:], sq[i][:, :Tn], c == 0, c == nch - 1, [sq_b[i], Bc], [pbb])
        act(lnv[:, :Tn], pb[:, :Tn], AF.Ln, [pbb, Bc], [nrm_b], bias=epsc[:, 0:1], scale=1.0 / Dn)
        act(rstd[:, :Tn], lnv[:, :Tn], AF.Exp, [nrm_b], [nrm_b], scale=-0.5)
        for c in range(nch):
            sbc = src_bufs[c] if isinstance(src_bufs[c], list) else [src_bufs[c]]
            stt(DVE, dst[:, c, :Tn], src[:, c, :Tn], gains[:, gcol + c:gcol + c + 1], rstd[:, :Tn],
                ALU.mult, ALU.mult, sbc + [nrm_b, Bc], [dst_buf[c] if isinstance(dst_buf, list) else dst_buf])

    def ingest_k(kt, R, src, src_buf):
        cp(POOL, kbf[:R, :], src, [src_buf], [kbf_b])
        for p in range(4):
            tr(psT[:, p * 128:p * 128 + R], kbf[:R, p * 128:(p + 1) * 128], ident_bf[:R, :R], [kbf_b, Bc], [psTb])
        P.add(DVE, lambda e: e.tensor_copy(
            out=KT[:, :, kt * 128:kt * 128 + R],
            in_=psT[:, 0:512].rearrange("p (a b) -> p a b", a=4)[:, :, 0:R]), reads=[psTb], writes=[store_b[kt]])

    def ingest_v(kt, R, src, src_buf):
        cp(POOL, VA[:R, kt, :], src, [src_buf], [store_b[kt]])

    def ingest_ckv_kr(kt, R, ckv_src, kr_src, src_buf):
        cp(POOL, ckbf[:R, 0:128], ckv_src, [src_buf], [ckbf_b])
        cp(POOL, ckbf[:R, 128:192], kr_src, [src_buf], [ckbf_b])
        cp(POOL, ckbf[:R, 192:256], kr_src, [src_buf], [ckbf_b])
        cp(POOL, CKV[:R, kt, :], ckbf[:R, 0:128], [ckbf_b], [store_b[kt]])
        tr(psT[:, 512:512 + R], ckbf[:R, 0:128], ident_bf[:R, :R], [ckbf_b, Bc], [psTb])
        tr(psT[:, 640:640 + R], ckbf[:R, 128:256], ident_bf[:R, :R], [ckbf_b, Bc], [psTb])
        cp(DVE, CKVT[:, kt * 128:kt * 128 + R], psT[:, 512:512 + R], [psTb], [store_b[kt]])
        cp(DVE, KRT[:, kt * 128:kt * 128 + R], psT[:, 640:640 + R], [psTb], [store_b[kt]])

    def ingest_logf(kt, R, src, src_buf, first):
        if first:
            P.add(POOL, lambda e: e.memset(cumtot[:], 0.0), writes=[cumtot_b])
        pm, pmb = bank("gen")
        mm(pm[:R, 0:8], tri_f[:R, :R], src, True, True, [src_buf, Bc], [pmb])
        mm(pm[:, 8:16], ones_f[:R, :], src, False, True, [src_buf, Bc], [pmb])
        tt(DVE, CUM[:R, kt, :], pm[:R, 0:8], cumtot[:R, :], ALU.add, [pmb, cumtot_b], [cum_b[kt]])
        tt(DVE, cumtot[:], pm[:, 8:16], cumtot[:], ALU.add, [pmb, cumtot_b], [cumtot_b])

    def next_P():
        i = prot[0] % NPB
        prot[0] += 1
        return Pt[i], Pt_b[i]

    def fox_attention(pr, q0, Tq, tiles, diag):
        accs = [bank("ac"), bank("ac")]
        n = len(tiles)
        sc = {}

        def qk(i):
            kt, R = tiles[i]
            c0 = diag.get(kt, 0)
            bs = [bank("sc"), bank("sc")]
            for hh in range(2):
                r0 = hh * 64
                sb_, sbb = bs[hh]
                mm(sb_[:R, c0:Tq], KT[r0:r0 + 64, pr, kt * 128:kt * 128 + R],
                   QT[r0:r0 + 64, pr, q0 + c0:q0 + Tq], True, kt not in diag, [store_b[kt], QT_b], [sbb])
            if kt in diag:
                w = min(128, Tq - c0)
                for hh in range(2):
                    sb_, sbb = bs[hh]
                    mm(sb_[:R, c0:c0 + w], ident_bf[:R, :R], mask_fox[:R, 0:w], False, True, [Bc], [sbb])
            sc[i] = (bs, c0)

        def pv(i):
            kt, R = tiles[i]
            bs, c0 = sc.pop(i)
            pt, ptb = next_P()
            for hh in range(2):
                h = pr * 2 + hh
                sb_, sbb = bs[hh]
                act(pt[:R, hh * 256 + c0:hh * 256 + Tq], sb_[:R, c0:Tq], AF.Exp,
                    [sbb, bias_b], [ptb], bias=biasall[:R, kt, h:h + 1], scale=1.0)
            for hh in range(2):
                h = pr * 2 + hh
                ab, abb = accs[hh]
                mm(ab[0:64, c0:Tq], VA[:R, kt, h * 64:(h + 1) * 64],
                   pt[:R, hh * 256 + c0:hh * 256 + Tq], i == 0, False, [store_b[kt], ptb], [abb])
                mm(ab[0:64, 256 + c0:256 + Tq], ones_bf[:R, 0:64],
                   pt[:R, hh * 256 + c0:hh * 256 + Tq], False, i == n - 1, [Bc, ptb], [abb])

        qk(0)
        for i in range(n):
            if i + 1 < n:
                qk(i + 1)
            pv(i)
        for hh in range(2):
            ab, abb = accs[hh]
            P.add(DVE, lambda e, ab=ab, hh=hh: e.reciprocal(out=rden[hh][0:64, :Tq], in_=ab[0:64, 256:256 + Tq]),
                  reads=[abb], writes=[rden_b[hh]])
            tt(DVE, oT[hh * 64:hh * 64 + 64, pr, q0:q0 + Tq], ab[0:64, 0:Tq], rden[hh][0:64, :Tq], ALU.mult,
               [abb, rden_b[hh]], [oT_b])

    def mla_attention(pr, q0, Tq, tiles, diag, l):
        accs = [bank("ac"), bank("ac")]
        n = len(tiles)
        sc = {}

        def qk(i):
            kt, R = tiles[i]
            c0 = diag.get(kt, 0)
            bs = [bank("sc"), bank("sc")]
            for hh in range(2):
                h = pr * 2 + hh
                sb_, sbb = bs[hh]
                mm(sb_[:R, c0:Tq], CKVT[:, kt * 128:kt * 128 + R], qaT[:, h, q0 + c0:q0 + Tq],
                   True, False, [store_b[kt], qaT_b], [sbb])
            for hh in range(2):
                r0 = hh * 64
                sb_, sbb = bs[hh]
                mm(sb_[:R, c0:Tq], KRT[r0:r0 + 64, kt * 128:kt * 128 + R],
                   qrT[r0:r0 + 64, pr, q0 + c0:q0 + Tq], False, kt not in diag, [store_b[kt], qrT_b], [sbb])
            if kt in diag:
                w = min(128, Tq - c0)
                for hh in range(2):
                    sb_, sbb = bs[hh]
                    mm(sb_[:R, c0:c0 + w], ident_bf[:R, :R], mask_mla[:R, 0:w], False, True, [Bc], [sbb])
            sc[i] = (bs, c0)

        def pv(i):
            kt, R = tiles[i]
            bs, c0 = sc.pop(i)
            pt, ptb = next_P()
            for hh in range(2):
                sb_, sbb = bs[hh]
                act(pt[:R, hh * 256 + c0:hh * 256 + Tq], sb_[:R, c0:Tq], AF.Exp, [sbb], [ptb])
            for hh in range(2):
                ab, abb = accs[hh]
                mm(ab[:, c0:Tq], CKV[:R, kt, :], pt[:R, hh * 256 + c0:hh * 256 + Tq], i == 0, False,
                   [store_b[kt], ptb], [abb])
                mm(ab[:, 256 + c0:256 + Tq], ones_bf[:R, :], pt[:R, hh * 256 + c0:hh * 256 + Tq], False, i == n - 1,
                   [Bc, ptb], [abb])

        qk(0)
        for i in range(n):
            if i + 1 < n:
                qk(i + 1)
            pv(i)
        for hh in range(2):
            h = pr * 2 + hh
            ab, abb = accs[hh]
            P.add(DVE, lambda e, ab=ab, hh=hh: e.reciprocal(out=rden[hh][:, :Tq], in_=ab[:, 256:256 + Tq]),
                  reads=[abb], writes=[rden_b[hh]])
            tt(DVE, olat[hh][:, :Tq], ab[:, 0:Tq], rden[hh][:, :Tq], ALU.mult, [abb, rden_b[hh]], [olat_b[hh]])
            gb, gbb = bank("gen")
            mm(gb[:, :Tq], wuv_sb[:, h * 128:(h + 1) * 128], olat[hh][:, :Tq], True, True, [wmla_b, olat_b[hh]], [gbb])
            cp(ACT, oT[:, 4 + h, q0:q0 + Tq], gb[:, :Tq], [gbb], [oT_b])

    def cross_attention(q0, Tq):
        for h in range(4):
            sb_, sbb = bank("sc")
            for j in range(2):
                for c in range(2):
                    ch = h * 2 + c
                    src = (qnT if ch < 4 else qaT)[:, ch % 4, q0:q0 + Tq]
                    mm(sb_[:, j * 256:j * 256 + Tq], MKT[:, ch, j * 128:(j + 1) * 128], src,
                       j == 0 and c == 0, c == 1, [memkv_b, qnT_b, qaT_b], [sbb])
            pt, ptb = next_P()
            if Tq == 256:
                act(pt[:, :], sb_[:, :], AF.Exp, [sbb], [ptb])
            else:
                for j in range(2):
                    act(pt[:, j * 256:j * 256 + Tq], sb_[:, j * 256:j * 256 + Tq], AF.Exp, [sbb], [ptb])
            ab, abb = bank("ac")
            db, dbb = bank("ac")
            for c in range(2):
                for j in range(2):
                    mm(ab[:, c * 256:c * 256 + Tq], MV[:, j, h * 256 + c * 128:h * 256 + (c + 1) * 128],
                       pt[:, j * 256:j * 256 + Tq], c == 0 and j == 0, j == 1, [memkv_b, ptb], [abb])
            for j in range(2):
                mm(db[:, 0:Tq], ones_bf[:], pt[:, j * 256:j * 256 + Tq], j == 0, j == 1, [Bc, ptb], [dbb])
            P.add(DVE, lambda e, db=db: e.reciprocal(out=rden[0][:, :Tq], in_=db[:, 0:Tq]), reads=[dbb], writes=[rden_b[0]])
            for c in range(2):
                tt(DVE, oT[:, h * 2 + c, q0:q0 + Tq], ab[:, c * 256:c * 256 + Tq], rden[0][:, :Tq], ALU.mult,
                   [abb, rden_b[0]], [oT_b])

    def ingest_mem(j, which):
        if which == "v":
            cp(POOL, MV[:, j, :], mtok[:], [mtok_b], [memkv_b])
            return
        cp(POOL, kbf[:, :], mtok[:, 0:512], [mtok_b], [kbf_b])
        for p in range(4):
            tr(psT[:, p * 128:(p + 1) * 128], kbf[:, p * 128:(p + 1) * 128], ident_bf[:], [kbf_b, Bc], [psTb])
        P.add(DVE, lambda e: e.tensor_copy(out=MKT[:, 0:4, j * 128:(j + 1) * 128],
                                           in_=psT[:, 0:512].rearrange("p (a b) -> p a b", a=4)),
              reads=[psTb], writes=[memkv_b])
        cp(POOL, kbf[:, :], mtok[:, 512:1024], [mtok_b], [kbf_b])
        for p in range(4):
            tr(psT[:, p * 128:(p + 1) * 128], kbf[:, p * 128:(p + 1) * 128], ident_bf[:], [kbf_b, Bc], [psTb])
        P.add(DVE, lambda e: e.tensor_copy(out=MKT[:, 4:8, j * 128:(j + 1) * 128],
                                           in_=psT[:, 0:512].rearrange("p (a b) -> p a b", a=4)),
              reads=[psTb], writes=[memkv_b])

    def prompt_memory_kv(l):
        for j in range(2):
            P.dma(SP, mtok[:], mem[j * 128:(j + 1) * 128, :], [], [mtok_b], mtok_b)
            for half in range(2):
                gb, gbb = bank("gen")
                for c in range(4):
                    tr(gb[:, c * 128:(c + 1) * 128], mtok[:, (half * 4 + c) * 128:(half * 4 + c + 1) * 128], ident_f[:],
                       [mtok_b, Bc], [gbb])
                for c in range(4):
                    cp(DVE, yT[:, half * 4 + c, j * 128:(j + 1) * 128], gb[:, c * 128:(c + 1) * 128], [gbb], xtok_b)
        rmsnorm_T(yT, [xtok_b] * 8, 8, 256, l * G_L + G_MEM, hT, hT_b, D)
        for which, key, dst in (("k", "wmk", mk_p), ("v", "wmv", mv_p)):
            for half in range(2):
                wv, wvb = load_w(key, l, wrows(key, l)[:, :, half * 512:(half + 1) * 512])
                for j in range(2):
                    gb, gbb = bank("gen")
                    for kc in range(8):
                        mm(gb[:, :], hT[:, kc, j * 128:(j + 1) * 128], wv[:, kc, :], kc == 0, kc == 7, [hT_b[kc], wvb], [gbb])
                    i = rot2["k"] % 2
                    rot2["k"] += 1
                    cp(DVE, ktok[i][:], gb[:, :], [gbb], [ktok_b[i]])
                    store(dst[l, j * 128:(j + 1) * 128, half * 512:(half + 1) * 512], ktok[i][:], [ktok_b[i]], ktok_b[i])
                    if which == "v":
                        cp(POOL, MV[:, j, half * 512:(half + 1) * 512], ktok[i][:], [ktok_b[i]], [memkv_b])
                    else:
                        cp(POOL, kbf[:, :], ktok[i][:], [ktok_b[i]], [kbf_b])
                        for p in range(4):
                            tr(psT[:, p * 128:(p + 1) * 128], kbf[:, p * 128:(p + 1) * 128], ident_bf[:], [kbf_b, Bc], [psTb])
                        P.add(DVE, lambda e, half=half, j=j: e.tensor_copy(
                            out=MKT[:, half * 4:half * 4 + 4, j * 128:(j + 1) * 128],
                            in_=psT[:, 0:512].rearrange("p (a b) -> p a b", a=4)), reads=[psTb], writes=[memkv_b])

    def sample_memory_kv(l, b):
        for j in range(2):
            P.dma(SP, mtok[:], cmk[l, b, j * 128:(j + 1) * 128, :], [], [mtok_b], mtok_b)
            ingest_mem(j, "k")
            P.dma(SP, mtok[:], cmv[l, b, j * 128:(j + 1) * 128, :], [], [mtok_b], mtok_b)
            ingest_mem(j, "v")

    def load_mla_weights(l):
        P.dma(SP, wuq_sb[:], wscr["wuq"][l].rearrange("(c p) n -> p c n", p=128), [wscr_b[("wuq", l)]], [wmla_b], wmla_b)
        P.dma(SP, wukT_sb[:], wscr["wukT"][l].rearrange("(c p) n -> p c n", p=128), [wscr_b[("wukT", l)]], [wmla_b], wmla_b)
        P.dma(SP, wuv_sb[:], wscr["wuv"][l], [wscr_b[("wuv", l)]], [wmla_b], wmla_b)

    def phase_A(l, X, X_b, Tn, tcol):
        rmsnorm_T(X, X_b, 8, Tn, l * G_L + G_MIX, hT, hT_b, D)
        P.dma(SP, cosT[:, :Tn], cosT_d[:, tcol:tcol + Tn], [], [rope_b], rope_b)
        P.dma(SP, sinT[:, :Tn], sinT_d[:, tcol:tcol + Tn], [], [rope_b], rope_b)
        win3 = wrows("win", l)
        wq, wqb = load_w("win", l, win3[:, :, 0:512])
        for p in range(4):
            gb, gbb = bank("gen")
            for kc in range(8):
                mm(gb[:, :Tn], wq[:, kc, p * 128:(p + 1) * 128], hT[:, kc, :Tn], kc == 0, kc == 7, [wqb, hT_b[kc]], [gbb])
            ts(DVE, QT[:, p, :Tn], gb[:, :Tn], 0.125, ALU.mult, [gbb], [QT_b])
        wr, wrb = load_w("win", l, win3[:, :, 1536:1992])
        for c in range(2):
            gb, gbb = bank("gen")
            for kc in range(8):
                mm(gb[:, :Tn], wr[:, kc, 8 + c * 128:8 + (c + 1) * 128], hT[:, kc, :Tn], kc == 0, kc == 7, [wrb, hT_b[kc]], [gbb])
            cp(ACT, cqf[:, c, :Tn], gb[:, :Tn], [gbb], [cqf_b])
        rmsnorm_T(cqf, [cqf_b] * 2, 2, Tn, l * G_L + G_QN, cqT, cqT_b, 256)
        qs = 192.0 ** -0.5
        for h in range(4):
            gb, gbb = bank("gen")
            for kc in range(2):
                mm(gb[:, :Tn], wuq_sb[:, kc, h * 128:(h + 1) * 128], cqT[:, kc, :Tn], kc == 0, kc == 1, [wmla_b, cqT_b], [gbb])
            cp(ACT, qnT[:, h, :Tn], gb[:, :Tn], [gbb], [qnT_b])
        for h in range(4):
            gb, gbb = bank("gen")
            mm(gb[:, :Tn], wukT_sb[:, h, :], qnT[:, h, :Tn], True, True, [wmla_b, qnT_b], [gbb])
            ts(DVE, qaT[:, h, :Tn], gb[:, :Tn], qs, ALU.mult, [gbb], [qaT_b])
        for p in range(2):
            for sw, dstt in ((0, qrf), (1, qrsf)):
                gb, gbb = bank("gen")
                c0 = 512 + sw * 256 + p * 128
                for kc in range(2):
                    mm(gb[:, :Tn], wuq_sb[:, kc, c0:c0 + 128], cqT[:, kc, :Tn], kc == 0, kc == 1, [wmla_b, cqT_b], [gbb])
                stt(DVE, dstt[:, :Tn], gb[:, :Tn], qs, (cosT if sw == 0 else sinT)[:, :Tn], ALU.mult, ALU.mult,
                    [gbb, rope_b], [qrf_b])
            tt(DVE, qrT[:, p, :Tn], qrf[:, :Tn], qrsf[:, :Tn], ALU.add, [qrf_b], [qrT_b])
        return None

    def phase_rows(l, W, groups, outs):
        fk, fv, lf, ckvo, kro = outs
        win3 = wrows("win", l)
        wk, wkb = load_w("win", l, win3[:, :, 512:1024])
        for (r0, R, kt, first, row0, trow, snap) in groups:
            gb, gbb = bank("gen")
            for kc in range(8):
                mm(gb[:R, :], hT[:, kc, r0:r0 + R], wk[:, kc, :], kc == 0, kc == 7, [hT_b[kc], wkb], [gbb])
            i = rot2["k"] % 2
            rot2["k"] += 1
            cp(ACT, ktok[i][:R, :], gb[:R, :], [gbb], [ktok_b[i]])
            store(fk[row0:row0 + R, :], ktok[i][:R, :], [ktok_b[i]], ktok_b[i])
            ingest_k(kt, R, ktok[i][:R, :], ktok_b[i])
        wv_, wvb = load_w("win", l, win3[:, :, 1024:1536])
        for (r0, R, kt, first, row0, trow, snap) in groups:
            gb, gbb = bank("gen")
            for kc in range(8):
                mm(gb[:R, :], hT[:, kc, r0:r0 + R], wv_[:, kc, :], kc == 0, kc == 7, [hT_b[kc], wvb], [gbb])
            i = rot2["v"] % 2
            rot2["v"] += 1
            cp(ACT, vtok[i][:R, :], gb[:R, :], [gbb], [vtok_b[i]])
            store(fv[row0:row0 + R, :], vtok[i][:R, :], [vtok_b[i]], vtok_b[i])
            ingest_v(kt, R, vtok[i][:R, :], vtok_b[i])
        wr, wrb = load_w("win", l, win3[:, :, 1536:1992])
        for (r0, R, kt, first, row0, trow, snap) in groups:
            gb, gbb = bank("gen")
            for kc in range(8):
                mm(gb[:R, 0:456], hT[:, kc, r0:r0 + R], wr[:, kc, :], kc == 0, kc == 7, [hT_b[kc], wrb], [gbb])
            i = rot2["sm"] % 2
            rot2["sm"] += 1
            sm, smb = smtok[i], smtok_b[i]
            tt(DVE, tmpa[:R, 0:8], gb[:R, 0:8], rowsb[:R, RB_BF + l * 8:RB_BF + l * 8 + 8], ALU.add, [gbb, Bc], [tmp_b])
            act(tmpa[:R, 0:8], tmpa[:R, 0:8], AF.Exp, [tmp_b], [tmp_b], scale=-1.0)
            act(tmpa[:R, 0:8], tmpa[:R, 0:8], AF.Ln, [tmp_b, Bc], [tmp_b], bias=onec[:R, 0:1], scale=1.0)
            ts(DVE, sm[:R, 0:8], tmpa[:R, 0:8], -1.0, ALU.mult, [tmp_b], [smb])
            cp(DVE, tmpa[:R, 8:136], gb[:R, 264:392], [gbb], [tmp_b])
            tt(DVE, tmpb[:R, 8:136], tmpa[:R, 8:136], tmpa[:R, 8:136], ALU.mult, [tmp_b], [tmp_b])
            P.add(DVE, lambda e, R=R: e.reduce_sum(out=stat[:R, 0:1], in_=tmpb[:R, 8:136], axis=AX.X),
                  reads=[tmp_b], writes=[stat_b])
            act(stat[:R, 1:2], stat[:R, 0:1], AF.Ln, [stat_b, Bc], [stat_b], bias=epsc[:R, 0:1], scale=1.0 / 128)
            act(stat[:R, 2:3], stat[:R, 1:2], AF.Exp, [stat_b], [stat_b], scale=-0.5)
            stt(DVE, sm[:R, 8:136], tmpa[:R, 8:136], stat[:R, 2:3], rowsb[:R, RB_KVN + l * 128:RB_KVN + (l + 1) * 128],
                ALU.mult, ALU.mult, [tmp_b, stat_b, Bc], [smb])
            P.dma(SP, cstok[:R, :], cstok_d[trow:trow + R, :], [], [ropetok_b], ropetok_b)
            P.dma(SP, sntok[:R, :], sntok_d[trow:trow + R, :], [], [ropetok_b], ropetok_b)
            tt(DVE, sm[:R, 136:200], gb[:R, 392:456], cstok[:R, :], ALU.mult, [gbb, ropetok_b], [smb])
            tt(DVE, tmpb[:R, 72:104], gb[:R, 424:456], sntok[:R, 0:32], ALU.mult, [gbb, ropetok_b], [tmp_b])
            tt(DVE, tmpb[:R, 104:136], gb[:R, 392:424], sntok[:R, 32:64], ALU.mult, [gbb, ropetok_b], [tmp_b])
            tt(DVE, sm[:R, 136:200], sm[:R, 136:200], tmpb[:R, 72:136], ALU.add, [smb, tmp_b], [smb])
            store(lf[row0:row0 + R, :], sm[:R, 0:8], [smb], smb)
            store(ckvo[row0:row0 + R, :], sm[:R, 8:136], [smb], smb)
            store(kro[row0:row0 + R, :], sm[:R, 136:200], [smb], smb)
            ingest_logf(kt, R, sm[:R, 0:8], smb, first)
            if snap:
                cp(DVE, cblk[:], cumtot[:], [cumtot_b], [cblk_b])
            ingest_ckv_kr(kt, R, sm[:R, 8:136], sm[:R, 136:200], smb)

    def proj_residual(key, l, src, src_b, Tn, X, X_b):
        w3 = wrows(key, l)
        for half in range(2):
            wv, wvb = load_w(key, l, w3[:, :, half * 512:(half + 1) * 512])
            for dd in range(4):
                d = half * 4 + dd
                gb, gbb = bank("gen")
                for kc in range(8):
                    mm(gb[:, :Tn], wv[:, kc, dd * 128:(dd + 1) * 128], src[:, kc, :Tn], kc == 0, kc == 7, [wvb, src_b], [gbb])
                tt(DVE, X[:, d, :Tn], gb[:, :Tn], X[:, d, :Tn], ALU.add, [gbb, X_b[d]], [X_b[d]])

    def proj_xq(l, Tn):
        w3 = wrows("wxq", l)
        for half in range(2):
            wv, wvb = load_w("wxq", l, w3[:, :, half * 512:(half + 1) * 512])
            for dd in range(4):
                gb, gbb = bank("gen")
                for kc in range(8):
                    mm(gb[:, :Tn], wv[:, kc, dd * 128:(dd + 1) * 128], hT[:, kc, :Tn], kc == 0, kc == 7, [wvb, hT_b[kc]], [gbb])
                dst, dstb = (qnT, qnT_b) if half == 0 else (qaT, qaT_b)
                ts(DVE, dst[:, dd, :Tn], gb[:, :Tn], 1.0 / 16.0, ALU.mult, [gbb], [dstb])

    def mlp(l, Tn, X, X_b):
        rmsnorm_T(X, X_b, 8, Tn, l * G_L + G_MLP, hT, hT_b, D)
        up3 = wrows("wup", l)
        dn3 = wscr["wdown"][l].rearrange("(g j p) n -> g p j n", j=4, p=128)
        accb = [(ps[i], psb[i]) for i in (3, 4, 5, 6)]
        mlp_mode[0] = True
        pendq = []

        def down(fi, ai, wd, wdb):
            for d in range(8):
                ab, abb = accb[d // 2]
                mm(ab[:, (d % 2) * 256:(d % 2) * 256 + Tn], wd[:, fi % 4, d * 128:(d + 1) * 128], actT[ai][:, :Tn],
                   fi == 0 and d % 2 == 0, fi == 31, [wdb, actT_b[ai]], [abb])

        for g in range(8):
            wu, wub = load_w("wup", l, up3[:, :, g * 512:(g + 1) * 512])
            wd, wdb = load_w("wdown", l, dn3[g])
            for j in range(4):
                fi = g * 4 + j
                gb, gbb = bank("gen")
                for kc in range(8):
                    mm(gb[:, :Tn], wu[:, kc, j * 128:(j + 1) * 128], hT[:, kc, :Tn], kc == 0, kc == 7, [wub, hT_b[kc]], [gbb])
                if len(pendq) >= 2:
                    down(*pendq.pop(0))
                ai = rot2["act"] % 3
                rot2["act"] += 1
                ts(DVE, rr[:, :Tn], gb[:, :Tn], 0.0, ALU.max, [gbb], [rr_b])
                tt(POOL, actT[ai][:, :Tn], rr[:, :Tn], rr[:, :Tn], ALU.mult, [rr_b], [actT_b[ai]])
                pendq.append((fi, ai, wd, wdb))
        while pendq:
            down(*pendq.pop(0))
        for d in range(8):
            ab, abb = accb[d // 2]
            tt(DVE, X[:, d, :Tn], ab[:, (d % 2) * 256:(d % 2) * 256 + Tn], X[:, d, :Tn], ALU.add, [abb, X_b[d]], [X_b[d]])
        mlp_mode[0] = False

    def final_out(X, X_b, Tn, dst, row0):
        rmsnorm_f32(X, X_b, Tn)
        for sub in range(Tn // 128):
            for half in range(2):
                gb, gbb = bank("gen")
                for c in range(4):
                    tr(gb[:, c * 128:(c + 1) * 128], yT[:, half * 4 + c, sub * 128:(sub + 1) * 128], ident_f[:], xtok_b + [Bc], [gbb])
                cp(ACT, ytok[:, half * 512:(half + 1) * 512], gb[:, :], [gbb], [ytok_b])
            store(dst[row0 + sub * 128:row0 + (sub + 1) * 128, :], ytok[:], [ytok_b], ytok_b)

    def rmsnorm_f32(X, X_b, Tn):
        pb, pbb = bank("gen")
        for c in range(8):
            i = rot2["sq"] % 2
            rot2["sq"] += 1
            tt(POOL, sq[i][:, :Tn], X[:, c, :Tn], X[:, c, :Tn], ALU.mult, [X_b[c]], [sq_b[i]])
            mm(pb[:, :Tn], ones_bf[:], sq[i][:, :Tn], c == 0, c == 7, [sq_b[i], Bc], [pbb])
        act(lnv[:, :Tn], pb[:, :Tn], AF.Ln, [pbb, Bc], [nrm_b], bias=epsc[:, 0:1], scale=1.0 / D)
        act(rstd[:, :Tn], lnv[:, :Tn], AF.Exp, [nrm_b], [nrm_b], scale=-0.5)
        for c in range(8):
            stt(DVE, yT[:, c, :Tn], X[:, c, :Tn], gains[:, G_FINAL + c:G_FINAL + c + 1], rstd[:, :Tn],
                ALU.mult, ALU.mult, [X_b[c], nrm_b, Bc], xtok_b)

    def compute_bias(nkt, mid_note=None):
        P.add(DVE, lambda e: e.tensor_tensor(out=biasall[:, 0:nkt, :],
                                             in0=cblk[:].unsqueeze(1).broadcast_to([128, nkt, 8]),
                                             in1=CUM[:, 0:nkt, :], op=ALU.subtract),
              reads=[cblk_b] + cum_b[0:nkt], writes=[bias_b])

    def _wrap(f, tag):
        def g(*a, **k):
            old = P.tag
            P.tag = tag
            try:
                return f(*a, **k)
            finally:
                P.tag = old
        return g

    rmsnorm_T = _wrap(rmsnorm_T, "norm"); rmsnorm_f32 = _wrap(rmsnorm_f32, "norm")
    ingest_k = _wrap(ingest_k, "ingest"); ingest_v = _wrap(ingest_v, "ingest"); ingest_ckv_kr = _wrap(ingest_ckv_kr, "ingest")
    ingest_logf = _wrap(ingest_logf, "ingest"); fox_attention = _wrap(fox_attention, "fox"); mla_attention = _wrap(mla_attention, "mla")
    cross_attention = _wrap(cross_attention, "cross"); prompt_memory_kv = _wrap(prompt_memory_kv, "memkv")
    phase_A = _wrap(phase_A, "A"); phase_rows = _wrap(phase_rows, "rows"); proj_residual = _wrap(proj_residual, "proj")
    proj_xq = _wrap(proj_xq, "proj"); mlp = _wrap(mlp, "mlp"); final_out = _wrap(final_out, "final")

    for l in range(dbg.get('nl', NL)):
        load_mla_weights(l)
        prompt_memory_kv(l)
        if stop == 'mem':
            break
        for ti in range(dbg.get('nt', NT)):
            t0 = ti * T
            if l == 0:
                for sub in range(2):
                    i = rot2["xtok"] % 2
                    rot2["xtok"] += 1
                    P.dma(SP, xtok[i], xp[t0 + sub * 128:t0 + (sub + 1) * 128, :], [], [xtok_b[i]], xtok_b[i])
                    for half in range(2):
                        gb, gbb = bank("gen")
                        for c in range(4):
                            tr(gb[:, c * 128:(c + 1) * 128], xtok[i][:, (half * 4 + c) * 128:(half * 4 + c + 1) * 128],
                               ident_f[:], [xtok_b[i], Bc], [gbb])
                        for c in range(4):
                            cp(DVE if half == 0 else ACT, xT[:, half * 4 + c, sub * 128:(sub + 1) * 128],
                               gb[:, c * 128:(c + 1) * 128], [gbb], [xT_b[half * 4 + c]])
            else:
                P.dma(SP, xT[:, :, :], xscr.rearrange("(c p) t -> p c t", p=128)[:, :, t0:t0 + T], [xscr_b[ti]], xT_b, xT_b[0])
            W = phase_A(l, xT, xT_b, T, t0)
            phase_rows(l, W, [(sub * 128, 128, ti * 2 + sub, (ti == 0 and sub == 0), t0 + sub * 128, t0 + sub * 128, sub == 0)
                              for sub in range(2)], (fk_p[l], fv_p[l], lf_p[l], ckv_p[l], kr_p[l]))
            if stop == 'A':
                continue
            tiles = [(kt, 128) for kt in range(ti * 2 + 2)]
            diag = {ti * 2: 0, ti * 2 + 1: 128}
            compute_bias(ti * 2 + 2)
            for pr in range(4):
                fox_attention(pr, 0, T, tiles, diag)
            for pr in range(2):
                mla_attention(pr, 0, T, tiles, diag, l)
            if stop == 'attn':
                dump()
                continue
            proj_residual("wout", l, oT, oT_b, T, xT, xT_b)
            if stop == 'wout':
                dump()
                continue
            rmsnorm_T(xT, xT_b, 8, T, l * G_L + G_CROSS, hT, hT_b, D)
            proj_xq(l, T)
            cross_attention(0, T)
            proj_residual("wxo", l, oT, oT_b, T, xT, xT_b)
            if stop == 'cross':
                dump()
                continue
            mlp(l, T, xT, xT_b)
            if stop == 'mlp':
                dump()
            if l == 0 and ti == 0:
                emit_casts(1)
            if l == 0:
                P.dma(STQ, xscr.rearrange("(c p) t -> p c t", p=128)[:, :, t0:t0 + T], xT[:, :, :], xT_b, [xscr_b[ti]], xscr_b[ti])
            else:
                final_out(xT, xT_b, T, y_p, t0)
        if not dbg.get('sample', True):
            continue
        if l == 0:
            i = rot2["xtok"] % 2
            rot2["xtok"] += 1
            P.dma(SP, xtok[i], xs, [], [xtok_b[i]], xtok_b[i])
            for half in range(2):
                gb, gbb = bank("gen")
                for c in range(4):
                    tr(gb[:, c * 128:(c + 1) * 128], xtok[i][:, (half * 4 + c) * 128:(half * 4 + c + 1) * 128],
                       ident_f[:], [xtok_b[i], Bc], [gbb])
                for c in range(4):
                    cp(DVE, xTs[:, half * 4 + c, :], gb[:, c * 128:(c + 1) * 128], [gbb], [xTs_b[half * 4 + c]])
        sstop = dbg.get('sstop', 'end')
        W = phase_A(l, xTs, xTs_b, 128, S)
        if sstop == 'A':
            continue
        for b in range(dbg.get('nsb', NSB)):
            for kt in range(16):
                i = rot2["k"] % 2
                rot2["k"] += 1
                P.dma(SP, ktok[i][:], cfk[l, b, kt * 128:(kt + 1) * 128, :], [], [ktok_b[i]], ktok_b[i])
                ingest_k(kt, 128, ktok[i][:], ktok_b[i])
                i = rot2["v"] % 2
                rot2["v"] += 1
                P.dma(SP, vtok[i][:], cfv[l, b, kt * 128:(kt + 1) * 128, :], [], [vtok_b[i]], vtok_b[i])
                ingest_v(kt, 128, vtok[i][:], vtok_b[i])
                i = rot2["sm"] % 2
                rot2["sm"] += 1
                sm, smb = smtok[i], smtok_b[i]
                P.dma(SP, sm[:, 0:8], clf[l, b, kt * 128:(kt + 1) * 128, :], [], [smb], smb)
                P.dma(SP, sm[:, 8:136], cckv[l, b, kt * 128:(kt + 1) * 128, :], [], [smb], smb)
                P.dma(SP, sm[:, 136:200], ckr[l, b, kt * 128:(kt + 1) * 128, :], [], [smb], smb)
                ingest_logf(kt, 128, sm[:, 0:8], smb, kt == 0)
                ingest_ckv_kr(kt, 128, sm[:, 8:136], sm[:, 136:200], smb)
            cp(DVE, cblk[:], cumtot[:], [cumtot_b], [cblk_b])
            if sstop == 'ingest':
                continue
            phase_rows(l, W, [(b * SQ, SQ, 16, False, b * SQ, S, False)],
                       (fk_s[l], fv_s[l], lf_s[l], ckv_s[l], kr_s[l]))
            if sstop == 'rows':
                continue
            compute_bias(17)
            tiles = [(kt, 128) for kt in range(16)] + [(16, SQ)]
            for pr in range(4):
                fox_attention(pr, b * SQ, SQ, tiles, {16: 0})
            for pr in range(2):
                mla_attention(pr, b * SQ, SQ, tiles, {}, l)
        if sstop == 'attn':
            continue
        proj_residual("wout", l, oT, oT_b, 128, xTs, xTs_b)
        rmsnorm_T(xTs, xTs_b, 8, 128, l * G_L + G_CROSS, hT, hT_b, D)
        proj_xq(l, 128)
        if sstop == 'wout':
            continue
        if sstop == 'xq':
            continue
        for b in range(dbg.get('ncb', NSB)):
            sample_memory_kv(l, b)
            if sstop == 'memkv':
                continue
            cross_attention(b * SQ, SQ)
        if sstop == 'cross':
            continue
        proj_residual("wxo", l, oT, oT_b, 128, xTs, xTs_b)
        mlp(l, 128, xTs, xTs_b)
        if l == NL - 1:
            final_out(xTs, xTs_b, 128, y_s, 0)

    P.emit(final_wait_bufs=out_bufs)
    return nc


_NC_CACHE = {}
SINGLE_LAUNCH = True


def _consts():
    half = 32
    inv = (10000.0 ** (-np.arange(half, dtype=np.float32) / half)).astype(np.float32)
    pos_p = np.arange(S, dtype=np.float32)
    pos_s = (PAST + np.arange(SQ)).astype(np.float32)
    ang_p = (pos_p[:, None] * inv[None, :]).astype(np.float32)
    ang_s = (pos_s[:, None] * inv[None, :]).astype(np.float32)
    ang_T = np.concatenate([ang_p] + [ang_s] * NSB, axis=0)
    cos = np.cos(ang_T.astype(np.float64)).astype(np.float32)
    sin = np.sin(ang_T.astype(np.float64)).astype(np.float32)
    cosT = np.ascontiguousarray(np.tile(cos.T, (4, 1)))
    sgn = np.where((np.arange(128) % 64) < 32, -1.0, 1.0).astype(np.float32)[:, None]
    sinT = np.ascontiguousarray(np.tile(sin.T, (4, 1)) * sgn)
    ang_tok = np.concatenate([ang_p, ang_s], axis=0)
    c = np.cos(ang_tok.astype(np.float64)).astype(np.float32)
    s_ = np.sin(ang_tok.astype(np.float64)).astype(np.float32)
    cstok = np.ascontiguousarray(np.concatenate([c, c], axis=1))
    sntok = np.ascontiguousarray(np.concatenate([-s_, s_], axis=1))
    return cosT, sinT, cstok, sntok


def kernel(x_prompt, x_sample, mem_prompt, cache_fox_k, cache_fox_v, cache_fox_logf,
           cache_mla_ckv, cache_mla_krope, cache_mem_k, cache_mem_v,
           norm_mix, w_in, b_forget, mla_q_norm, w_uq, mla_kv_norm, w_ukv, w_out,
           norm_cross, norm_mem, w_xq, w_mk, w_mv, w_xo, norm_mlp, w_up, w_down, norm_final):
    f = lambda a: np.ascontiguousarray(np.asarray(a, dtype=np.float32))
    (x_prompt, x_sample, mem_prompt, cache_fox_k, cache_fox_v, cache_fox_logf, cache_mla_ckv, cache_mla_krope,
     cache_mem_k, cache_mem_v, norm_mix, w_in, b_forget, mla_q_norm, w_uq, mla_kv_norm, w_ukv, w_out, norm_cross,
     norm_mem, w_xq, w_mk, w_mv, w_xo, norm_mlp, w_up, w_down, norm_final) = map(f, (
        x_prompt, x_sample, mem_prompt, cache_fox_k, cache_fox_v, cache_fox_logf, cache_mla_ckv, cache_mla_krope,
        cache_mem_k, cache_mem_v, norm_mix, w_in, b_forget, mla_q_norm, w_uq, mla_kv_norm, w_ukv, w_out, norm_cross,
        norm_mem, w_xq, w_mk, w_mv, w_xo, norm_mlp, w_up, w_down, norm_final))
    if "nc" not in _NC_CACHE:
        _NC_CACHE["nc"] = build()
    nc = _NC_CACHE["nc"]
    n = 8
    gains = np.zeros((128, GC), np.float32)

    def fm(v):
        return v.reshape(-1, 128).T

    for l in range(NL):
        o = l * G_L
        gains[:, o + G_MIX:o + G_MIX + 8] = fm(norm_mix[l])
        gains[:, o + G_CROSS:o + G_CROSS + 8] = fm(norm_cross[l])
        gains[:, o + G_MEM:o + G_MEM + 8] = fm(norm_mem[l])
        gains[:, o + G_MLP:o + G_MLP + 8] = fm(norm_mlp[l])
        gains[:, o + G_QN:o + G_QN + 2] = fm(mla_q_norm[l])
    gains[:, G_FINAL:G_FINAL + 8] = fm(norm_final)
    rowsb = np.zeros((128, RBC), np.float32)
    for l in range(NL):
        rowsb[:, RB_BF + l * 8:RB_BF + l * 8 + 8] = b_forget[l][None, :]
        rowsb[:, RB_KVN + l * 128:RB_KVN + (l + 1) * 128] = mla_kv_norm[l][None, :]
    w_uq_r = np.zeros((NL, 256, 1024), np.float32)
    w_ukT = np.zeros((NL, 512, 128), np.float32)
    w_uv = np.zeros((NL, 128, 512), np.float32)
    for h in range(4):
        w_uq_r[:, :, h * 128:(h + 1) * 128] = w_uq[:, :, h * 192:h * 192 + 128]
        w_uq_r[:, :, 512 + h * 64:512 + (h + 1) * 64] = w_uq[:, :, h * 192 + 128:h * 192 + 192]
        w_uq_r[:, :, 768 + h * 64:768 + h * 64 + 32] = w_uq[:, :, h * 192 + 160:h * 192 + 192]
        w_uq_r[:, :, 768 + h * 64 + 32:768 + (h + 1) * 64] = w_uq[:, :, h * 192 + 128:h * 192 + 160]
        w_ukT[:, h * 128:(h + 1) * 128, :] = np.transpose(w_ukv[:, :, h * 256:h * 256 + 128], (0, 2, 1))
        w_uv[:, :, h * 128:(h + 1) * 128] = w_ukv[:, :, h * 256 + 128:h * 256 + 256]
    cosT, sinT, cstok, sntok = _consts()
    shared = {
        "w_in": w_in, "w_uq_r": w_uq_r, "w_ukT": w_ukT, "w_uv": w_uv, "w_out": w_out, "w_xq": w_xq, "w_xo": w_xo,
        "w_mk": w_mk, "w_mv": w_mv, "w_up": w_up, "w_down": w_down, "gains": gains, "rowsb": rowsb,
        "cosT": cosT, "sinT": sinT, "cstok": cstok, "sntok": sntok,
    }
    in_maps = []
    for c in range(n):
        sl = slice(c * NSB, (c + 1) * NSB)
        m = dict(shared)
        m["xp"] = x_prompt[c]
        m["xs"] = np.ascontiguousarray(x_sample[sl].reshape(NSB * SQ, D))
        m["mem"] = mem_prompt[c]
        m["cfk"] = np.ascontiguousarray(cache_fox_k[:, sl].reshape(NL, NSB, PAST, 512))
        m["cfv"] = np.ascontiguousarray(cache_fox_v[:, sl].reshape(NL, NSB, PAST, 512))
        m["clf"] = np.ascontiguousarray(cache_fox_logf[:, sl])
        m["cckv"] = np.ascontiguousarray(cache_mla_ckv[:, sl])
        m["ckr"] = np.ascontiguousarray(cache_mla_krope[:, sl])
        m["cmk"] = np.ascontiguousarray(cache_mem_k[:, sl].reshape(NL, NSB, 256, D))
        m["cmv"] = np.ascontiguousarray(cache_mem_v[:, sl].reshape(NL, NSB, 256, D))
        in_maps.append(m)
    if SINGLE_LAUNCH:
        res = run_bass_kernel_spmd(nc, in_maps, core_ids=list(range(n)))
        R = res.results
    else:
        R = []
        for c in range(n):
            r = run_bass_kernel_spmd(nc, [in_maps[c]], core_ids=[0])
            R.append(r.results[0])

    def cat_p(key, shp):
        return np.stack([np.asarray(R[c][key], dtype=np.float32).reshape((NL, S) + shp) for c in range(n)], axis=1)

    def cat_s(key, shp):
        return np.concatenate([np.asarray(R[c][key], dtype=np.float32).reshape((NL, NSB, SQ) + shp) for c in range(n)], axis=1)

    y_prompt = np.stack([np.asarray(R[c]["y_p"], dtype=np.float32) for c in range(n)], axis=0)
    y_sample = np.concatenate([np.asarray(R[c]["y_s"], dtype=np.float32).reshape(NSB, SQ, D) for c in range(n)], axis=0)
    mk = np.stack([np.asarray(R[c]["mk_p"], dtype=np.float32).reshape(NL, 256, 4, 256) for c in range(n)], axis=1)
    mv = np.stack([np.asarray(R[c]["mv_p"], dtype=np.float32).reshape(NL, 256, 4, 256) for c in range(n)], axis=1)
    return (y_prompt, y_sample,
            cat_p("fk_p", (8, 64)), cat_p("fv_p", (8, 64)), cat_p("lf_p", (8,)), cat_p("ckv_p", (128,)), cat_p("kr_p", (64,)),
            mk, mv,
            cat_s("fk_s", (8, 64)), cat_s("fv_s", (8, 64)), cat_s("lf_s", (8,)), cat_s("ckv_s", (128,)), cat_s("kr_s", (64,)))
```

```python
import contextlib
import numpy as np
import concourse.bass as bass
import concourse.mybir as mybir
from concourse.bass_utils import run_bass_kernel_spmd

F32 = mybir.dt.float32
BF16 = mybir.dt.bfloat16
ALU = mybir.AluOpType
AF = mybir.ActivationFunctionType
AX = mybir.AxisListType

PE, ACT, DVE, POOL, SP = "tensor", "scalar", "vector", "gpsimd", "sync"
COMPUTE = (PE, ACT, DVE, POOL)


class Trk:
    __slots__ = ("name", "sem", "sem_cnt")

    def __init__(self, name):
        self.name = name
        self.sem = None
        self.sem_cnt = 0


class Buf:
    __slots__ = ("name", "w_eng", "w_dma", "r_eng", "r_dma", "trk")

    def __init__(self, name):
        self.name = name
        self.w_eng = {}
        self.w_dma = []
        self.r_eng = {}
        self.r_dma = []
        self.trk = {}


class Op:
    __slots__ = ("eng", "fn", "deps_eng", "deps_dma", "need_sig", "sig", "idx", "is_dma", "dma_buf", "dma_val", "tag")

    def __init__(self, eng, fn):
        self.eng = eng
        self.fn = fn
        self.deps_eng = {}
        self.deps_dma = {}
        self.need_sig = False
        self.sig = None
        self.is_dma = False
        self.dma_buf = None
        self.dma_val = 0


class Prog:
    def __init__(self, nc):
        self.nc = nc
        self.ops = {e: [] for e in (PE, ACT, DVE, POOL, SP)}
        self.nbuf = 0
        self.dma_bufs = []
        self.tag = ''

    def buf(self, name=None):
        self.nbuf += 1
        return Buf(f"{name or 'b'}{self.nbuf}")

    def _dep(self, op, other):
        if other is op:
            return
        if other.is_dma:
            b = other.dma_buf
            if op.deps_dma.get(b, 0) < other.dma_val:
                op.deps_dma[b] = other.dma_val
        else:
            if other.eng == PE and op.eng == PE and not op.is_dma:
                return
            cur = op.deps_eng.get(other.eng)
            if cur is None or cur.idx < other.idx:
                op.deps_eng[other.eng] = other

    def add(self, eng, fn, reads=(), writes=(), dma=None, after=()):
        op = Op(eng, fn)
        op.tag = self.tag
        op.idx = len(self.ops[eng])
        if dma is not None:
            op.is_dma = True
            t = dma.trk.get(eng)
            if t is None:
                t = Trk(f"{dma.name}_{eng[:2]}")
                dma.trk[eng] = t
                self.dma_bufs.append(t)
            op.dma_buf = t
            t.sem_cnt += 1
            op.dma_val = 16 * t.sem_cnt
        for o in after:
            self._dep(op, o)
        for b in reads:
            for w in b.w_eng.values():
                self._dep(op, w)
            for w in b.w_dma:
                self._dep(op, w)
        for b in writes:
            if b.r_eng or b.r_dma:
                for r in b.r_eng.values():
                    self._dep(op, r)
                for r in b.r_dma:
                    self._dep(op, r)
                b.w_eng = {}
                b.w_dma = []
                b.r_eng = {}
                b.r_dma = []
            else:
                for w in b.w_eng.values():
                    if w.eng != eng or op.is_dma:
                        self._dep(op, w)
                if not op.is_dma:
                    for w in b.w_dma:
                        self._dep(op, w)
        for b in reads:
            if op.is_dma:
                b.r_dma.append(op)
            else:
                b.r_eng[eng] = op
        for b in writes:
            if op.is_dma:
                b.w_dma.append(op)
            else:
                b.w_eng[eng] = op
        for o in op.deps_eng.values():
            o.need_sig = True
        self.ops[eng].append(op)
        return op

    def dma(self, q, out, in_, reads, writes, track, after=()):
        return self.add(q, lambda e: e.dma_start(out=out, in_=in_), reads=reads, writes=writes, dma=track, after=after)

    def emit(self, final_wait_bufs=()):
        nc = self.nc
        with contextlib.ExitStack() as st:
            esem = {e: st.enter_context(nc.semaphore(f"s_{e}")) for e in COMPUTE}
            for b in self.dma_bufs:
                b.sem = st.enter_context(nc.semaphore(f"d_{b.name}"))
            for e in COMPUTE:
                k = 0
                for op in self.ops[e]:
                    if op.need_sig:
                        k += 1
                        op.sig = k
            block = st.enter_context(nc.Block())
            prog = self

            def run(e, eng):
                waited = {}
                for op in prog.ops[e]:
                    for pe_, o in op.deps_eng.items():
                        s = esem[pe_]
                        if waited.get(s.num, 0) < o.sig:
                            eng.wait_ge(s, o.sig)
                            waited[s.num] = o.sig
                    for b, v in op.deps_dma.items():
                        if waited.get(b.sem.num, 0) < v:
                            eng.wait_ge(b.sem, v)
                            waited[b.sem.num] = v
                    ins = op.fn(eng)
                    if op.is_dma:
                        ins.then_inc(op.dma_buf.sem, 16)
                    elif op.need_sig:
                        ins.then_inc(esem[e], 1)
                if e == SP:
                    for b in final_wait_bufs:
                        for t in b.trk.values():
                            eng.wait_ge(t.sem, 16 * t.sem_cnt)

            @block.tensor
            def _(eng):
                run(PE, eng)

            @block.scalar
            def _(eng):
                run(ACT, eng)

            @block.vector
            def _(eng):
                run(DVE, eng)

            @block.gpsimd
            def _(eng):
                run(POOL, eng)

            @block.sync
            def _(eng):
                run(SP, eng)


D = 1024
S = 4096
NL = 2
PAST = 2048
NSB = 4
SQ = 32
T = 256
NT = S // T
DFF = 4096
EPS = 1e-6
IN_COLS = 1992
NEG = -30000.0
G_MIX, G_CROSS, G_MEM, G_MLP, G_QN = 0, 8, 16, 24, 32
G_L = 34
G_FINAL = 68
GC = 76
RB_BF = 0
RB_KVN = 16
RBC = 16 + 256


def build(dbg=None):
    dbg = dbg or {}
    stop = dbg.get('stop', 'end')
    nc = bass.Bass("TRN2", target_bir_lowering=False)
    P = Prog(nc)

    out_bufs = []

    def din(name, shape, dt=F32):
        return nc.dram_tensor(name, list(shape), dt, kind="ExternalInput").ap()

    def dout(name, shape):
        return nc.dram_tensor(name, list(shape), F32, kind="ExternalOutput").ap()

    def dscr(name, shape, dt):
        return nc.dram_tensor(name, list(shape), dt, kind="Internal").ap()

    def sb(name, shape, dt):
        return nc.alloc_sbuf_tensor("sb_" + name, list(shape), dt)

    xp = din("xp", [S, D]); xs = din("xs", [128, D]); mem = din("mem", [256, D])
    cfk = din("cfk", [NL, NSB, PAST, 512]); cfv = din("cfv", [NL, NSB, PAST, 512])
    clf = din("clf", [NL, NSB, PAST, 8]); cckv = din("cckv", [NL, NSB, PAST, 128])
    ckr = din("ckr", [NL, NSB, PAST, 64])
    cmk = din("cmk", [NL, NSB, 256, D]); cmv = din("cmv", [NL, NSB, 256, D])
    wsrc = {
        "win": din("w_in", [NL, D, IN_COLS]), "wuq": din("w_uq_r", [NL, 256, 1024]),
        "wukT": din("w_ukT", [NL, 512, 128]), "wuv": din("w_uv", [NL, 128, 512]),
        "wout": din("w_out", [NL, D, D]), "wxq": din("w_xq", [NL, D, D]), "wxo": din("w_xo", [NL, D, D]),
        "wmk": din("w_mk", [NL, D, D]), "wmv": din("w_mv", [NL, D, D]),
        "wup": din("w_up", [NL, D, DFF]), "wdown": din("w_down", [NL, DFF, D]),
    }
    gains_d = din("gains", [128, GC]); rowsb_d = din("rowsb", [128, RBC])
    cosT_d = din("cosT", [128, S + 128]); sinT_d = din("sinT", [128, S + 128])
    cstok_d = din("cstok", [S + SQ, 64]); sntok_d = din("sntok", [S + SQ, 64])

    y_p = dout("y_p", [S, D]); y_s = dout("y_s", [128, D])
    fk_p = dout("fk_p", [NL, S, 512]); fv_p = dout("fv_p", [NL, S, 512]); lf_p = dout("lf_p", [NL, S, 8])
    ckv_p = dout("ckv_p", [NL, S, 128]); kr_p = dout("kr_p", [NL, S, 64])
    mk_p = dout("mk_p", [NL, 256, D]); mv_p = dout("mv_p", [NL, 256, D])
    fk_s = dout("fk_s", [NL, 128, 512]); fv_s = dout("fv_s", [NL, 128, 512]); lf_s = dout("lf_s", [NL, 128, 8])
    ckv_s = dout("ckv_s", [NL, 128, 128]); kr_s = dout("kr_s", [NL, 128, 64])

    if dbg.get("dump"):
        dbgx = dout("dbgx", [D, T])
        dbgo = nc.dram_tensor("dbgo", [D, T], BF16, kind="ExternalOutput").ap()
    dump_b = P.buf("dump")

    def dump():
        if not dbg.get("dump"):
            return
        for c in range(8):
            P.dma(POOL, dbgx[c * 128:(c + 1) * 128, :], xT[:, c, :], [xT_b[c]], [], dump_b)
            P.dma(POOL, dbgo[c * 128:(c + 1) * 128, :], oT[:, c, :], [oT_b], [], dump_b)
        if dump_b not in out_bufs:
            out_bufs.append(dump_b)

    wscr = {k: dscr("s_" + k, list(v.shape), BF16) for k, v in wsrc.items()}
    _grp = {"wmk": 0, "wmv": 0, "win": 0, "wuq": 1, "wukT": 1, "wuv": 1, "wout": 1, "wxq": 1, "wxo": 1, "wup": 2, "wdown": 2}
    _gb = {(g, l): P.buf(f"ws{g}_{l}") for g in range(3) for l in range(NL)}
    wscr_b = {(k, l): _gb[(_grp[k] if l == 0 else 0, l)] for k in wsrc for l in range(NL)}
    xscr = nc.dram_tensor("xscr", [D, S], F32, kind=dbg.get("xscr_kind", "ExternalOutput")).ap()
    _xscr_one = P.buf("xscr")
    xscr_b = [_xscr_one for _ in range(NT)]

    ident_bf = sb("ident_bf", [128, 128], BF16); ident_f = sb("ident_f", [128, 128], F32)
    ones_bf = sb("ones_bf", [128, 128], BF16); ones_f = sb("ones_f", [128, 128], F32)
    tri_f = sb("tri_f", [128, 128], F32)
    mask_fox = sb("mask_fox", [128, 128], BF16)
    mask_mla = sb("mask_mla", [128, 128], BF16)
    epsc = sb("epsc", [128, 1], F32); onec = sb("onec", [128, 1], F32)
    gains = sb("gains", [128, GC], F32); rowsb = sb("rowsb", [128, RBC], F32)
    Bc = P.buf("consts")

    def cinit(eng, fn):
        P.add(eng, fn, writes=[Bc])

    cinit(POOL, lambda e: e.memset(ident_bf[:], 0.0))
    P.add(POOL, lambda e: e.affine_select(out=ident_bf[:], in_=ident_bf[:], pattern=[[-1, 128]],
                                          compare_op=ALU.not_equal, fill=1.0, base=0, channel_multiplier=1),
          reads=[Bc], writes=[Bc])
    cinit(POOL, lambda e: e.memset(ident_f[:], 0.0))
    P.add(POOL, lambda e: e.affine_select(out=ident_f[:], in_=ident_f[:], pattern=[[-1, 128]],
                                          compare_op=ALU.not_equal, fill=1.0, base=0, channel_multiplier=1),
          reads=[Bc], writes=[Bc])
    cinit(POOL, lambda e: e.memset(ones_bf[:], 1.0))
    cinit(POOL, lambda e: e.memset(ones_f[:], 1.0))
    cinit(POOL, lambda e: e.memset(tri_f[:], 1.0))
    P.add(POOL, lambda e: e.affine_select(out=tri_f[:], in_=tri_f[:], pattern=[[1, 128]],
                                          compare_op=ALU.is_ge, fill=0.0, base=0, channel_multiplier=-1),
          reads=[Bc], writes=[Bc])
    cinit(POOL, lambda e: e.memset(mask_fox[:], 0.0))
    P.add(POOL, lambda e: e.affine_select(out=mask_fox[:], in_=mask_fox[:], pattern=[[1, 128]],
                                          compare_op=ALU.is_ge, fill=NEG, base=0, channel_multiplier=-1),
          reads=[Bc], writes=[Bc])
    cinit(POOL, lambda e: e.memset(mask_mla[:], 0.0))
    P.add(POOL, lambda e: e.memset(mask_mla[64:128, 0:64], NEG), reads=[Bc], writes=[Bc])
    cinit(POOL, lambda e: e.memset(epsc[:], EPS))
    cinit(POOL, lambda e: e.memset(onec[:], 1.0))
    P.dma(SP, gains[:], gains_d, [], [Bc], Bc)
    P.dma(SP, rowsb[:], rowsb_d, [], [Bc], Bc)

    ps = [nc.alloc_psum_tensor(f"ps{i}", [128, 512], F32) for i in range(7)]
    psT = nc.alloc_psum_tensor("psT", [128, 1024], BF16)
    psb = [P.buf(f"psb{i}") for i in range(7)]
    psTb = P.buf("psTb")
    SHORT = [0, 1, 2, 3, 4]
    AC = [5, 6]
    rot = {"short": 0, "ac": 0}
    mlp_mode = [False]

    def bank(kind):
        if kind == "ac":
            i = AC[rot["ac"] % 2]
            rot["ac"] += 1
        else:
            lst = [0, 1, 2] if mlp_mode[0] else SHORT
            i = lst[rot["short"] % len(lst)]
            rot["short"] += 1
        return ps[i], psb[i]

    NKT = S // 128
    KT = sb("KT", [128, 4, S], BF16)
    VA = sb("VA", [128, NKT, 512], BF16)
    CKVT = sb("CKVT", [128, S], BF16)
    CKV = sb("CKV", [128, NKT, 128], BF16)
    KRT = sb("KRT", [128, S], BF16)
    CUM = sb("CUM", [128, NKT, 8], F32)
    cumtot = sb("cumtot", [128, 8], F32)
    cblk = sb("cblk", [128, 8], F32)
    biasall = sb("biasall", [128, NKT, 8], F32)
    store_b = [P.buf("store") for _ in range(NKT)]
    cum_b = [P.buf("cum") for _ in range(NKT)]
    cumtot_b = P.buf("cumtot"); cblk_b = P.buf("cblk"); bias_b = P.buf("bias")

    xy = sb("xy", [128, 2 * D], F32)
    xtok = [xy[:, i * D:(i + 1) * D] for i in range(2)]
    xtok_b = [P.buf("xtok") for _ in range(2)]
    xT = sb("xT", [128, 8, T], F32); xT_b = [P.buf("xT") for _ in range(8)]
    xTs = sb("xTs", [128, 8, 128], F32); xTs_b = [P.buf("xTs") for _ in range(8)]
    hT = sb("hT", [128, 8, T], BF16); hT_b = [P.buf("hT") for _ in range(8)]
    sq = [sb(f"sq{i}", [128, T], BF16) for i in range(2)]; sq_b = [P.buf("sq") for _ in range(2)]
    rstd = sb("rstd", [128, T], F32); nrm_b = P.buf("nrm")
    lnv = rstd
    QT = sb("QT", [128, 4, T], BF16); QT_b = P.buf("QT")
    cqf = sb("cqf", [128, 2, T], F32); cqf_b = P.buf("cqf")
    cqT = sb("cqT", [128, 2, T], BF16); cqT_b = P.buf("cqT")
    qnT = sb("qnT", [128, 4, T], BF16); qnT_b = P.buf("qnT")
    qaT = sb("qaT", [128, 4, T], BF16); qaT_b = P.buf("qaT")
    qrf = sb("qrf", [128, T], F32); qrsf = sb("qrsf", [128, T], F32); qrf_b = P.buf("qrf")
    qrT = sb("qrT", [128, 2, T], BF16); qrT_b = P.buf("qrT")
    oT = sb("oT", [128, 8, T], BF16); oT_b = P.buf("oT")
    olat = [sb(f"olat{i}", [128, T], BF16) for i in range(2)]; olat_b = [P.buf("olat") for _ in range(2)]
    rden = [sb(f"rden{i}", [128, T], F32) for i in range(2)]; rden_b = [P.buf("rden") for _ in range(2)]
    NPB = 3
    Pt = [sb(f"Pt{i}", [128, 512], BF16) for i in range(NPB)]; Pt_b = [P.buf("Pt") for _ in range(NPB)]
    prot = [0]
    cosT = sb("cosT", [128, T], F32); sinT = sb("sinT", [128, T], F32); rope_b = P.buf("rope")
    cstok = sb("cstok", [128, 64], F32); sntok = sb("sntok", [128, 64], F32); ropetok_b = P.buf("ropetok")
    ktok = [sb(f"ktok{i}", [128, 512], F32) for i in range(2)]; ktok_b = [P.buf("ktok") for _ in range(2)]
    vtok = [sb(f"vtok{i}", [128, 512], F32) for i in range(2)]; vtok_b = [P.buf("vtok") for _ in range(2)]
    smtok = [sb(f"smtok{i}", [128, 200], F32) for i in range(2)]; smtok_b = [P.buf("smtok") for _ in range(2)]
    kbf = sb("kbf", [128, 512], BF16); kbf_b = P.buf("kbf")
    ckbf = sb("ckbf", [128, 256], BF16); ckbf_b = P.buf("ckbf")
    tmpa = sb("tmpa", [128, 136], F32); tmpb = sb("tmpb", [128, 136], F32); tmp_b = P.buf("tmp")
    stat = sb("stat", [128, 4], F32); stat_b = P.buf("stat")
    rr = sb("rr", [128, T], F32); rr_b = P.buf("rr")
    actT = [sb(f"actT{i}", [128, T], BF16) for i in range(3)]; actT_b = [P.buf("actT") for _ in range(3)]
    NWB = 3
    wbuf = [sb(f"wbuf{i}", [128, 4096], BF16) for i in range(NWB)]; wbuf_b = [P.buf("wbuf") for _ in range(NWB)]
    wrot = [0]
    wuq_sb = sb("wuq_sb", [128, 2, 1024], BF16); wukT_sb = sb("wukT_sb", [128, 4, 128], BF16)
    wuv_sb = sb("wuv_sb", [128, 512], BF16); wmla_b = P.buf("wmla")
    MKT = sb("MKT", [128, 8, 256], BF16); MV = sb("MV", [128, 2, D], BF16); memkv_b = P.buf("memkv")
    mtok = sb("mtok", [128, D], F32); mtok_b = P.buf("mtok")
    ytok = sb("ytok", [128, D], F32); ytok_b = P.buf("ytok")
    yT = xy[:, :].rearrange("p (c t) -> p c t", c=8)
    rot2 = {"k": 0, "v": 0, "sm": 0, "act": 0, "xtok": 0, "sq": 0}

    def mm(out, lhsT, rhs, start, stop, reads, writes):
        P.add(PE, lambda e: e.matmul(out, lhsT=lhsT, rhs=rhs, start=start, stop=stop, skip_group_check=True),
              reads=reads, writes=writes)

    def tr(out, in_, ident, reads, writes):
        P.add(PE, lambda e: e.transpose(out=out, in_=in_, identity=ident), reads=reads, writes=writes)

    def tt(eng, out, in0, in1, op, reads, writes):
        P.add(eng, lambda e: e.tensor_tensor(out=out, in0=in0, in1=in1, op=op), reads=reads, writes=writes)

    def ts(eng, out, in_, scalar, op, reads, writes):
        P.add(eng, lambda e: e.tensor_single_scalar(out=out, in_=in_, scalar=scalar, op=op), reads=reads, writes=writes)

    def stt(eng, out, in0, scalar, in1, op0, op1, reads, writes):
        P.add(eng, lambda e: e.scalar_tensor_tensor(out=out, in0=in0, scalar=scalar, in1=in1, op0=op0, op1=op1),
              reads=reads, writes=writes)

    def cp(eng, out, in_, reads, writes):
        if eng == ACT:
            P.add(ACT, lambda e: e.copy(out=out, in_=in_), reads=reads, writes=writes)
        else:
            P.add(eng, lambda e: e.tensor_copy(out=out, in_=in_), reads=reads, writes=writes)

    def act(out, in_, func, reads, writes, bias=None, scale=1.0):
        if bias is None:
            P.add(ACT, lambda e: e.activation(out=out, in_=in_, func=func, scale=scale), reads=reads, writes=writes)
        else:
            P.add(ACT, lambda e: e.activation(out=out, in_=in_, func=func, bias=bias, scale=scale),
                  reads=reads, writes=writes)

    STQ = {"sp": SP, "pool": POOL, "act": ACT}[dbg.get("stq", "pool")]

    def store(dst, src, src_bufs, track):
        P.dma(STQ, dst, src, src_bufs, [], track)
        if track not in out_bufs:
            out_bufs.append(track)

    def load_w(key, l, src3):
        i = wrot[0] % NWB
        wrot[0] += 1
        a, b = src3.shape[1], src3.shape[2]
        view = wbuf[i][:, 0:a * b].rearrange("p (a b) -> p a b", a=a)
        P.dma(SP, view, src3, [wscr_b[(key, l)]], [wbuf_b[i]], wbuf_b[i])
        return view, wbuf_b[i]

    def wrows(key, l):
        return wscr[key][l].rearrange("(c p) n -> p c n", p=128)

    order = ["wmk", "wmv", "win", "wuq", "wukT", "wuv", "wout", "wxq", "wxo", "wup", "wdown"]
    cast_ops = []
    CAST_DEPTH = dbg.get("cast_depth", 1000)
    def emit_casts(l):
        for k in order:
            rows = wsrc[k].shape[1]
            nsplit = 4 if rows >= 1024 else 1
            step = rows // nsplit
            for j in range(nsplit):
                op_ = P.dma(POOL, wscr[k][l, j * step:(j + 1) * step, :], wsrc[k][l, j * step:(j + 1) * step, :],
                            [], [wscr_b[(k, l)]], wscr_b[(k, l)], after=cast_ops[-CAST_DEPTH:-CAST_DEPTH + 1] if len(cast_ops) >= CAST_DEPTH else ())
                cast_ops.append(op_)

    emit_casts(0)

    def rmsnorm_T(src, src_bufs, nch, Tn, gcol, dst, dst_buf, Dn):
        pb, pbb = bank("gen")
        for c in range(nch):
            i = rot2["sq"] % 2
            rot2["sq"] += 1
            sbc = src_bufs[c] if isinstance(src_bufs[c], list) else [src_bufs[c]]
            tt(POOL if c % 2 == 0 else DVE, sq[i][:, :Tn], src[:, c, :Tn], src[:, c, :Tn], ALU.mult, sbc, [sq_b[i]])
            mm(pb[:, :Tn], ones_bf[:], sq[i][:, :Tn], c == 0, c == nch - 1, [sq_b[i], Bc], [pbb])
        act(lnv[:, :Tn], pb[:, :Tn], AF.Ln, [pbb, Bc], [nrm_b], bias=epsc[:, 0:1], scale=1.0 / Dn)
        act(rstd[:, :Tn], lnv[:, :Tn], AF.Exp, [nrm_b], [nrm_b], scale=-0.5)
        for c in range(nch):
            sbc = src_bufs[c] if isinstance(src_bufs[c], list) else [src_bufs[c]]
            stt(DVE, dst[:, c, :Tn], src[:, c, :Tn], gains[:, gcol + c:gcol + c + 1], rstd[:, :Tn],
                ALU.mult, ALU.mult, sbc + [nrm_b, Bc], [dst_buf[c] if isinstance(dst_buf, list) else dst_buf])

    def ingest_k(kt, R, src, src_buf):
        cp(POOL, kbf[:R, :], src, [src_buf], [kbf_b])
        for p in range(4):
            tr(psT[:, p * 128:p * 128 + R], kbf[:R, p * 128:(p + 1) * 128], ident_bf[:R, :R], [kbf_b, Bc], [psTb])
        P.add(DVE, lambda e: e.tensor_copy(
            out=KT[:, :, kt * 128:kt * 128 + R],
            in_=psT[:, 0:512].rearrange("p (a b) -> p a b", a=4)[:, :, 0:R]), reads=[psTb], writes=[store_b[kt]])

    def ingest_v(kt, R, src, src_buf):
        cp(POOL, VA[:R, kt, :], src, [src_buf], [store_b[kt]])

    def ingest_ckv_kr(kt, R, ckv_src, kr_src, src_buf):
        cp(POOL, ckbf[:R, 0:128], ckv_src, [src_buf], [ckbf_b])
        cp(POOL, ckbf[:R, 128:192], kr_src, [src_buf], [ckbf_b])
        cp(POOL, ckbf[:R, 192:256], kr_src, [src_buf], [ckbf_b])
        cp(POOL, CKV[:R, kt, :], ckbf[:R, 0:128], [ckbf_b], [store_b[kt]])
        tr(psT[:, 512:512 + R], ckbf[:R, 0:128], ident_bf[:R, :R], [ckbf_b, Bc], [psTb])
        tr(psT[:, 640:640 + R], ckbf[:R, 128:256], ident_bf[:R, :R], [ckbf_b, Bc], [psTb])
        cp(DVE, CKVT[:, kt * 128:kt * 128 + R], psT[:, 512:512 + R], [psTb], [store_b[kt]])
        cp(DVE, KRT[:, kt * 128:kt * 128 + R], psT[:, 640:640 + R], [psTb], [store_b[kt]])

    def ingest_logf(kt, R, src, src_buf, first):
        if first:
            P.add(POOL, lambda e: e.memset(cumtot[:], 0.0), writes=[cumtot_b])
        pm, pmb = bank("gen")
        mm(pm[:R, 0:8], tri_f[:R, :R], src, True, True, [src_buf, Bc], [pmb])
        mm(pm[:, 8:16], ones_f[:R, :], src, False, True, [src_buf, Bc], [pmb])
        tt(DVE, CUM[:R, kt, :], pm[:R, 0:8], cumtot[:R, :], ALU.add, [pmb, cumtot_b], [cum_b[kt]])
        tt(DVE, cumtot[:], pm[:, 8:16], cumtot[:], ALU.add, [pmb, cumtot_b], [cumtot_b])

    def next_P():
        i = prot[0] % NPB
        prot[0] += 1
        return Pt[i], Pt_b[i]

    def fox_attention(pr, q0, Tq, tiles, diag):
        accs = [bank("ac"), bank("ac")]
        n = len(tiles)
        sc = {}

        def qk(i):
            kt, R = tiles[i]
            c0 = diag.get(kt, 0)
            bs = [bank("sc"), bank("sc")]
            for hh in range(2):
                r0 = hh * 64
                sb_, sbb = bs[hh]
                mm(sb_[:R, c0:Tq], KT[r0:r0 + 64, pr, kt * 128:kt * 128 + R],
                   QT[r0:r0 + 64, pr, q0 + c0:q0 + Tq], True, kt not in diag, [store_b[kt], QT_b], [sbb])
            if kt in diag:
                w = min(128, Tq - c0)
                for hh in range(2):
                    sb_, sbb = bs[hh]
                    mm(sb_[:R, c0:c0 + w], ident_bf[:R, :R], mask_fox[:R, 0:w], False, True, [Bc], [sbb])
            sc[i] = (bs, c0)

        def pv(i):
            kt, R = tiles[i]
            bs, c0 = sc.pop(i)
            pt, ptb = next_P()
            for hh in range(2):
                h = pr * 2 + hh
                sb_, sbb = bs[hh]
                act(pt[:R, hh * 256 + c0:hh * 256 + Tq], sb_[:R, c0:Tq], AF.Exp,
                    [sbb, bias_b], [ptb], bias=biasall[:R, kt, h:h + 1], scale=1.0)
            for hh in range(2):
                h = pr * 2 + hh
                ab, abb = accs[hh]
                mm(ab[0:64, c0:Tq], VA[:R, kt, h * 64:(h + 1) * 64],
                   pt[:R, hh * 256 + c0:hh * 256 + Tq], i == 0, False, [store_b[kt], ptb], [abb])
                mm(ab[0:64, 256 + c0:256 + Tq], ones_bf[:R, 0:64],
                   pt[:R, hh * 256 + c0:hh * 256 + Tq], False, i == n - 1, [Bc, ptb], [abb])

        qk(0)
        for i in range(n):
            if i + 1 < n:
                qk(i + 1)
            pv(i)
        for hh in range(2):
            ab, abb = accs[hh]
            P.add(DVE, lambda e, ab=ab, hh=hh: e.reciprocal(out=rden[hh][0:64, :Tq], in_=ab[0:64, 256:256 + Tq]),
                  reads=[abb], writes=[rden_b[hh]])
            tt(DVE, oT[hh * 64:hh * 64 + 64, pr, q0:q0 + Tq], ab[0:64, 0:Tq], rden[hh][0:64, :Tq], ALU.mult,
               [abb, rden_b[hh]], [oT_b])

    def mla_attention(pr, q0, Tq, tiles, diag, l):
        accs = [bank("ac"), bank("ac")]
        n = len(tiles)
        sc = {}

        def qk(i):
            kt, R = tiles[i]
            c0 = diag.get(kt, 0)
            bs = [bank("sc"), bank("sc")]
            for hh in range(2):
                h = pr * 2 + hh
                sb_, sbb = bs[hh]
                mm(sb_[:R, c0:Tq], CKVT[:, kt * 128:kt * 128 + R], qaT[:, h, q0 + c0:q0 + Tq],
                   True, False, [store_b[kt], qaT_b], [sbb])
            for hh in range(2):
                r0 = hh * 64
                sb_, sbb = bs[hh]
                mm(sb_[:R, c0:Tq], KRT[r0:r0 + 64, kt * 128:kt * 128 + R],
                   qrT[r0:r0 + 64, pr, q0 + c0:q0 + Tq], False, kt not in diag, [store_b[kt], qrT_b], [sbb])
            if kt in diag:
                w = min(128, Tq - c0)
                for hh in range(2):
                    sb_, sbb = bs[hh]
                    mm(sb_[:R, c0:c0 + w], ident_bf[:R, :R], mask_mla[:R, 0:w], False, True, [Bc], [sbb])
            sc[i] = (bs, c0)

        def pv(i):
            kt, R = tiles[i]
            bs, c0 = sc.pop(i)
            pt, ptb = next_P()
            for hh in range(2):
                sb_, sbb = bs[hh]
                act(pt[:R, hh * 256 + c0:hh * 256 + Tq], sb_[:R, c0:Tq], AF.Exp, [sbb], [ptb])
            for hh in range(2):
                ab, abb = accs[hh]
                mm(ab[:, c0:Tq], CKV[:R, kt, :], pt[:R, hh * 256 + c0:hh * 256 + Tq], i == 0, False,
                   [store_b[kt], ptb], [abb])
                mm(ab[:, 256 + c0:256 + Tq], ones_bf[:R, :], pt[:R, hh * 256 + c0:hh * 256 + Tq], False, i == n - 1,
                   [Bc, ptb], [abb])

        qk(0)
        for i in range(n):
            if i + 1 < n:
                qk(i + 1)
            pv(i)
        for hh in range(2):
            h = pr * 2 + hh
            ab, abb = accs[hh]
            P.add(DVE, lambda e, ab=ab, hh=hh: e.reciprocal(out=rden[hh][:, :Tq], in_=ab[:, 256:256 + Tq]),
                  reads=[abb], writes=[rden_b[hh]])
            tt(DVE, olat[hh][:, :Tq], ab[:, 0:Tq], rden[hh][:, :Tq], ALU.mult, [abb, rden_b[hh]], [olat_b[hh]])
            gb, gbb = bank("gen")
            mm(gb[:, :Tq], wuv_sb[:, h * 128:(h + 1) * 128], olat[hh][:, :Tq], True, True, [wmla_b, olat_b[hh]], [gbb])
            cp(ACT, oT[:, 4 + h, q0:q0 + Tq], gb[:, :Tq], [gbb], [oT_b])

    def cross_attention(q0, Tq):
        for h in range(4):
            sb_, sbb = bank("sc")
            for j in range(2):
                for c in range(2):
                    ch = h * 2 + c
                    src = (qnT if ch < 4 else qaT)[:, ch % 4, q0:q0 + Tq]
                    mm(sb_[:, j * 256:j * 256 + Tq], MKT[:, ch, j * 128:(j + 1) * 128], src,
                       j == 0 and c == 0, c == 1, [memkv_b, qnT_b, qaT_b], [sbb])
            pt, ptb = next_P()
            if Tq == 256:
                act(pt[:, :], sb_[:, :], AF.Exp, [sbb], [ptb])
            else:
                for j in range(2):
                    act(pt[:, j * 256:j * 256 + Tq], sb_[:, j * 256:j * 256 + Tq], AF.Exp, [sbb], [ptb])
            ab, abb = bank("ac")
            db, dbb = bank("ac")
            for c in range(2):
                for j in range(2):
                    mm(ab[:, c * 256:c * 256 + Tq], MV[:, j, h * 256 + c * 128:h * 256 + (c + 1) * 128],
                       pt[:, j * 256:j * 256 + Tq], c == 0 and j == 0, j == 1, [memkv_b, ptb], [abb])
            for j in range(2):
                mm(db[:, 0:Tq], ones_bf[:], pt[:, j * 256:j * 256 + Tq], j == 0, j == 1, [Bc, ptb], [dbb])
            P.add(DVE, lambda e, db=db: e.reciprocal(out=rden[0][:, :Tq], in_=db[:, 0:Tq]), reads=[dbb], writes=[rden_b[0]])
            for c in range(2):
                tt(DVE, oT[:, h * 2 + c, q0:q0 + Tq], ab[:, c * 256:c * 256 + Tq], rden[0][:, :Tq], ALU.mult,
                   [abb, rden_b[0]], [oT_b])

    def ingest_mem(j, which):
        if which == "v":
            cp(POOL, MV[:, j, :], mtok[:], [mtok_b], [memkv_b])
            return
        cp(POOL, kbf[:, :], mtok[:, 0:512], [mtok_b], [kbf_b])
        for p in range(4):
            tr(psT[:, p * 128:(p + 1) * 128], kbf[:, p * 128:(p + 1) * 128], ident_bf[:], [kbf_b, Bc], [psTb])
        P.add(DVE, lambda e: e.tensor_copy(out=MKT[:, 0:4, j * 128:(j + 1) * 128],
                                           in_=psT[:, 0:512].rearrange("p (a b) -> p a b", a=4)),
              reads=[psTb], writes=[memkv_b])
        cp(POOL, kbf[:, :], mtok[:, 512:1024], [mtok_b], [kbf_b])
        for p in range(4):
            tr(psT[:, p * 128:(p + 1) * 128], kbf[:, p * 128:(p + 1) * 128], ident_bf[:], [kbf_b, Bc], [psTb])
        P.add(DVE, lambda e: e.tensor_copy(out=MKT[:, 4:8, j * 128:(j + 1) * 128],
                                           in_=psT[:, 0:512].rearrange("p (a b) -> p a b", a=4)),
              reads=[psTb], writes=[memkv_b])

    def prompt_memory_kv(l):
        for j in range(2):
            P.dma(SP, mtok[:], mem[j * 128:(j + 1) * 128, :], [], [mtok_b], mtok_b)
            for half in range(2):
                gb, gbb = bank("gen")
                for c in range(4):
                    tr(gb[:, c * 128:(c + 1) * 128], mtok[:, (half * 4 + c) * 128:(half * 4 + c + 1) * 128], ident_f[:],
                       [mtok_b, Bc], [gbb])
                for c in range(4):
                    cp(DVE, yT[:, half * 4 + c, j * 128:(j + 1) * 128], gb[:, c * 128:(c + 1) * 128], [gbb], xtok_b)
        rmsnorm_T(yT, [xtok_b] * 8, 8, 256, l * G_L + G_MEM, hT, hT_b, D)
        for which, key, dst in (("k", "wmk", mk_p), ("v", "wmv", mv_p)):
            for half in range(2):
                wv, wvb = load_w(key, l, wrows(key, l)[:, :, half * 512:(half + 1) * 512])
                for j in range(2):
                    gb, gbb = bank("gen")
                    for kc in range(8):
                        mm(gb[:, :], hT[:, kc, j * 128:(j + 1) * 128], wv[:, kc, :], kc == 0, kc == 7, [hT_b[kc], wvb], [gbb])
                    i = rot2["k"] % 2
                    rot2["k"] += 1
                    cp(DVE, ktok[i][:], gb[:, :], [gbb], [ktok_b[i]])
                    store(dst[l, j * 128:(j + 1) * 128, half * 512:(half + 1) * 512], ktok[i][:], [ktok_b[i]], ktok_b[i])
                    if which == "v":
                        cp(POOL, MV[:, j, half * 512:(half + 1) * 512], ktok[i][:], [ktok_b[i]], [memkv_b])
                    else:
                        cp(POOL, kbf[:, :], ktok[i][:], [ktok_b[i]], [kbf_b])
                        for p in range(4):
                            tr(psT[:, p * 128:(p + 1) * 128], kbf[:, p * 128:(p + 1) * 128], ident_bf[:], [kbf_b, Bc], [psTb])
                        P.add(DVE, lambda e, half=half, j=j: e.tensor_copy(
                            out=MKT[:, half * 4:half * 4 + 4, j * 128:(j + 1) * 128],
                            in_=psT[:, 0:512].rearrange("p (a b) -> p a b", a=4)), reads=[psTb], writes=[memkv_b])

    def sample_memory_kv(l, b):
        for j in range(2):
            P.dma(SP, mtok[:], cmk[l, b, j * 128:(j + 1) * 128, :], [], [mtok_b], mtok_b)
            ingest_mem(j, "k")
            P.dma(SP, mtok[:], cmv[l, b, j * 128:(j + 1) * 128, :], [], [mtok_b], mtok_b)
            ingest_mem(j, "v")

    def load_mla_weights(l):
        P.dma(SP, wuq_sb[:], wscr["wuq"][l].rearrange("(c p) n -> p c n", p=128), [wscr_b[("wuq", l)]], [wmla_b], wmla_b)
        P.dma(SP, wukT_sb[:], wscr["wukT"][l].rearrange("(c p) n -> p c n", p=128), [wscr_b[("wukT", l)]], [wmla_b], wmla_b)
        P.dma(SP, wuv_sb[:], wscr["wuv"][l], [wscr_b[("wuv", l)]], [wmla_b], wmla_b)

    def phase_A(l, X, X_b, Tn, tcol):
        rmsnorm_T(X, X_b, 8, Tn, l * G_L + G_MIX, hT, hT_b, D)
        P.dma(SP, cosT[:, :Tn], cosT_d[:, tcol:tcol + Tn], [], [rope_b], rope_b)
        P.dma(SP, sinT[:, :Tn], sinT_d[:, tcol:tcol + Tn], [], [rope_b], rope_b)
        win3 = wrows("win", l)
        wq, wqb = load_w("win", l, win3[:, :, 0:512])
        for p in range(4):
            gb, gbb = bank("gen")
            for kc in range(8):
                mm(gb[:, :Tn], wq[:, kc, p * 128:(p + 1) * 128], hT[:, kc, :Tn], kc == 0, kc == 7, [wqb, hT_b[kc]], [gbb])
            ts(DVE, QT[:, p, :Tn], gb[:, :Tn], 0.125, ALU.mult, [gbb], [QT_b])
        wr, wrb = load_w("win", l, win3[:, :, 1536:1992])
        for c in range(2):
            gb, gbb = bank("gen")
            for kc in range(8):
                mm(gb[:, :Tn], wr[:, kc, 8 + c * 128:8 + (c + 1) * 128], hT[:, kc, :Tn], kc == 0, kc == 7, [wrb, hT_b[kc]], [gbb])
            cp(ACT, cqf[:, c, :Tn], gb[:, :Tn], [gbb], [cqf_b])
        rmsnorm_T(cqf, [cqf_b] * 2, 2, Tn, l * G_L + G_QN, cqT, cqT_b, 256)
        qs = 192.0 ** -0.5
        for h in range(4):
            gb, gbb = bank("gen")
            for kc in range(2):
                mm(gb[:, :Tn], wuq_sb[:, kc, h * 128:(h + 1) * 128], cqT[:, kc, :Tn], kc == 0, kc == 1, [wmla_b, cqT_b], [gbb])
            cp(ACT, qnT[:, h, :Tn], gb[:, :Tn], [gbb], [qnT_b])
        for h in range(4):
            gb, gbb = bank("gen")
            mm(gb[:, :Tn], wukT_sb[:, h, :], qnT[:, h, :Tn], True, True, [wmla_b, qnT_b], [gbb])
            ts(DVE, qaT[:, h, :Tn], gb[:, :Tn], qs, ALU.mult, [gbb], [qaT_b])
        for p in range(2):
            for sw, dstt in ((0, qrf), (1, qrsf)):
                gb, gbb = bank("gen")
                c0 = 512 + sw * 256 + p * 128
                for kc in range(2):
                    mm(gb[:, :Tn], wuq_sb[:, kc, c0:c0 + 128], cqT[:, kc, :Tn], kc == 0, kc == 1, [wmla_b, cqT_b], [gbb])
                stt(DVE, dstt[:, :Tn], gb[:, :Tn], qs, (cosT if sw == 0 else sinT)[:, :Tn], ALU.mult, ALU.mult,
                    [gbb, rope_b], [qrf_b])
            tt(DVE, qrT[:, p, :Tn], qrf[:, :Tn], qrsf[:, :Tn], ALU.add, [qrf_b], [qrT_b])
        return None

    def phase_rows(l, W, groups, outs):
        fk, fv, lf, ckvo, kro = outs
        win3 = wrows("win", l)
        wk, wkb = load_w("win", l, win3[:, :, 512:1024])
        for (r0, R, kt, first, row0, trow, snap) in groups:
            gb, gbb = bank("gen")
            for kc in range(8):
                mm(gb[:R, :], hT[:, kc, r0:r0 + R], wk[:, kc, :], kc == 0, kc == 7, [hT_b[kc], wkb], [gbb])
            i = rot2["k"] % 2
            rot2["k"] += 1
            cp(ACT, ktok[i][:R, :], gb[:R, :], [gbb], [ktok_b[i]])
            store(fk[row0:row0 + R, :], ktok[i][:R, :], [ktok_b[i]], ktok_b[i])
            ingest_k(kt, R, ktok[i][:R, :], ktok_b[i])
        wv_, wvb = load_w("win", l, win3[:, :, 1024:1536])
        for (r0, R, kt, first, row0, trow, snap) in groups:
            gb, gbb = bank("gen")
            for kc in range(8):
                mm(gb[:R, :], hT[:, kc, r0:r0 + R], wv_[:, kc, :], kc == 0, kc == 7, [hT_b[kc], wvb], [gbb])
            i = rot2["v"] % 2
            rot2["v"] += 1
            cp(ACT, vtok[i][:R, :], gb[:R, :], [gbb], [vtok_b[i]])
            store(fv[row0:row0 + R, :], vtok[i][:R, :], [vtok_b[i]], vtok_b[i])
            ingest_v(kt, R, vtok[i][:R, :], vtok_b[i])
        wr, wrb = load_w("win", l, win3[:, :, 1536:1992])
        for (r0, R, kt, first, row0, trow, snap) in groups:
            gb, gbb = bank("gen")
            for kc in range(8):
                mm(gb[:R, 0:456], hT[:, kc, r0:r0 + R], wr[:, kc, :], kc == 0, kc == 7, [hT_b[kc], wrb], [gbb])
            i = rot2["sm"] % 2
            rot2["sm"] += 1
            sm, smb = smtok[i], smtok_b[i]
            tt(DVE, tmpa[:R, 0:8], gb[:R, 0:8], rowsb[:R, RB_BF + l * 8:RB_BF + l * 8 + 8], ALU.add, [gbb, Bc], [tmp_b])
            act(tmpa[:R, 0:8], tmpa[:R, 0:8], AF.Exp, [tmp_b], [tmp_b], scale=-1.0)
            act(tmpa[:R, 0:8], tmpa[:R, 0:8], AF.Ln, [tmp_b, Bc], [tmp_b], bias=onec[:R, 0:1], scale=1.0)
            ts(DVE, sm[:R, 0:8], tmpa[:R, 0:8], -1.0, ALU.mult, [tmp_b], [smb])
            cp(DVE, tmpa[:R, 8:136], gb[:R, 264:392], [gbb], [tmp_b])
            tt(DVE, tmpb[:R, 8:136], tmpa[:R, 8:136], tmpa[:R, 8:136], ALU.mult, [tmp_b], [tmp_b])
            P.add(DVE, lambda e, R=R: e.reduce_sum(out=stat[:R, 0:1], in_=tmpb[:R, 8:136], axis=AX.X),
                  reads=[tmp_b], writes=[stat_b])
            act(stat[:R, 1:2], stat[:R, 0:1], AF.Ln, [stat_b, Bc], [stat_b], bias=epsc[:R, 0:1], scale=1.0 / 128)
            act(stat[:R, 2:3], stat[:R, 1:2], AF.Exp, [stat_b], [stat_b], scale=-0.5)
            stt(DVE, sm[:R, 8:136], tmpa[:R, 8:136], stat[:R, 2:3], rowsb[:R, RB_KVN + l * 128:RB_KVN + (l + 1) * 128],
                ALU.mult, ALU.mult, [tmp_b, stat_b, Bc], [smb])
            P.dma(SP, cstok[:R, :], cstok_d[trow:trow + R, :], [], [ropetok_b], ropetok_b)
            P.dma(SP, sntok[:R, :], sntok_d[trow:trow + R, :], [], [ropetok_b], ropetok_b)
            tt(DVE, sm[:R, 136:200], gb[:R, 392:456], cstok[:R, :], ALU.mult, [gbb, ropetok_b], [smb])
            tt(DVE, tmpb[:R, 72:104], gb[:R, 424:456], sntok[:R, 0:32], ALU.mult, [gbb, ropetok_b], [tmp_b])
            tt(DVE, tmpb[:R, 104:136], gb[:R, 392:424], sntok[:R, 32:64], ALU.mult, [gbb, ropetok_b], [tmp_b])
            tt(DVE, sm[:R, 136:200], sm[:R, 136:200], tmpb[:R, 72:136], ALU.add, [smb, tmp_b], [smb])
            store(lf[row0:row0 + R, :], sm[:R, 0:8], [smb], smb)
            store(ckvo[row0:row0 + R, :], sm[:R, 8:136], [smb], smb)
            store(kro[row0:row0 + R, :], sm[:R, 136:200], [smb], smb)
            ingest_logf(kt, R, sm[:R, 0:8], smb, first)
            if snap:
                cp(DVE, cblk[:], cumtot[:], [cumtot_b], [cblk_b])
            ingest_ckv_kr(kt, R, sm[:R, 8:136], sm[:R, 136:200], smb)

    def proj_residual(key, l, src, src_b, Tn, X, X_b):
        w3 = wrows(key, l)
        for half in range(2):
            wv, wvb = load_w(key, l, w3[:, :, half * 512:(half + 1) * 512])
            for dd in range(4):
                d = half * 4 + dd
                gb, gbb = bank("gen")
                for kc in range(8):
                    mm(gb[:, :Tn], wv[:, kc, dd * 128:(dd + 1) * 128], src[:, kc, :Tn], kc == 0, kc == 7, [wvb, src_b], [gbb])
                tt(DVE, X[:, d, :Tn], gb[:, :Tn], X[:, d, :Tn], ALU.add, [gbb, X_b[d]], [X_b[d]])

    def proj_xq(l, Tn):
        w3 = wrows("wxq", l)
        for half in range(2):
            wv, wvb = load_w("wxq", l, w3[:, :, half * 512:(half + 1) * 512])
            for dd in range(4):
                gb, gbb = bank("gen")
                for kc in range(8):
                    mm(gb[:, :Tn], wv[:, kc, dd * 128:(dd + 1) * 128], hT[:, kc, :Tn], kc == 0, kc == 7, [wvb, hT_b[kc]], [gbb])
                dst, dstb = (qnT, qnT_b) if half == 0 else (qaT, qaT_b)
                ts(DVE, dst[:, dd, :Tn], gb[:, :Tn], 1.0 / 16.0, ALU.mult, [gbb], [dstb])

    def mlp(l, Tn, X, X_b):
        rmsnorm_T(X, X_b, 8, Tn, l * G_L + G_MLP, hT, hT_b, D)
        up3 = wrows("wup", l)
        dn3 = wscr["wdown"][l].rearrange("(g j p) n -> g p j n", j=4, p=128)
        accb = [(ps[i], psb[i]) for i in (3, 4, 5, 6)]
        mlp_mode[0] = True
        pendq = []

        def down(fi, ai, wd, wdb):
            for d in range(8):
                ab, abb = accb[d // 2]
                mm(ab[:, (d % 2) * 256:(d % 2) * 256 + Tn], wd[:, fi % 4, d * 128:(d + 1) * 128], actT[ai][:, :Tn],
                   fi == 0 and d % 2 == 0, fi == 31, [wdb, actT_b[ai]], [abb])

        for g in range(8):
            wu, wub = load_w("wup", l, up3[:, :, g * 512:(g + 1) * 512])
            wd, wdb = load_w("wdown", l, dn3[g])
            for j in range(4):
                fi = g * 4 + j
                gb, gbb = bank("gen")
                for kc in range(8):
                    mm(gb[:, :Tn], wu[:, kc, j * 128:(j + 1) * 128], hT[:, kc, :Tn], kc == 0, kc == 7, [wub, hT_b[kc]], [gbb])
                if len(pendq) >= 2:
                    down(*pendq.pop(0))
                ai = rot2["act"] % 3
                rot2["act"] += 1
                ts(DVE, rr[:, :Tn], gb[:, :Tn], 0.0, ALU.max, [gbb], [rr_b])
                tt(POOL, actT[ai][:, :Tn], rr[:, :Tn], rr[:, :Tn], ALU.mult, [rr_b], [actT_b[ai]])
                pendq.append((fi, ai, wd, wdb))
        while pendq:
            down(*pendq.pop(0))
        for d in range(8):
            ab, abb = accb[d // 2]
            tt(DVE, X[:, d, :Tn], ab[:, (d % 2) * 256:(d % 2) * 256 + Tn], X[:, d, :Tn], ALU.add, [abb, X_b[d]], [X_b[d]])
        mlp_mode[0] = False

    def final_out(X, X_b, Tn, dst, row0):
        rmsnorm_f32(X, X_b, Tn)
        for sub in range(Tn // 128):
            for half in range(2):
                gb, gbb = bank("gen")
                for c in range(4):
                    tr(gb[:, c * 128:(c + 1) * 128], yT[:, half * 4 + c, sub * 128:(sub + 1) * 128], ident_f[:], xtok_b + [Bc], [gbb])
                cp(ACT, ytok[:, half * 512:(half + 1) * 512], gb[:, :], [gbb], [ytok_b])
            store(dst[row0 + sub * 128:row0 + (sub + 1) * 128, :], ytok[:], [ytok_b], ytok_b)

    def rmsnorm_f32(X, X_b, Tn):
        pb, pbb = bank("gen")
        for c in range(8):
            i = rot2["sq"] % 2
            rot2["sq"] += 1
            tt(POOL if c % 2 == 0 else DVE, sq[i][:, :Tn], X[:, c, :Tn], X[:, c, :Tn], ALU.mult, [X_b[c]], [sq_b[i]])
            mm(pb[:, :Tn], ones_bf[:], sq[i][:, :Tn], c == 0, c == 7, [sq_b[i], Bc], [pbb])
        act(lnv[:, :Tn], pb[:, :Tn], AF.Ln, [pbb, Bc], [nrm_b], bias=epsc[:, 0:1], scale=1.0 / D)
        act(rstd[:, :Tn], lnv[:, :Tn], AF.Exp, [nrm_b], [nrm_b], scale=-0.5)
        for c in range(8):
            stt(DVE, yT[:, c, :Tn], X[:, c, :Tn], gains[:, G_FINAL + c:G_FINAL + c + 1], rstd[:, :Tn],
                ALU.mult, ALU.mult, [X_b[c], nrm_b, Bc], xtok_b)

    def compute_bias(nkt, mid_note=None):
        P.add(DVE, lambda e: e.tensor_tensor(out=biasall[:, 0:nkt, :],
                                             in0=cblk[:].unsqueeze(1).broadcast_to([128, nkt, 8]),
                                             in1=CUM[:, 0:nkt, :], op=ALU.subtract),
              reads=[cblk_b] + cum_b[0:nkt], writes=[bias_b])

    def _wrap(f, tag):
        def g(*a, **k):
            old = P.tag
            P.tag = tag
            try:
                return f(*a, **k)
            finally:
                P.tag = old
        return g

    rmsnorm_T = _wrap(rmsnorm_T, "norm"); rmsnorm_f32 = _wrap(rmsnorm_f32, "norm")
    ingest_k = _wrap(ingest_k, "ingest"); ingest_v = _wrap(ingest_v, "ingest"); ingest_ckv_kr = _wrap(ingest_ckv_kr, "ingest")
    ingest_logf = _wrap(ingest_logf, "ingest"); fox_attention = _wrap(fox_attention, "fox"); mla_attention = _wrap(mla_attention, "mla")
    cross_attention = _wrap(cross_attention, "cross"); prompt_memory_kv = _wrap(prompt_memory_kv, "memkv")
    phase_A = _wrap(phase_A, "A"); phase_rows = _wrap(phase_rows, "rows"); proj_residual = _wrap(proj_residual, "proj")
    proj_xq = _wrap(proj_xq, "proj"); mlp = _wrap(mlp, "mlp"); final_out = _wrap(final_out, "final")

    for l in range(dbg.get('nl', NL)):
        load_mla_weights(l)
        prompt_memory_kv(l)
        if stop == 'mem':
            break
        for ti in range(dbg.get('nt', NT)):
            t0 = ti * T
            if l == 0:
                for sub in range(2):
                    i = rot2["xtok"] % 2
                    rot2["xtok"] += 1
                    P.dma(SP, xtok[i], xp[t0 + sub * 128:t0 + (sub + 1) * 128, :], [], [xtok_b[i]], xtok_b[i])
                    for half in range(2):
                        gb, gbb = bank("gen")
                        for c in range(4):
                            tr(gb[:, c * 128:(c + 1) * 128], xtok[i][:, (half * 4 + c) * 128:(half * 4 + c + 1) * 128],
                               ident_f[:], [xtok_b[i], Bc], [gbb])
                        for c in range(4):
                            cp(DVE if half == 0 else ACT, xT[:, half * 4 + c, sub * 128:(sub + 1) * 128],
                               gb[:, c * 128:(c + 1) * 128], [gbb], [xT_b[half * 4 + c]])
            else:
                P.dma(SP, xT[:, :, :], xscr.rearrange("(c p) t -> p c t", p=128)[:, :, t0:t0 + T], [xscr_b[ti]], xT_b, xT_b[0])
            W = phase_A(l, xT, xT_b, T, t0)
            phase_rows(l, W, [(sub * 128, 128, ti * 2 + sub, (ti == 0 and sub == 0), t0 + sub * 128, t0 + sub * 128, sub == 0)
                              for sub in range(2)], (fk_p[l], fv_p[l], lf_p[l], ckv_p[l], kr_p[l]))
            if stop == 'A':
                continue
            tiles = [(kt, 128) for kt in range(ti * 2 + 2)]
            diag = {ti * 2: 0, ti * 2 + 1: 128}
            compute_bias(ti * 2 + 2)
            for pr in range(4):
                fox_attention(pr, 0, T, tiles, diag)
            for pr in range(2):
                mla_attention(pr, 0, T, tiles, diag, l)
            if stop == 'attn':
                dump()
                continue
            proj_residual("wout", l, oT, oT_b, T, xT, xT_b)
            if stop == 'wout':
                dump()
                continue
            rmsnorm_T(xT, xT_b, 8, T, l * G_L + G_CROSS, hT, hT_b, D)
            proj_xq(l, T)
            cross_attention(0, T)
            proj_residual("wxo", l, oT, oT_b, T, xT, xT_b)
            if stop == 'cross':
                dump()
                continue
            mlp(l, T, xT, xT_b)
            if stop == 'mlp':
                dump()
            if l == 0 and ti == 0:
                emit_casts(1)
            if l == 0:
                P.dma(STQ, xscr.rearrange("(c p) t -> p c t", p=128)[:, :, t0:t0 + T], xT[:, :, :], xT_b, [xscr_b[ti]], xscr_b[ti])
            else:
                final_out(xT, xT_b, T, y_p, t0)
        if not dbg.get('sample', True):
            continue
        if l == 0:
            i = rot2["xtok"] % 2
            rot2["xtok"] += 1
            P.dma(SP, xtok[i], xs, [], [xtok_b[i]], xtok_b[i])
            for half in range(2):
                gb, gbb = bank("gen")
                for c in range(4):
                    tr(gb[:, c * 128:(c + 1) * 128], xtok[i][:, (half * 4 + c) * 128:(half * 4 + c + 1) * 128],
                       ident_f[:], [xtok_b[i], Bc], [gbb])
                for c in range(4):
                    cp(DVE, xTs[:, half * 4 + c, :], gb[:, c * 128:(c + 1) * 128], [gbb], [xTs_b[half * 4 + c]])
        sstop = dbg.get('sstop', 'end')
        W = phase_A(l, xTs, xTs_b, 128, S)
        if sstop == 'A':
            continue
        for b in range(dbg.get('nsb', NSB)):
            for kt in range(16):
                i = rot2["k"] % 2
                rot2["k"] += 1
                P.dma(SP, ktok[i][:], cfk[l, b, kt * 128:(kt + 1) * 128, :], [], [ktok_b[i]], ktok_b[i])
                ingest_k(kt, 128, ktok[i][:], ktok_b[i])
                i = rot2["v"] % 2
                rot2["v"] += 1
                P.dma(SP, vtok[i][:], cfv[l, b, kt * 128:(kt + 1) * 128, :], [], [vtok_b[i]], vtok_b[i])
                ingest_v(kt, 128, vtok[i][:], vtok_b[i])
                i = rot2["sm"] % 2
                rot2["sm"] += 1
                sm, smb = smtok[i], smtok_b[i]
                P.dma(SP, sm[:, 0:8], clf[l, b, kt * 128:(kt + 1) * 128, :], [], [smb], smb)
                P.dma(SP, sm[:, 8:136], cckv[l, b, kt * 128:(kt + 1) * 128, :], [], [smb], smb)
                P.dma(SP, sm[:, 136:200], ckr[l, b, kt * 128:(kt + 1) * 128, :], [], [smb], smb)
                ingest_logf(kt, 128, sm[:, 0:8], smb, kt == 0)
                ingest_ckv_kr(kt, 128, sm[:, 8:136], sm[:, 136:200], smb)
            cp(DVE, cblk[:], cumtot[:], [cumtot_b], [cblk_b])
            if sstop == 'ingest':
                continue
            phase_rows(l, W, [(b * SQ, SQ, 16, False, b * SQ, S, False)],
                       (fk_s[l], fv_s[l], lf_s[l], ckv_s[l], kr_s[l]))
            if sstop == 'rows':
                continue
            compute_bias(17)
            tiles = [(kt, 128) for kt in range(16)] + [(16, SQ)]
            for pr in range(4):
                fox_attention(pr, b * SQ, SQ, tiles, {16: 0})
            for pr in range(2):
                mla_attention(pr, b * SQ, SQ, tiles, {}, l)
        if sstop == 'attn':
            continue
        proj_residual("wout", l, oT, oT_b, 128, xTs, xTs_b)
        rmsnorm_T(xTs, xTs_b, 8, 128, l * G_L + G_CROSS, hT, hT_b, D)
        proj_xq(l, 128)
        if sstop == 'wout':
            continue
        if sstop == 'xq':
            continue
        for b in range(dbg.get('ncb', NSB)):
            sample_memory_kv(l, b)
            if sstop == 'memkv':
                continue
            cross_attention(b * SQ, SQ)
        if sstop == 'cross':
            continue
        proj_residual("wxo", l, oT, oT_b, 128, xTs, xTs_b)
        mlp(l, 128, xTs, xTs_b)
        if l == NL - 1:
            final_out(xTs, xTs_b, 128, y_s, 0)

    P.emit(final_wait_bufs=out_bufs)
    return nc


_NC_CACHE = {}
SINGLE_LAUNCH = True


def _consts():
    half = 32
    inv = (10000.0 ** (-np.arange(half, dtype=np.float32) / half)).astype(np.float32)
    pos_p = np.arange(S, dtype=np.float32)
    pos_s = (PAST + np.arange(SQ)).astype(np.float32)
    ang_p = (pos_p[:, None] * inv[None, :]).astype(np.float32)
    ang_s = (pos_s[:, None] * inv[None, :]).astype(np.float32)
    ang_T = np.concatenate([ang_p] + [ang_s] * NSB, axis=0)
    cos = np.cos(ang_T.astype(np.float64)).astype(np.float32)
    sin = np.sin(ang_T.astype(np.float64)).astype(np.float32)
    cosT = np.ascontiguousarray(np.tile(cos.T, (4, 1)))
    sgn = np.where((np.arange(128) % 64) < 32, -1.0, 1.0).astype(np.float32)[:, None]
    sinT = np.ascontiguousarray(np.tile(sin.T, (4, 1)) * sgn)
    ang_tok = np.concatenate([ang_p, ang_s], axis=0)
    c = np.cos(ang_tok.astype(np.float64)).astype(np.float32)
    s_ = np.sin(ang_tok.astype(np.float64)).astype(np.float32)
    cstok = np.ascontiguousarray(np.concatenate([c, c], axis=1))
    sntok = np.ascontiguousarray(np.concatenate([-s_, s_], axis=1))
    return cosT, sinT, cstok, sntok


def kernel(x_prompt, x_sample, mem_prompt, cache_fox_k, cache_fox_v, cache_fox_logf,
           cache_mla_ckv, cache_mla_krope, cache_mem_k, cache_mem_v,
           norm_mix, w_in, b_forget, mla_q_norm, w_uq, mla_kv_norm, w_ukv, w_out,
           norm_cross, norm_mem, w_xq, w_mk, w_mv, w_xo, norm_mlp, w_up, w_down, norm_final):
    f = lambda a: np.ascontiguousarray(np.asarray(a, dtype=np.float32))
    (x_prompt, x_sample, mem_prompt, cache_fox_k, cache_fox_v, cache_fox_logf, cache_mla_ckv, cache_mla_krope,
     cache_mem_k, cache_mem_v, norm_mix, w_in, b_forget, mla_q_norm, w_uq, mla_kv_norm, w_ukv, w_out, norm_cross,
     norm_mem, w_xq, w_mk, w_mv, w_xo, norm_mlp, w_up, w_down, norm_final) = map(f, (
        x_prompt, x_sample, mem_prompt, cache_fox_k, cache_fox_v, cache_fox_logf, cache_mla_ckv, cache_mla_krope,
        cache_mem_k, cache_mem_v, norm_mix, w_in, b_forget, mla_q_norm, w_uq, mla_kv_norm, w_ukv, w_out, norm_cross,
        norm_mem, w_xq, w_mk, w_mv, w_xo, norm_mlp, w_up, w_down, norm_final))
    if "nc" not in _NC_CACHE:
        _NC_CACHE["nc"] = build()
    nc = _NC_CACHE["nc"]
    n = 8
    gains = np.zeros((128, GC), np.float32)

    def fm(v):
        return v.reshape(-1, 128).T

    for l in range(NL):
        o = l * G_L
        gains[:, o + G_MIX:o + G_MIX + 8] = fm(norm_mix[l])
        gains[:, o + G_CROSS:o + G_CROSS + 8] = fm(norm_cross[l])
        gains[:, o + G_MEM:o + G_MEM + 8] = fm(norm_mem[l])
        gains[:, o + G_MLP:o + G_MLP + 8] = fm(norm_mlp[l])
        gains[:, o + G_QN:o + G_QN + 2] = fm(mla_q_norm[l])
    gains[:, G_FINAL:G_FINAL + 8] = fm(norm_final)
    rowsb = np.zeros((128, RBC), np.float32)
    for l in range(NL):
        rowsb[:, RB_BF + l * 8:RB_BF + l * 8 + 8] = b_forget[l][None, :]
        rowsb[:, RB_KVN + l * 128:RB_KVN + (l + 1) * 128] = mla_kv_norm[l][None, :]
    w_uq_r = np.zeros((NL, 256, 1024), np.float32)
    w_ukT = np.zeros((NL, 512, 128), np.float32)
    w_uv = np.zeros((NL, 128, 512), np.float32)
    for h in range(4):
        w_uq_r[:, :, h * 128:(h + 1) * 128] = w_uq[:, :, h * 192:h * 192 + 128]
        w_uq_r[:, :, 512 + h * 64:512 + (h + 1) * 64] = w_uq[:, :, h * 192 + 128:h * 192 + 192]
        w_uq_r[:, :, 768 + h * 64:768 + h * 64 + 32] = w_uq[:, :, h * 192 + 160:h * 192 + 192]
        w_uq_r[:, :, 768 + h * 64 + 32:768 + (h + 1) * 64] = w_uq[:, :, h * 192 + 128:h * 192 + 160]
        w_ukT[:, h * 128:(h + 1) * 128, :] = np.transpose(w_ukv[:, :, h * 256:h * 256 + 128], (0, 2, 1))
        w_uv[:, :, h * 128:(h + 1) * 128] = w_ukv[:, :, h * 256 + 128:h * 256 + 256]
    cosT, sinT, cstok, sntok = _consts()
    shared = {
        "w_in": w_in, "w_uq_r": w_uq_r, "w_ukT": w_ukT, "w_uv": w_uv, "w_out": w_out, "w_xq": w_xq, "w_xo": w_xo,
        "w_mk": w_mk, "w_mv": w_mv, "w_up": w_up, "w_down": w_down, "gains": gains, "rowsb": rowsb,
        "cosT": cosT, "sinT": sinT, "cstok": cstok, "sntok": sntok,
    }
    in_maps = []
    for c in range(n):
        sl = slice(c * NSB, (c + 1) * NSB)
        m = dict(shared)
        m["xp"] = x_prompt[c]
        m["xs"] = np.ascontiguousarray(x_sample[sl].reshape(NSB * SQ, D))
        m["mem"] = mem_prompt[c]
        m["cfk"] = np.ascontiguousarray(cache_fox_k[:, sl].reshape(NL, NSB, PAST, 512))
        m["cfv"] = np.ascontiguousarray(cache_fox_v[:, sl].reshape(NL, NSB, PAST, 512))
        m["clf"] = np.ascontiguousarray(cache_fox_logf[:, sl])
        m["cckv"] = np.ascontiguousarray(cache_mla_ckv[:, sl])
        m["ckr"] = np.ascontiguousarray(cache_mla_krope[:, sl])
        m["cmk"] = np.ascontiguousarray(cache_mem_k[:, sl].reshape(NL, NSB, 256, D))
        m["cmv"] = np.ascontiguousarray(cache_mem_v[:, sl].reshape(NL, NSB, 256, D))
        in_maps.append(m)
    if SINGLE_LAUNCH:
        res = run_bass_kernel_spmd(nc, in_maps, core_ids=list(range(n)))
        R = res.results
    else:
        R = []
        for c in range(n):
            r = run_bass_kernel_spmd(nc, [in_maps[c]], core_ids=[0])
            R.append(r.results[0])

    def cat_p(key, shp):
        return np.stack([np.asarray(R[c][key], dtype=np.float32).reshape((NL, S) + shp) for c in range(n)], axis=1)

    def cat_s(key, shp):
        return np.concatenate([np.asarray(R[c][key], dtype=np.float32).reshape((NL, NSB, SQ) + shp) for c in range(n)], axis=1)

    y_prompt = np.stack([np.asarray(R[c]["y_p"], dtype=np.float32) for c in range(n)], axis=0)
    y_sample = np.concatenate([np.asarray(R[c]["y_s"], dtype=np.float32).reshape(NSB, SQ, D) for c in range(n)], axis=0)
    mk = np.stack([np.asarray(R[c]["mk_p"], dtype=np.float32).reshape(NL, 256, 4, 256) for c in range(n)], axis=1)
    mv = np.stack([np.asarray(R[c]["mv_p"], dtype=np.float32).reshape(NL, 256, 4, 256) for c in range(n)], axis=1)
    return (y_prompt, y_sample,
            cat_p("fk_p", (8, 64)), cat_p("fv_p", (8, 64)), cat_p("lf_p", (8,)), cat_p("ckv_p", (128,)), cat_p("kr_p", (64,)),
            mk, mv,
            cat_s("fk_s", (8, 64)), cat_s("fv_s", (8, 64)), cat_s("lf_s", (8,)), cat_s("ckv_s", (128,)), cat_s("kr_s", (64,)))
```
